# Optimizing a Trainium2 kernel written in Bass

```python
import math
import jax
import jax.numpy as jnp
from jax import lax
import numpy as np

D_MODEL = 2048
BATCH = 8
SEQ = 2048
DEPTH = 2

GRID_W = 64
Q_BLOCK = 128
NORM_EPS = 1e-6

A_HEADS = 8
A_KV_HEADS = 2
A_HEAD_DIM = 128
A_WIDTH = A_HEADS * A_HEAD_DIM
ROPE_THETA = 10000.0

B_WIDTH = 512
B_EMB_DIM = 33
B_FILTER_HIDDEN = 64
B_DECAY_TARGET = 1e-2
B_FAST_DECAY_PCT = 0.3
B_SLOW_DECAY_PCT = 1.5

C_HEADS = 8
C_HEAD_DIM = 64
C_WIDTH = C_HEADS * C_HEAD_DIM
C_DECAY_LORA = 96
C_AAA_LORA = 96
C_GATE_LORA = 256
C_SHIFT = 3 * C_WIDTH + C_DECAY_LORA + C_AAA_LORA
C_GN_EPS = 64e-5

D_HEADS = 4
D_HEAD_DIM = 64
D_V_DIM = 2 * D_HEAD_DIM
D_WIDTH = D_HEADS * D_V_DIM

N_BRANCHES = 4
BRANCH_WIDTHS = (A_WIDTH, B_WIDTH, C_WIDTH, D_WIDTH)
D_MIX = A_WIDTH + B_WIDTH + C_WIDTH + D_WIDTH
IN_SPLITS = (A_WIDTH, A_KV_HEADS * A_HEAD_DIM, A_KV_HEADS * A_HEAD_DIM, 3 * B_WIDTH, C_SHIFT, C_GATE_LORA, 2 * D_HEADS * D_HEAD_DIM, 2 * D_HEADS * D_HEAD_DIM, D_WIDTH)
D_IN = sum(IN_SPLITS)
FFN_HIDDEN = -(-8 * D_MODEL // (3 * 256)) * 256

kernel_name = 'hybrid_gated_parallel_encoder'


def _offsets(sizes):
    return [int(v) for v in np.cumsum(sizes)[:-1]]


def rms_norm(x, g):
    xf = x.astype(jnp.float32)
    y = xf * lax.rsqrt(jnp.mean(xf * xf, axis=-1, keepdims=True) + NORM_EPS)
    return (y * g.astype(jnp.float32)).astype(x.dtype)


def axial_rope_tables(L):
    rows = L // GRID_W
    row_idx = jnp.repeat(jnp.arange(rows, dtype=jnp.float32), GRID_W)
    col_idx = jnp.tile(jnp.arange(GRID_W, dtype=jnp.float32), rows)
    axis_dim = A_HEAD_DIM // 2
    inv_freq = ROPE_THETA ** (-jnp.arange(0, axis_dim, 2, dtype=jnp.float32) / axis_dim)
    ang_r = row_idx[:, None] * inv_freq[None, :]
    ang_c = col_idx[:, None] * inv_freq[None, :]
    ang = jnp.concatenate([ang_r, ang_r, ang_c, ang_c], axis=-1)
    return jnp.cos(ang), jnp.sin(ang)


def apply_axial_rope(x, cos, sin):
    xr = x.reshape(x.shape[:-1] + (2, 2, A_HEAD_DIM // 4))
    rot = jnp.stack([-xr[..., 1, :], xr[..., 0, :]], axis=-2).reshape(x.shape)
    return x * cos[:, None, :] + rot * sin[:, None, :]


def gqa_axial_attention(q, k, v, q_norm, k_norm, cos, sin):
    B, T = q.shape[:2]
    G = A_HEADS // A_KV_HEADS
    nb = T // Q_BLOCK
    q = apply_axial_rope(rms_norm(q.reshape(B, T, A_HEADS, A_HEAD_DIM), q_norm), cos, sin)
    k = apply_axial_rope(rms_norm(k.reshape(B, T, A_KV_HEADS, A_HEAD_DIM), k_norm), cos, sin)
    v = v.reshape(B, T, A_KV_HEADS, A_HEAD_DIM)
    scale = A_HEAD_DIM ** -0.5
    q_blocks = q.reshape(B, nb, Q_BLOCK, A_KV_HEADS, G, A_HEAD_DIM).transpose(1, 0, 3, 4, 2, 5)

    def block(qb):
        s = jnp.einsum('bhgqd,bkhd->bhgqk', qb, k).astype(jnp.float32) * scale
        p = jax.nn.softmax(s, axis=-1).astype(v.dtype)
        return jnp.einsum('bhgqk,bkhd->bhgqd', p, v)

    o = lax.map(block, q_blocks)
    return o.transpose(1, 0, 4, 2, 3, 5).reshape(B, T, A_WIDTH)


def hyena_two_sided_filter(L, w1, b1, w2, b2, w3, b3, w4, freq):
    f32 = jnp.float32
    w1, b1, w2, b2, w3, b3, w4, freq = (a.astype(f32) for a in (w1, b1, w2, b2, w3, b3, w4, freq))
    t = jnp.linspace(0.0, 1.0, L, dtype=f32)[:, None]
    n_bands = (B_EMB_DIM - 1) // 2
    bands = jnp.linspace(1e-4, n_bands - 1, n_bands, dtype=f32)[None, :]
    ang = (2.0 * math.pi / L) * jnp.arange(L, dtype=f32)[:, None] * bands
    z = jnp.concatenate([t, jnp.cos(ang), -jnp.sin(ang)], axis=-1)
    hid = jnp.sin(freq * (z @ w1 + b1))
    hid = jnp.sin(freq * (hid @ w2 + b2))
    hid = jnp.sin(freq * (hid @ w3 + b3))
    h = hid @ w4
    max_decay = math.log(B_DECAY_TARGET) / B_FAST_DECAY_PCT
    min_decay = math.log(B_DECAY_TARGET) / B_SLOW_DECAY_PCT
    deltas = jnp.abs(jnp.linspace(min_decay, max_decay, B_WIDTH, dtype=f32))
    window = jnp.exp(-t * deltas[None, :])
    h_fwd = h[:, :B_WIDTH] * window
    h_bwd = h[:, B_WIDTH:] * window
    kbuf = jnp.concatenate([h_fwd, jnp.zeros((1, B_WIDTH), f32), h_bwd[1:][::-1]], axis=0)
    return kbuf / jnp.sum(jnp.abs(kbuf), axis=0, keepdims=True)


def hyena_mixer(u, conv_w, conv_b, w1, b1, w2, b2, w3, b3, w4, freq, skip):
    B, T = u.shape[:2]
    up = jnp.pad(u, ((0, 0), (1, 1), (0, 0)))
    uc = conv_w[0] * up[:, :-2] + conv_w[1] * up[:, 1:-1] + conv_w[2] * up[:, 2:] + conv_b
    v, x1, x0 = jnp.split(uc, 3, axis=-1)
    z = (v * x1).astype(jnp.float32)
    kbuf = hyena_two_sided_filter(T, w1, b1, w2, b2, w3, b3, w4, freq)
    n_fft = 2 * T
    zf = jnp.fft.rfft(z, n=n_fft, axis=1)
    kf = jnp.fft.rfft(kbuf, n=n_fft, axis=0)
    y = jnp.fft.irfft(zf * kf[None], n=n_fft, axis=1)[:, :T]
    y = y + z * skip.astype(jnp.float32)
    return y.astype(u.dtype) * x0


def rwkv7_scan(r, w, k, v, a_vec, b_vec, reverse):
    B, T, H, N = r.shape

    def step(S, inp):
        r_t, w_t, k_t, v_t, a_t, b_t = inp
        sa = jnp.einsum('bhvk,bhk->bhv', S, a_t)
        S = S * w_t[:, :, None, :] + sa[..., None] * b_t[:, :, None, :] + v_t[..., None] * k_t[:, :, None, :]
        return S, jnp.einsum('bhvk,bhk->bhv', S, r_t)

    xs = tuple(jnp.moveaxis(a, 1, 0) for a in (r, w, k, v, a_vec, b_vec))
    S0 = jnp.zeros((B, H, N, N), jnp.float32)
    _, o = lax.scan(step, S0, xs, reverse=reverse)
    return jnp.moveaxis(o, 0, 1)


def rwkv7_mixer(feats, g_lo, mu, w0, w_up, a0, a_up, g_up, k_k, k_a, r_k, ln_w, ln_b):
    f32 = jnp.float32
    B, T = feats.shape[:2]
    prev = jnp.pad(feats, ((0, 0), (1, 0), (0, 0)))[:, :-1]
    nxt = jnp.pad(feats, ((0, 0), (0, 1), (0, 0)))[:, 1:]
    g = jax.nn.sigmoid(g_lo) @ g_up
    split_at = [C_WIDTH, 2 * C_WIDTH, 3 * C_WIDTH, 3 * C_WIDTH + C_DECAY_LORA]
    heads = lambda a: a.reshape(B, T, C_HEADS, C_HEAD_DIM).astype(f32)
    ln_w_h = ln_w.reshape(C_HEADS, C_HEAD_DIM).astype(f32)
    ln_b_h = ln_b.reshape(C_HEADS, C_HEAD_DIM).astype(f32)
    r_k_h = r_k.astype(f32)
    outs = []
    for d, shifted in enumerate((prev, nxt)):
        f = feats + (shifted - feats) * mu[d]
        r, k, v, w_lo, a_lo = jnp.split(f, split_at, axis=-1)
        w = -jax.nn.softplus(-(w0[d] + jnp.tanh(w_lo) @ w_up[d])) - 0.5
        decay = jnp.exp(-jnp.exp(w.astype(f32)))
        a = jax.nn.sigmoid(a0[d] + a_lo @ a_up[d])
        kk = heads(k * k_k)
        kk = kk / jnp.maximum(jnp.sqrt(jnp.sum(kk * kk, axis=-1, keepdims=True)), 1e-12)
        k = k * (1.0 + (a - 1.0) * k_a)
        r_h, k_h, v_h, a_h, w_h = heads(r), heads(k), heads(v), heads(a), heads(decay)
        o = rwkv7_scan(r_h, w_h, k_h, v_h, -kk, kk * a_h, reverse=(d == 1))
        mean = jnp.mean(o, axis=-1, keepdims=True)
        var = jnp.mean(jnp.square(o - mean), axis=-1, keepdims=True)
        o = (o - mean) * lax.rsqrt(var + C_GN_EPS) * ln_w_h + ln_b_h
        o = o + jnp.sum(r_h * k_h * r_k_h, axis=-1, keepdims=True) * v_h
        outs.append(o)
    y = (outs[0] + outs[1]).reshape(B, T, C_WIDTH)
    return y.astype(g.dtype) * g


def diff_attention(q, k, v, lq1, lk1, lq2, lk2, subln, lam_init):
    f32 = jnp.float32
    B, T = q.shape[:2]
    nb = T // Q_BLOCK
    q = q.reshape(B, T, D_HEADS, 2, D_HEAD_DIM)
    k = k.reshape(B, T, D_HEADS, 2, D_HEAD_DIM)
    v = v.reshape(B, T, D_HEADS, D_V_DIM)
    lam = (jnp.exp(jnp.sum(lq1.astype(f32) * lk1.astype(f32))) - jnp.exp(jnp.sum(lq2.astype(f32) * lk2.astype(f32))) + lam_init)
    slopes = 2.0 ** (-8.0 * jnp.arange(1, D_HEADS + 1, dtype=f32) / D_HEADS)
    scale = D_HEAD_DIM ** -0.5
    q_blocks = q.reshape(B, nb, Q_BLOCK, D_HEADS, 2, D_HEAD_DIM).transpose(1, 0, 2, 3, 4, 5)
    q_pos = jnp.arange(T).reshape(nb, Q_BLOCK)
    k_pos = jnp.arange(T)

    def block(args):
        qb, qp = args
        s = jnp.einsum('bqhcd,bkhcd->bhcqk', qb, k).astype(f32) * scale
        dist = jnp.abs(qp[:, None] - k_pos[None, :]).astype(f32)
        s = s - slopes[:, None, None, None] * dist
        p = jax.nn.softmax(s, axis=-1)
        attn = p[:, :, 0] - lam * p[:, :, 1]
        return jnp.einsum('bhqk,bkhe->bqhe', attn.astype(v.dtype), v)

    o = lax.map(block, (q_blocks, q_pos))
    o = o.transpose(1, 0, 2, 3, 4).reshape(B, T, D_HEADS, D_V_DIM)
    o = rms_norm(o, subln) * (1.0 - lam_init)
    return o.reshape(B, T, D_WIDTH)


def setup_inputs(seed: int = 0) -> dict:
    key = jax.random.key(seed)
    keys = iter(jax.random.split(key, 48))
    f32 = jnp.float32

    def nrm(shape, scale):
        return scale * jax.random.normal(next(keys), shape, f32)

    def gain(shape):
        return 1.0 + nrm(shape, 0.02)

    br_scale = jnp.concatenate([jnp.full((w,), w ** -0.5, f32) for w in BRANCH_WIDTHS])
    return {
        'x': nrm((BATCH, SEQ, D_MODEL), 1.0),
        'norm_mix': gain((DEPTH, D_MODEL)),
        'w_in': nrm((DEPTH, D_MODEL, D_IN), D_MODEL ** -0.5),
        'a_q_norm': gain((DEPTH, A_HEAD_DIM)),
        'a_k_norm': gain((DEPTH, A_HEAD_DIM)),
        'b_conv_w': nrm((DEPTH, 3, 3 * B_WIDTH), 0.5),
        'b_conv_b': nrm((DEPTH, 3 * B_WIDTH), 0.02),
        'b_filt_w1': nrm((DEPTH, B_EMB_DIM, B_FILTER_HIDDEN), B_EMB_DIM ** -0.5),
        'b_filt_b1': nrm((DEPTH, B_FILTER_HIDDEN), 0.02),
        'b_filt_w2': nrm((DEPTH, B_FILTER_HIDDEN, B_FILTER_HIDDEN), B_FILTER_HIDDEN ** -0.5),
        'b_filt_b2': nrm((DEPTH, B_FILTER_HIDDEN), 0.02),
        'b_filt_w3': nrm((DEPTH, B_FILTER_HIDDEN, B_FILTER_HIDDEN), B_FILTER_HIDDEN ** -0.5),
        'b_filt_b3': nrm((DEPTH, B_FILTER_HIDDEN), 0.02),
        'b_filt_w4': nrm((DEPTH, B_FILTER_HIDDEN, 2 * B_WIDTH), B_FILTER_HIDDEN ** -0.5),
        'b_filt_freq': gain((DEPTH, B_FILTER_HIDDEN)),
        'b_skip': nrm((DEPTH, B_WIDTH), 0.5),
        'c_mu': jax.random.uniform(next(keys), (DEPTH, 2, C_SHIFT), f32),
        'c_w0': jnp.linspace(-6.0, -1.0, C_WIDTH, dtype=f32) + nrm((DEPTH, 2, C_WIDTH), 0.1),
        'c_w_up': nrm((DEPTH, 2, C_DECAY_LORA, C_WIDTH), 0.1 * C_DECAY_LORA ** -0.5),
        'c_a0': nrm((DEPTH, 2, C_WIDTH), 0.1),
        'c_a_up': nrm((DEPTH, 2, C_AAA_LORA, C_WIDTH), C_AAA_LORA ** -0.5),
        'c_g_up': nrm((DEPTH, C_GATE_LORA, C_WIDTH), C_GATE_LORA ** -0.5),
        'c_k_k': 0.85 + nrm((DEPTH, C_WIDTH), 0.02),
        'c_k_a': gain((DEPTH, C_WIDTH)),
        'c_r_k': nrm((DEPTH, C_HEADS, C_HEAD_DIM), 0.1),
        'c_ln_w': gain((DEPTH, C_WIDTH)),
        'c_ln_b': nrm((DEPTH, C_WIDTH), 0.02),
        'd_lq1': nrm((DEPTH, D_HEAD_DIM), 0.1),
        'd_lk1': nrm((DEPTH, D_HEAD_DIM), 0.1),
        'd_lq2': nrm((DEPTH, D_HEAD_DIM), 0.1),
        'd_lk2': nrm((DEPTH, D_HEAD_DIM), 0.1),
        'd_subln': gain((DEPTH, D_V_DIM)),
        'w_gate': nrm((DEPTH, N_BRANCHES, D_MODEL, D_MODEL), D_MODEL ** -0.5),
        'w_branch': nrm((DEPTH, D_MIX, D_MODEL), 1.0) * br_scale[None, :, None],
        'w_out': nrm((DEPTH, D_MODEL, D_MODEL), D_MODEL ** -0.5),
        'norm_ffn': gain((DEPTH, D_MODEL)),
        'w_ff_gate': nrm((DEPTH, D_MODEL, FFN_HIDDEN), D_MODEL ** -0.5),
        'w_ff_up': nrm((DEPTH, D_MODEL, FFN_HIDDEN), D_MODEL ** -0.5),
        'w_ff_down': nrm((DEPTH, FFN_HIDDEN, D_MODEL), FFN_HIDDEN ** -0.5),
        'norm_final': gain((D_MODEL,)),
    }


def reference(x, norm_mix, w_in, a_q_norm, a_k_norm, b_conv_w, b_conv_b, b_filt_w1, b_filt_b1, b_filt_w2, b_filt_b2, b_filt_w3, b_filt_b3, b_filt_w4, b_filt_freq, b_skip, c_mu, c_w0, c_w_up, c_a0, c_a_up, c_g_up, c_k_k, c_k_a, c_r_k, c_ln_w, c_ln_b, d_lq1, d_lk1, d_lq2, d_lk2, d_subln, w_gate, w_branch, w_out, norm_ffn, w_ff_gate, w_ff_up, w_ff_down, norm_final):
    L = x.shape[1]
    cos, sin = axial_rope_tables(L)
    in_off = _offsets(IN_SPLITS)
    br_off = _offsets(BRANCH_WIDTHS)
    for l in range(DEPTH):
        h = rms_norm(x, norm_mix[l])
        aq, ak, av, bu, cf, cg, dq, dk, dv = jnp.split(h @ w_in[l], in_off, axis=-1)
        o_a = gqa_axial_attention(aq, ak, av, a_q_norm[l], a_k_norm[l], cos, sin)
        o_b = hyena_mixer(bu, b_conv_w[l], b_conv_b[l], b_filt_w1[l], b_filt_b1[l], b_filt_w2[l], b_filt_b2[l], b_filt_w3[l], b_filt_b3[l], b_filt_w4[l], b_filt_freq[l], b_skip[l])
        o_c = rwkv7_mixer(cf, cg, c_mu[l], c_w0[l], c_w_up[l], c_a0[l], c_a_up[l], c_g_up[l], c_k_k[l], c_k_a[l], c_r_k[l], c_ln_w[l], c_ln_b[l])
        lam_init = 0.8 - 0.6 * math.exp(-0.3 * l)
        o_d = diff_attention(dq, dk, dv, d_lq1[l], d_lk1[l], d_lq2[l], d_lk2[l], d_subln[l], lam_init)
        branches = (o_a, o_b, o_c, o_d)
        w_br = jnp.split(w_branch[l], br_off, axis=0)
        merged = jax.nn.sigmoid(h @ w_gate[l, 0]) * (branches[0] @ w_br[0])
        for i in range(1, N_BRANCHES):
            merged = merged + jax.nn.sigmoid(h @ w_gate[l, i]) * (branches[i] @ w_br[i])
        x = x + merged @ w_out[l]
        h2 = rms_norm(x, norm_ffn[l])
        x = x + (jax.nn.silu(h2 @ w_ff_gate[l]) * (h2 @ w_ff_up[l])) @ w_ff_down[l]
    return rms_norm(x, norm_final)
```

```python
import numpy as np
from contextlib import ExitStack
import concourse.bass as bass
import concourse.mybir as mybir

F32 = mybir.dt.float32
BF16 = mybir.dt.bfloat16
AF = mybir.ActivationFunctionType
ALU = mybir.AluOpType
AX = mybir.AxisListType

EPOCH = 30000
NDMASEM = 24


class _Rec:
    def __init__(self):
        self.call = None

    def __getattr__(self, name):
        def f(*a, **k):
            self.call = (name, a, k)
            return self
        return f


class Prog:
    ENGS = ("pe", "act", "dve", "pool", "sp")

    def __init__(self):
        self.nc = bass.Bass("TRN2", target_bir_lowering=False)
        self.es = ExitStack()
        self.items = {e: [] for e in self.ENGS}
        self.seq = {e: 0 for e in self.ENGS}
        self.esems = {e: [] for e in self.ENGS}
        self.known = {e: {} for e in self.ENGS}
        self.lastw = {}
        self.readers = {}
        self.dma_cnt = {e: 0 for e in self.ENGS}
        self.dsems = {e: [] for e in self.ENGS}
        self.nsem = 0
        self._uid = 0

    def uid(self, p="t"):
        self._uid += 1
        return f"{p}{self._uid}"

    def sem(self, name):
        self.nsem += 1
        return self.es.enter_context(self.nc.semaphore(name))

    def sbuf(self, shape, dtype, name=None, stack=None):
        st = stack if stack is not None else self.es
        return st.enter_context(self.nc.sbuf_tensor(name or self.uid("sb"), list(shape), dtype))

    def psum(self, shape, dtype=F32, name=None, stack=None):
        st = stack if stack is not None else self.es
        return st.enter_context(self.nc.psum_tensor(name or self.uid("ps"), list(shape), dtype))

    def dram(self, name, shape, dtype, kind="Internal"):
        return self.nc.dram_tensor(name, list(shape), dtype, kind=kind)

    def _esem(self, e, idx):
        while len(self.esems[e]) <= idx:
            self.esems[e].append(self.sem(f"s_{e}_{len(self.esems[e])}"))
        return self.esems[e][idx]

    def _tok_sem(self, tok):
        kind = tok[0]
        if kind == "e":
            _, e, s = tok
            idx = (s - 1) // EPOCH
            return ("e", e, idx), self._esem(e, idx), (s - 1) % EPOCH + 1
        _, q, i = tok
        return ("d", q, i % NDMASEM), self.dsems[q][i % NDMASEM], 16 * (i // NDMASEM + 1)

    def _need(self, F, tok):
        key, sem, val = self._tok_sem(tok)
        if tok[0] == "e":
            _, e, idx = key
            for k2, v2 in self.known[F].items():
                if k2[0] == "e" and k2[1] == e and k2[2] > idx:
                    return
        if self.known[F].get(key, 0) >= val:
            return
        self.known[F][key] = val
        self.items[F].append(("wait", sem, val))

    def _deps(self, F, reads, writes, skip_self=False):
        toks = []
        for k in reads:
            t = self.lastw.get(k)
            if t is not None:
                toks.append(t)
        for k in writes:
            t = self.lastw.get(k)
            if t is not None:
                toks.append(t)
            toks.extend(self.readers.get(k, ()))
        for t in toks:
            if skip_self and t[0] == "e" and t[1] == F:
                continue
            self._need(F, t)

    def _commit(self, tok, reads, writes):
        for k in reads:
            lst = self.readers.setdefault(k, [])
            if tok[0] == "e":
                lst[:] = [t for t in lst if not (t[0] == "e" and t[1] == tok[1])]
            lst.append(tok)
        for k in writes:
            self.lastw[k] = tok
            self.readers[k] = []

    def op(self, eng, fn, reads=(), writes=(), skip_self=False):
        self._deps(eng, reads, writes, skip_self)
        self.seq[eng] += 1
        s = self.seq[eng]
        idx = (s - 1) // EPOCH
        sem = self._esem(eng, idx)
        r = _Rec()
        fn(r)
        assert r.call is not None
        self.items[eng].append(("op", r.call, sem, 1))
        tok = ("e", eng, s)
        self._commit(tok, reads, writes)
        return tok

    def dma(self, q, out, in_, reads=(), writes=(), **kw):
        if not self.dsems[q]:
            self.dsems[q] = [self.sem(f"d_{q}_{i}") for i in range(NDMASEM)]
        i = self.dma_cnt[q]
        self.dma_cnt[q] += 1
        if i >= NDMASEM:
            self._need(q, ("d", q, i - NDMASEM))
        self._deps(q, reads, writes)
        sem = self.dsems[q][i % NDMASEM]
        self.items[q].append(("op", ("dma_start", (), dict(out=out, in_=in_, **kw)), sem, 16))
        tok = ("d", q, i)
        self._commit(tok, reads, writes)
        return tok

    def barrier(self):
        toks = []
        for e in self.ENGS:
            if self.seq[e] > 0:
                toks.append(("e", e, self.seq[e]))
            n = self.dma_cnt[e]
            for i in range(max(0, n - NDMASEM), n):
                toks.append(("d", e, i))
        for F in self.ENGS:
            for t in toks:
                self._need(F, t)
        self.lastw.clear()
        self.readers.clear()

    def wait_all_on(self, F, keys):
        for k in keys:
            t = self.lastw.get(k)
            if t is not None:
                self._need(F, t)

    def emit(self):
        nc = self.nc
        block = self.es.enter_context(nc.Block())
        items = self.items

        def replay(eng, lst):
            for it in lst:
                if it[0] == "wait":
                    eng.wait_ge(it[1], it[2])
                else:
                    name, a, k = it[1]
                    ins = getattr(eng, name)(*a, **k)
                    ins.then_inc(it[2], it[3])

        @block.tensor
        def _(e):
            replay(e, items["pe"])

        @block.scalar
        def _(e):
            replay(e, items["act"])

        @block.vector
        def _(e):
            replay(e, items["dve"])

        @block.gpsimd
        def _(e):
            replay(e, items["pool"])

        @block.sync
        def _(e):
            replay(e, items["sp"])

    def close(self):
        self.es.close()

import math
import numpy as np

T = 2048
D = 2048
KC = D // 128
D_IN = 6592
FFN = 5632
EPS = 1e-6
I32 = mybir.dt.int32


def setup_common(P):
    P.banks = [P.psum([128, 512], F32, name=f"bank{i}") for i in range(8)]
    P.bank_rr = 0
    P.ident_f = P.sbuf([128, 128], F32, name="ident_f")
    P.ident_b = P.sbuf([128, 128], BF16, name="ident_b")
    P.ones_f = P.sbuf([128, 128], F32, name="ones_f")
    P.ones_b = P.sbuf([128, 128], BF16, name="ones_b")
    c = P.consts
    P.dma("sp", P.ident_f[:], c["ident"], writes=["ident_f"])
    P.dma("pool", P.ident_b[:], c["ident"], writes=["ident_b"])
    P.op("pool", lambda e: e.memset(P.ones_f[:], 1.0), writes=["ones_f"])
    P.op("pool", lambda e: e.memset(P.ones_b[:], 1.0), writes=["ones_b"])


def next_bank(P, lo=0, hi=8):
    n = hi - lo
    i = lo + (P.bank_rr % n)
    P.bank_rr += 1
    return i


def mm(P, out, lhsT, rhs, start, stop, reads, writes):
    return P.op("pe", lambda e: e.matmul(out, lhsT, rhs, start=start, stop=stop), reads=reads, writes=writes, skip_self=True)


def rstd_from_ss(P, ss, rstd, scale, eps, kss, krstd):
    P.op("act", lambda e: e.activation(out=rstd, in_=ss, func=AF.Sqrt, bias=eps, scale=scale), reads=[kss], writes=[krstd])
    P.op("dve", lambda e: e.reciprocal(out=rstd, in_=rstd), reads=[krstd], writes=[krstd])


def phase_norm_hT(P, xsrc, gain, hT_d):
    with ExitStack() as st:
        g_t = P.sbuf([128, KC], F32, stack=st)
        P.dma("sp", g_t[:], gain.rearrange("(kc p) -> p kc", p=128), writes=["g_t"], allow_slow_non_contiguous=True)
        xt = [P.sbuf([128, D], F32, stack=st) for _ in range(2)]
        junk = P.sbuf([128, D], BF16, stack=st)
        xn = [P.sbuf([128, D], BF16, stack=st) for _ in range(2)]
        ss = [P.sbuf([128, 1], F32, stack=st) for _ in range(2)]
        rs = [P.sbuf([128, 1], F32, stack=st) for _ in range(2)]
        hst = [P.sbuf([128, KC, 512], BF16, stack=st) for _ in range(2)]
        for tt in range(T // 128):
            s = tt % 2
            grp, gi = divmod(tt, 4)
            hs = grp % 2
            P.dma("sp", xt[s][:], xsrc[tt * 128:(tt + 1) * 128, :], writes=[f"xt{s}"])
            P.op("pool", lambda e, s=s: e.memset(ss[s][:], 0.0), writes=[f"ss{s}"])
            P.op("act", lambda e, s=s: e.activation(out=junk[:], in_=xt[s][:], func=AF.Square, accum_out=ss[s][:]),
                 reads=[f"xt{s}"], writes=["junk", f"ss{s}"])
            rstd_from_ss(P, ss[s][:], rs[s][:], 1.0 / D, EPS, f"ss{s}", f"rs{s}")
            P.op("dve", lambda e, s=s: e.tensor_scalar(out=xn[s][:], in0=xt[s][:], scalar1=rs[s][:, 0:1], scalar2=None, op0=ALU.mult),
                 reads=[f"xt{s}", f"rs{s}"], writes=[f"xn{s}"])
            for half in range(2):
                b = next_bank(P)
                pb = P.banks[b][:].bitcast(BF16)
                for j in range(8):
                    kc = half * 8 + j
                    P.op("pe", lambda e, pb=pb, j=j, kc=kc, s=s: e.transpose(pb[:, j * 128:(j + 1) * 128], xn[s][:, kc * 128:(kc + 1) * 128], P.ident_b[:]),
                         reads=[f"xn{s}", "ident_b"], writes=[f"pb{b}"], skip_self=True)
                P.op("dve", lambda e, pb=pb, half=half, hs=hs, gi=gi: e.tensor_tensor(
                    out=hst[hs][:, half * 8:(half + 1) * 8, gi * 128:(gi + 1) * 128],
                    in0=pb.rearrange("p (j t) -> p j t", j=8),
                    in1=g_t[:, half * 8:(half + 1) * 8].unsqueeze(2).to_broadcast([128, 8, 128]), op=ALU.mult),
                    reads=[f"pb{b}", "g_t"], writes=[f"hst{hs}"])
            if gi == 3:
                P.dma("pool", hT_d[:, grp * 512:(grp + 1) * 512].rearrange("(kc p) t -> p kc t", p=128), hst[hs][:],
                      reads=[f"hst{hs}"], writes=["hT_d"])
    P.barrier()


def load_fm(P, dst, src_d, key, q="sp"):
    P.dma(q, dst, src_d.rearrange("(kc p) t -> p kc t", p=128), reads=[], writes=[key])


def phase_inproj(P, hT_d, w_in, projT_d, avt_d, dvt_d):
    with ExitStack() as st:
        hT = P.sbuf([128, KC, T], BF16, stack=st)
        for q4 in range(4):
            P.dma("sp", hT[:, q4 * 4:(q4 + 1) * 4, :], hT_d[q4 * 512:(q4 + 1) * 512, :].rearrange("(kc p) t -> p kc t", p=128), writes=["hT"])
        wt = [P.sbuf([128, KC, 512], BF16, stack=st) for _ in range(2)]
        stg = [P.sbuf([128, T], F32, stack=st) for _ in range(2)]
        ncb = (D_IN + 511) // 512
        nfb = 0
        for cb in range(ncb):
            c0 = cb * 512
            w = min(512, D_IN - c0)
            s = cb % 2
            P.dma("pool", wt[s][:, :, 0:w], w_in[:, c0:c0 + w].rearrange("(kc p) c -> p kc c", p=128), writes=[f"wt{s}"])
            for fbi in range(w // 128 if w % 128 == 0 else (w + 127) // 128):
                f0 = fbi * 128
                fw_ = min(128, w - f0)
                ss_ = nfb % 2
                nfb += 1
                for tt in range(4):
                    b = next_bank(P)
                    for kc in range(KC):
                        mm(P, P.banks[b][0:fw_, :], wt[s][:, kc, f0:f0 + fw_], hT[:, kc, tt * 512:(tt + 1) * 512], kc == 0, kc == KC - 1,
                           [f"wt{s}", "hT"], [f"pb{b}"])
                    eng = "act" if tt % 2 == 0 else "dve"
                    if eng == "act":
                        P.op("act", lambda e, b=b, ss_=ss_, tt=tt, fw_=fw_: e.copy(out=stg[ss_][0:fw_, tt * 512:(tt + 1) * 512], in_=P.banks[b][0:fw_, :]),
                             reads=[f"pb{b}"], writes=[f"stg{ss_}"])
                    else:
                        P.op("dve", lambda e, b=b, ss_=ss_, tt=tt, fw_=fw_: e.tensor_copy(out=stg[ss_][0:fw_, tt * 512:(tt + 1) * 512], in_=P.banks[b][0:fw_, :]),
                             reads=[f"pb{b}"], writes=[f"stg{ss_}"])
                P.dma("sp", projT_d[c0 + f0:c0 + f0 + fw_, :], stg[ss_][0:fw_, :], reads=[f"stg{ss_}"], writes=["projT_d"])
        for (c0, w, dst, nm) in ((1280, 256, avt_d, "av"), (6080, 512, dvt_d, "dv")):
            wv = P.sbuf([128, KC, w], BF16, stack=st)
            P.dma("pool", wv[:], w_in[:, c0:c0 + w].rearrange("(kc p) c -> p kc c", p=128), writes=[nm + "w"])
            vst = [P.sbuf([128, w], BF16, stack=st) for _ in range(2)]
            for tt in range(T // 128):
                s = tt % 2
                b = next_bank(P)
                for kc in range(KC):
                    mm(P, P.banks[b][:, 0:w], hT[:, kc, tt * 128:(tt + 1) * 128], wv[:, kc, :], kc == 0, kc == KC - 1, [nm + "w", "hT"], [f"pb{b}"])
                P.op("act", lambda e, b=b, s=s, w=w, vst=vst: e.copy(out=vst[s][:], in_=P.banks[b][:, 0:w]), reads=[f"pb{b}"], writes=[f"{nm}st{s}"])
                P.dma("sp", dst[tt * 128:(tt + 1) * 128, :], vst[s][:], reads=[f"{nm}st{s}"], writes=[nm + "_d"])
    P.barrier()


BR_OFF = (0, 1024, 1536, 2048)
BR_W = (1024, 512, 512, 512)


def phase_merge(P, hT_d, obrT_d, w_gate, w_branch, mergedT_d):
    with ExitStack() as st:
        hT = P.sbuf([128, KC, 1024], BF16, stack=st)
        ob = P.sbuf([128, 20, 1024], BF16, stack=st)
        wg = [P.sbuf([128, KC, 512], BF16, stack=st) for _ in range(2)]
        wb = [P.sbuf([128, 8, 512], BF16, stack=st) for _ in range(2)]
        acc = P.sbuf([128, 4, 2, 512], F32, stack=st)
        sg = [P.sbuf([128, 512], F32, stack=st) for _ in range(2)]
        tmp = [P.sbuf([128, 512], F32, stack=st) for _ in range(2)]
        mst = [P.sbuf([128, 1024], BF16, stack=st) for _ in range(2)]
        it = 0
        nw = 0
        for th in range(2):
            t0 = th * 1024
            for q4 in range(4):
                P.dma("sp", hT[:, q4 * 4:(q4 + 1) * 4, :], hT_d[q4 * 512:(q4 + 1) * 512, t0:t0 + 1024].rearrange("(kc p) t -> p kc t", p=128), writes=["hT"])
            for q4 in range(5):
                P.dma("sp", ob[:, q4 * 4:(q4 + 1) * 4, :], obrT_d[q4 * 512:(q4 + 1) * 512, t0:t0 + 1024].rearrange("(kc p) t -> p kc t", p=128), writes=["ob"])
            for cb in range(4):
                for i in range(4):
                    s = nw % 2
                    nw += 1
                    kci = BR_W[i] // 128
                    P.dma("pool", wg[s][:], w_gate[i][:, cb * 512:(cb + 1) * 512].rearrange("(kc p) c -> p kc c", p=128), writes=[f"wg{s}"])
                    P.dma("pool", wb[s][:, 0:kci, :], w_branch[BR_OFF[i]:BR_OFF[i] + BR_W[i], cb * 512:(cb + 1) * 512].rearrange("(kc p) c -> p kc c", p=128), writes=[f"wb{s}"])
                    for fbi in range(4):
                        for tt in range(2):
                            bg = next_bank(P)
                            by = next_bank(P)
                            for kc in range(KC):
                                mm(P, P.banks[bg][:], wg[s][:, kc, fbi * 128:(fbi + 1) * 128], hT[:, kc, tt * 512:(tt + 1) * 512], kc == 0, kc == KC - 1, [f"wg{s}", "hT"], [f"pb{bg}"])
                            for kc in range(kci):
                                mm(P, P.banks[by][:], wb[s][:, kc, fbi * 128:(fbi + 1) * 128], ob[:, BR_OFF[i] // 128 + kc, tt * 512:(tt + 1) * 512], kc == 0, kc == kci - 1, [f"wb{s}", "ob"], [f"pb{by}"])
                            u = it % 2
                            it += 1
                            P.op("act", lambda e, bg=bg, u=u: e.activation(out=sg[u][:], in_=P.banks[bg][:], func=AF.Sigmoid), reads=[f"pb{bg}"], writes=[f"sg{u}"])
                            ak = f"acc{fbi}_{tt}"
                            if i == 0:
                                P.op("dve", lambda e, by=by, u=u, fbi=fbi, tt=tt: e.tensor_tensor(out=acc[:, fbi, tt, :], in0=sg[u][:], in1=P.banks[by][:], op=ALU.mult),
                                     reads=[f"sg{u}", f"pb{by}"], writes=[ak])
                            else:
                                P.op("dve", lambda e, by=by, u=u: e.tensor_tensor(out=tmp[u][:], in0=sg[u][:], in1=P.banks[by][:], op=ALU.mult),
                                     reads=[f"sg{u}", f"pb{by}"], writes=[f"tmp{u}"])
                                if i < 3:
                                    P.op("dve", lambda e, u=u, fbi=fbi, tt=tt: e.tensor_tensor(out=acc[:, fbi, tt, :], in0=acc[:, fbi, tt, :], in1=tmp[u][:], op=ALU.add),
                                         reads=[f"tmp{u}", ak], writes=[ak])
                                else:
                                    ms = fbi % 2
                                    P.op("dve", lambda e, u=u, fbi=fbi, tt=tt, ms=ms: e.tensor_tensor(out=mst[ms][:, tt * 512:(tt + 1) * 512], in0=acc[:, fbi, tt, :], in1=tmp[u][:], op=ALU.add),
                                         reads=[f"tmp{u}", ak], writes=[f"mst{ms}"])
                                    if tt == 1:
                                        r0 = cb * 512 + fbi * 128
                                        P.dma("sp", mergedT_d[r0:r0 + 128, t0:t0 + 1024], mst[ms][:], reads=[f"mst{ms}"], writes=["mergedT_d"])
    P.barrier()


def phase_outproj(P, mergedT_d, w_out, xsrc, xdst):
    with ExitStack() as st:
        mT = P.sbuf([128, KC, T], BF16, stack=st)
        for q4 in range(4):
            P.dma("sp", mT[:, q4 * 4:(q4 + 1) * 4, :], mergedT_d[q4 * 512:(q4 + 1) * 512, :].rearrange("(kc p) t -> p kc t", p=128), writes=["mT"])
        wt = [P.sbuf([128, KC, 512], BF16, stack=st) for _ in range(2)]
        xt = [P.sbuf([128, 512], F32, stack=st) for _ in range(3)]
        n = 0
        for cb in range(4):
            s = cb % 2
            P.dma("pool", wt[s][:], w_out[:, cb * 512:(cb + 1) * 512].rearrange("(kc p) c -> p kc c", p=128), writes=[f"wt{s}"])
            for tt in range(T // 128):
                u = n % 3
                n += 1
                P.dma("sp", xt[u][:], xsrc[tt * 128:(tt + 1) * 128, cb * 512:(cb + 1) * 512], writes=[f"xt{u}"])
                b = next_bank(P)
                for kc in range(KC):
                    mm(P, P.banks[b][:], mT[:, kc, tt * 128:(tt + 1) * 128], wt[s][:, kc, :], kc == 0, kc == KC - 1, [f"wt{s}", "mT"], [f"pb{b}"])
                P.op("dve", lambda e, b=b, u=u: e.tensor_tensor(out=xt[u][:], in0=xt[u][:], in1=P.banks[b][:], op=ALU.add), reads=[f"pb{b}", f"xt{u}"], writes=[f"xt{u}"])
                P.dma("act", xdst[tt * 128:(tt + 1) * 128, cb * 512:(cb + 1) * 512], xt[u][:], reads=[f"xt{u}"], writes=[f"xd{tt}_{cb}"])
    P.barrier()


def phase_ffn_up(P, hT_d, w_g, w_u, act_d):
    with ExitStack() as st:
        hT = P.sbuf([128, KC, T], BF16, stack=st)
        for q4 in range(4):
            P.dma("sp", hT[:, q4 * 4:(q4 + 1) * 4, :], hT_d[q4 * 512:(q4 + 1) * 512, :].rearrange("(kc p) t -> p kc t", p=128), writes=["hT"])
        wg = [P.sbuf([128, KC, 512], BF16, stack=st) for _ in range(2)]
        wu = [P.sbuf([128, KC, 512], BF16, stack=st) for _ in range(2)]
        sg = [P.sbuf([128, 512], F32, stack=st) for _ in range(2)]
        ast = [P.sbuf([128, 512], BF16, stack=st) for _ in range(3)]
        it = 0
        for hb in range(FFN // 512):
            s = hb % 2
            P.dma("pool", wg[s][:], w_g[:, hb * 512:(hb + 1) * 512].rearrange("(kc p) c -> p kc c", p=128), writes=[f"wg{s}"])
            P.dma("pool", wu[s][:], w_u[:, hb * 512:(hb + 1) * 512].rearrange("(kc p) c -> p kc c", p=128), writes=[f"wu{s}"])
            for fbi in range(4):
                kcf = hb * 4 + fbi
                for tt in range(4):
                    bg = next_bank(P)
                    bu = next_bank(P)
                    for kc in range(KC):
                        mm(P, P.banks[bg][:], wg[s][:, kc, fbi * 128:(fbi + 1) * 128], hT[:, kc, tt * 512:(tt + 1) * 512], kc == 0, kc == KC - 1, [f"wg{s}", "hT"], [f"pb{bg}"])
                    for kc in range(KC):
                        mm(P, P.banks[bu][:], wu[s][:, kc, fbi * 128:(fbi + 1) * 128], hT[:, kc, tt * 512:(tt + 1) * 512], kc == 0, kc == KC - 1, [f"wu{s}", "hT"], [f"pb{bu}"])
                    u = it % 2
                    a3 = it % 3
                    it += 1
                    P.op("act", lambda e, bg=bg, u=u: e.activation(out=sg[u][:], in_=P.banks[bg][:], func=AF.Silu), reads=[f"pb{bg}"], writes=[f"sg{u}"])
                    P.op("dve", lambda e, bu=bu, u=u, a3=a3: e.tensor_tensor(out=ast[a3][:], in0=sg[u][:], in1=P.banks[bu][:], op=ALU.mult),
                         reads=[f"sg{u}", f"pb{bu}"], writes=[f"ast{a3}"])
                    P.dma("sp", act_d[tt * 4:(tt + 1) * 4, :, kcf, :].rearrange("a p t -> p a t"), ast[a3][:].rearrange("p (a t) -> p a t", a=4),
                          reads=[f"ast{a3}"], writes=["act_d"])
    P.barrier()


def phase_ffn_down(P, act_d, w_d, xsrc, xdst):
    NK = FFN // 128
    with ExitStack() as st:
        wd = [P.sbuf([128, NK, 512], BF16, stack=st) for _ in range(2)]
        at = [P.sbuf([128, NK, 128], BF16, stack=st) for _ in range(2)]
        xt = [P.sbuf([128, 512], F32, stack=st) for _ in range(3)]
        n = 0
        for cb in range(4):
            s = cb % 2
            for h2 in range(2):
                P.dma("pool", wd[s][:, h2 * 22:(h2 + 1) * 22, :], w_d[h2 * 2816:(h2 + 1) * 2816, cb * 512:(cb + 1) * 512].rearrange("(kc p) c -> p kc c", p=128), writes=[f"wd{s}"])
            for tt in range(T // 128):
                u = n % 3
                a = n % 2
                n += 1
                P.dma("sp", at[a][:], act_d[tt], writes=[f"at{a}"])
                P.dma("sp", xt[u][:], xsrc[tt * 128:(tt + 1) * 128, cb * 512:(cb + 1) * 512], writes=[f"xt{u}"])
                b = next_bank(P)
                for kc in range(NK):
                    mm(P, P.banks[b][:], at[a][:, kc, :], wd[s][:, kc, :], kc == 0, kc == NK - 1, [f"wd{s}", f"at{a}"], [f"pb{b}"])
                P.op("dve", lambda e, b=b, u=u: e.tensor_tensor(out=xt[u][:], in0=xt[u][:], in1=P.banks[b][:], op=ALU.add), reads=[f"pb{b}", f"xt{u}"], writes=[f"xt{u}"])
                P.dma("act", xdst[tt * 128:(tt + 1) * 128, cb * 512:(cb + 1) * 512], xt[u][:], reads=[f"xt{u}"], writes=[f"xd{tt}_{cb}"])
    P.barrier()


def phase_final_norm(P, xsrc, gain, out_d):
    with ExitStack() as st:
        gb = P.sbuf([128, D], F32, stack=st)
        P.dma("sp", gb[:], gain.partition_broadcast(128), writes=["gb"])
        xt = [P.sbuf([128, D], F32, stack=st) for _ in range(2)]
        junk = P.sbuf([128, D], BF16, stack=st)
        ss = [P.sbuf([128, 1], F32, stack=st) for _ in range(2)]
        rs = [P.sbuf([128, 1], F32, stack=st) for _ in range(2)]
        for tt in range(T // 128):
            s = tt % 2
            P.dma("sp", xt[s][:], xsrc[tt * 128:(tt + 1) * 128, :], writes=[f"xt{s}"])
            P.op("pool", lambda e, s=s: e.memset(ss[s][:], 0.0), writes=[f"ss{s}"])
            P.op("act", lambda e, s=s: e.activation(out=junk[:], in_=xt[s][:], func=AF.Square, accum_out=ss[s][:]), reads=[f"xt{s}"], writes=["junk", f"ss{s}"])
            rstd_from_ss(P, ss[s][:], rs[s][:], 1.0 / D, EPS, f"ss{s}", f"rs{s}")
            P.op("dve", lambda e, s=s: e.scalar_tensor_tensor(out=xt[s][:], in0=xt[s][:], scalar=rs[s][:, 0:1], in1=gb[:], op0=ALU.mult, op1=ALU.mult),
                 reads=[f"xt{s}", f"rs{s}", "gb"], writes=[f"xt{s}"])
            P.dma("sp", out_d[tt * 128:(tt + 1) * 128, :], xt[s][:], reads=[f"xt{s}"], writes=[f"out{tt}"])
    P.barrier()

import math
import numpy as np


def phase_attn_a(P, projT_d, avt_d, qnorm, knorm, obrT_d):
    c = P.consts
    with ExitStack() as st:
        cosT = P.sbuf([128, T], F32, stack=st)
        sinT = P.sbuf([128, T], F32, stack=st)
        Rm = P.sbuf([128, 128], BF16, stack=st)
        P.dma("sp", cosT[:], c["ropecos"], writes=["cosT"])
        P.dma("sp", sinT[:], c["ropesin"], writes=["sinT"])
        P.dma("pool", Rm[:], c["rotmat"], writes=["Rm"])
        gq = P.sbuf([128, 1], F32, stack=st)
        gk = P.sbuf([128, 1], F32, stack=st)
        P.dma("sp", gq[:], qnorm.rearrange("(p o) -> p o", o=1), writes=["gq"])
        P.dma("sp", gk[:], knorm.rearrange("(p o) -> p o", o=1), writes=["gk"])
        P.op("dve", lambda e: e.tensor_scalar(out=gq[:], in0=gq[:], scalar1=128.0 ** -0.5, scalar2=None, op0=ALU.mult), reads=["gq"], writes=["gq"])
        qr = P.sbuf([128, 10, T], BF16, stack=st)
        with ExitStack() as st2:
            qf = [P.sbuf([128, T], F32, stack=st2) for _ in range(2)]
            sq = [P.sbuf([128, T], BF16, stack=st2) for _ in range(2)]
            rstd = [P.sbuf([128, 512], F32, stack=st2) for _ in range(2)]
            qn = [P.sbuf([128, 512], BF16, stack=st2) for _ in range(2)]
            t1 = [P.sbuf([128, 512], F32, stack=st2) for _ in range(2)]
            t2 = [P.sbuf([128, 512], F32, stack=st2) for _ in range(2)]
            n = 0
            for hh in range(10):
                s = hh % 2
                r0 = hh * 128 if hh < 8 else 1024 + (hh - 8) * 128
                g = gq if hh < 8 else gk
                gkey = "gq" if hh < 8 else "gk"
                P.dma("sp", qf[s][:], projT_d[r0:r0 + 128, :], writes=[f"qf{s}"])
                P.op("act", lambda e, s=s: e.activation(out=sq[s][:], in_=qf[s][:], func=AF.Square), reads=[f"qf{s}"], writes=[f"sq{s}"])
                for tt in range(4):
                    u = n % 2
                    n += 1
                    sl = slice(tt * 512, (tt + 1) * 512)
                    b = next_bank(P)
                    mm(P, P.banks[b][:], P.ones_b[:], sq[s][:, sl], True, True, ["ones_b", f"sq{s}"], [f"pb{b}"])
                    P.op("act", lambda e, b=b, u=u: e.activation(out=rstd[u][:], in_=P.banks[b][:], func=AF.Sqrt, bias=EPS, scale=1.0 / 128), reads=[f"pb{b}"], writes=[f"rstd{u}"])
                    P.op("dve", lambda e, u=u: e.reciprocal(out=rstd[u][:], in_=rstd[u][:]), reads=[f"rstd{u}"], writes=[f"rstd{u}"])
                    P.op("dve", lambda e, u=u, s=s, sl=sl, g=g: e.scalar_tensor_tensor(out=qn[u][:], in0=qf[s][:, sl], scalar=g[:, 0:1], in1=rstd[u][:], op0=ALU.mult, op1=ALU.mult),
                         reads=[f"qf{s}", gkey, f"rstd{u}"], writes=[f"qn{u}"])
                    b2 = next_bank(P)
                    mm(P, P.banks[b2][:], Rm[:], qn[u][:], True, True, ["Rm", f"qn{u}"], [f"pb{b2}"])
                    P.op("pool", lambda e, u=u, sl=sl: e.tensor_tensor(out=t1[u][:], in0=qn[u][:], in1=cosT[:, sl], op=ALU.mult), reads=[f"qn{u}", "cosT"], writes=[f"t1{u}"])
                    P.op("dve", lambda e, u=u, sl=sl, b2=b2: e.tensor_tensor(out=t2[u][:], in0=P.banks[b2][:], in1=sinT[:, sl], op=ALU.mult), reads=[f"pb{b2}", "sinT"], writes=[f"t2{u}"])
                    P.op("pool", lambda e, u=u, sl=sl, hh=hh: e.tensor_tensor(out=qr[:, hh, sl], in0=t1[u][:], in1=t2[u][:], op=ALU.add), reads=[f"t1{u}", f"t2{u}"], writes=["qr"])
        P.barrier()
        V = [P.sbuf([128, 16, 128], BF16, stack=st) for _ in range(2)]
        pT = [P.sbuf([128, 512], BF16, stack=st) for _ in range(4)]
        rden = [P.sbuf([128, 512], F32, stack=st) for _ in range(2)]
        ost = [P.sbuf([128, 512], BF16, stack=st) for _ in range(2)]
        iters = [(g, hq, qt, kt) for g in range(2) for hq in range(4) for qt in range(4) for kt in range(16)]
        sbank = {}

        def issue_S(it):
            g, hq, qt, kt = it
            h = g * 4 + hq
            if hq == 0 and qt == 0 and kt == 0:
                P.dma("sp", V[g][:], avt_d[:, g * 128:(g + 1) * 128].rearrange("(kt p) d -> p kt d", p=128), writes=[f"V{g}"])
            bs = next_bank(P, 0, 4)
            sbank[it] = bs
            mm(P, P.banks[bs][:], qr[:, 8 + g, kt * 128:(kt + 1) * 128], qr[:, h, qt * 512:(qt + 1) * 512], True, True, ["qr"], [f"pb{bs}"])

        DEPTH = 2
        for n in range(DEPTH):
            issue_S(iters[n])
        for n, it in enumerate(iters):
            if n + DEPTH < len(iters):
                issue_S(iters[n + DEPTH])
            g, hq, qt, kt = it
            h = g * 4 + hq
            u = (n // 16) % 2
            bo, bd = 4 + u, 6 + u
            qs = slice(qt * 512, (qt + 1) * 512)
            bs = sbank.pop(it)
            v = n % 4
            P.op("act", lambda e, bs=bs, v=v: e.activation(out=pT[v][:], in_=P.banks[bs][:], func=AF.Exp), reads=[f"pb{bs}"], writes=[f"pT{v}"])
            mm(P, P.banks[bo][:], V[g][:, kt, :], pT[v][:], kt == 0, kt == 15, [f"V{g}", f"pT{v}"], [f"pb{bo}"])
            mm(P, P.banks[bd][:], P.ones_b[:], pT[v][:], kt == 0, kt == 15, ["ones_b", f"pT{v}"], [f"pb{bd}"])
            if kt == 15:
                P.op("dve", lambda e, u=u, bd=bd: e.reciprocal(out=rden[u][:], in_=P.banks[bd][:]), reads=[f"pb{bd}"], writes=[f"rden{u}"])
                P.op("dve", lambda e, u=u, bo=bo: e.tensor_tensor(out=ost[u][:], in0=P.banks[bo][:], in1=rden[u][:], op=ALU.mult), reads=[f"pb{bo}", f"rden{u}"], writes=[f"ost{u}"])
                P.dma("sp", obrT_d[h * 128:(h + 1) * 128, qs], ost[u][:], reads=[f"ost{u}"], writes=["obrT_d"])
    P.barrier()


def phase_attn_d(P, projT_d, dvt_d, lq1, lk1, lq2, lk2, subln, lam_init, obrT_d):
    c = P.consts
    QOFF, KOFF = 5056, 5568
    with ExitStack() as st:
        dist = P.sbuf([128, 3968], F32, stack=st)
        P.dma("sp", dist[:], c["alibi"], writes=["dist"])
        lv = [P.sbuf([128, 64], F32, stack=st) for _ in range(4)]
        for i, a in enumerate((lq1, lk1, lq2, lk2)):
            P.dma("sp", lv[i][:], a.partition_broadcast(128), writes=[f"lv{i}"])
        pr = P.sbuf([128, 64], F32, stack=st)
        e12 = P.sbuf([128, 2], F32, stack=st)
        nlam = P.sbuf([128, 1], F32, stack=st)
        for j in range(2):
            P.op("dve", lambda e, j=j: e.tensor_tensor(out=pr[:], in0=lv[2 * j][:], in1=lv[2 * j + 1][:], op=ALU.mult), reads=[f"lv{2*j}", f"lv{2*j+1}"], writes=["pr"])
            P.op("dve", lambda e, j=j: e.reduce_sum(out=e12[:, j:j + 1], in_=pr[:], axis=AX.X), reads=["pr"], writes=["e12"])
        P.op("act", lambda e: e.activation(out=e12[:], in_=e12[:], func=AF.Exp), reads=["e12"], writes=["e12"])
        P.op("dve", lambda e: e.tensor_tensor(out=nlam[:], in0=e12[:, 1:2], in1=e12[:, 0:1], op=ALU.subtract), reads=["e12"], writes=["nlam"])
        P.op("dve", lambda e: e.tensor_scalar(out=nlam[:], in0=nlam[:], scalar1=-float(lam_init), scalar2=None, op0=ALU.add), reads=["nlam"], writes=["nlam"])
        gs = P.sbuf([128, 1], F32, stack=st)
        P.dma("sp", gs[:], subln.rearrange("(p o) -> p o", o=1), writes=["gs"])
        P.op("dve", lambda e: e.tensor_scalar(out=gs[:], in0=gs[:], scalar1=float(1.0 - lam_init), scalar2=None, op0=ALU.mult), reads=["gs"], writes=["gs"])
        qd = P.sbuf([128, 4, T], BF16, stack=st)
        kd = P.sbuf([128, 4, T], BF16, stack=st)
        with ExitStack() as st2:
            tmpf = [P.sbuf([128, T], F32, stack=st2) for _ in range(2)]
            for i in range(8):
                s = i % 2
                h = i % 4
                isq = i < 4
                r0 = (QOFF if isq else KOFF) + h * 128
                P.dma("sp", tmpf[s][:], projT_d[r0:r0 + 128, :], writes=[f"tmpf{s}"])
                dst = qd if isq else kd
                P.op("act", lambda e, s=s, h=h, dst=dst, isq=isq: e.mul(out=dst[:, h, :], in_=tmpf[s][:], mul=(0.125 if isq else 1.0)),
                     reads=[f"tmpf{s}"], writes=["qd" if isq else "kd"])
        P.barrier()
        V = [P.sbuf([128, 16, 128], BF16, stack=st) for _ in range(2)]
        sb = [P.sbuf([128, 512], F32, stack=st) for _ in range(4)]
        pT = [P.sbuf([128, 512], BF16, stack=st) for _ in range(4)]
        rd = [P.sbuf([128, 512], F32, stack=st) for _ in range(2)]
        o0 = P.sbuf([128, 512], F32, stack=st)
        o1 = P.sbuf([128, 512], F32, stack=st)
        osq = P.sbuf([128, 512], F32, stack=st)
        rstd = P.sbuf([128, 512], F32, stack=st)
        ost = [P.sbuf([128, 512], BF16, stack=st) for _ in range(2)]
        iters = [(h, qt, kt, cc) for h in range(4) for qt in range(4) for kt in range(16) for cc in range(2)]
        sbank = {}

        def issue_S(it):
            h, qt, kt, cc = it
            vs = h % 2
            if qt == 0 and kt == 0 and cc == 0:
                P.dma("sp", V[vs][:], dvt_d[:, h * 128:(h + 1) * 128].rearrange("(kt p) d -> p kt d", p=128), writes=[f"V{vs}"])
            bs = next_bank(P, 0, 4)
            sbank[it] = bs
            pr_ = slice(cc * 64, (cc + 1) * 64)
            mm(P, P.banks[bs][:], kd[pr_, h, kt * 128:(kt + 1) * 128], qd[pr_, h, qt * 512:(qt + 1) * 512], True, True, ["qd", "kd"], [f"pb{bs}"])

        DEPTH = 2
        for n in range(DEPTH):
            issue_S(iters[n])
        nq = 0
        for n, it in enumerate(iters):
            if n + DEPTH < len(iters):
                issue_S(iters[n + DEPTH])
            h, qt, kt, cc = it
            slope = 2.0 ** (-2.0 * (h + 1))
            vs = h % 2
            qs = slice(qt * 512, (qt + 1) * 512)
            off = qt * 512 - kt * 128 + 1920
            bs = sbank.pop(it)
            v = n % 4
            P.op("dve", lambda e, v=v, bs=bs, off=off, slope=slope: e.scalar_tensor_tensor(out=sb[v][:], in0=dist[:, off:off + 512], scalar=-slope, in1=P.banks[bs][:], op0=ALU.mult, op1=ALU.add),
                 reads=["dist", f"pb{bs}"], writes=[f"sb{v}"])
            P.op("act", lambda e, v=v: e.activation(out=pT[v][:], in_=sb[v][:], func=AF.Exp), reads=[f"sb{v}"], writes=[f"pT{v}"])
            mm(P, P.banks[4 + cc][:], V[vs][:, kt, :], pT[v][:], kt == 0, kt == 15, [f"V{vs}", f"pT{v}"], [f"pb{4+cc}"])
            mm(P, P.banks[6 + cc][:], P.ones_b[:], pT[v][:], kt == 0, kt == 15, ["ones_b", f"pT{v}"], [f"pb{6+cc}"])
            if kt == 15 and cc == 1:
                for c2 in range(2):
                    P.op("dve", lambda e, c2=c2: e.reciprocal(out=rd[c2][:], in_=P.banks[6 + c2][:]), reads=[f"pb{6+c2}"], writes=[f"rd{c2}"])
                P.op("dve", lambda e: e.tensor_tensor(out=o0[:], in0=P.banks[4][:], in1=rd[0][:], op=ALU.mult), reads=["pb4", "rd0"], writes=["o0"])
                P.op("dve", lambda e: e.tensor_tensor(out=o1[:], in0=P.banks[5][:], in1=rd[1][:], op=ALU.mult), reads=["pb5", "rd1"], writes=["o1"])
                P.op("dve", lambda e: e.scalar_tensor_tensor(out=o0[:], in0=o1[:], scalar=nlam[:, 0:1], in1=o0[:], op0=ALU.mult, op1=ALU.add), reads=["o0", "o1", "nlam"], writes=["o0"])
                P.op("act", lambda e: e.activation(out=osq[:], in_=o0[:], func=AF.Square), reads=["o0"], writes=["osq"])
                bn_ = next_bank(P, 0, 4)
                mm(P, P.banks[bn_][:], P.ones_f[:], osq[:], True, True, ["ones_f", "osq"], [f"pb{bn_}"])
                P.op("act", lambda e, bn_=bn_: e.activation(out=rstd[:], in_=P.banks[bn_][:], func=AF.Sqrt, bias=EPS, scale=1.0 / 128), reads=[f"pb{bn_}"], writes=["rstd"])
                P.op("dve", lambda e: e.reciprocal(out=rstd[:], in_=rstd[:]), reads=["rstd"], writes=["rstd"])
                u = nq % 2
                nq += 1
                P.op("dve", lambda e, u=u: e.scalar_tensor_tensor(out=ost[u][:], in0=o0[:], scalar=gs[:, 0:1], in1=rstd[:], op0=ALU.mult, op1=ALU.mult), reads=["o0", "gs", "rstd"], writes=[f"ost{u}"])
                P.dma("sp", obrT_d[2048 + h * 128:2048 + (h + 1) * 128, qs], ost[u][:], reads=[f"ost{u}"], writes=["obrT_d"])
    P.barrier()


def attn_consts():
    f32 = np.float32
    GRID_W, HD = 64, 128
    rows = T // GRID_W
    row_idx = np.repeat(np.arange(rows, dtype=f32), GRID_W)
    col_idx = np.tile(np.arange(GRID_W, dtype=f32), rows)
    axis_dim = HD // 2
    inv_freq = (f32(10000.0) ** (-np.arange(0, axis_dim, 2, dtype=f32) / f32(axis_dim))).astype(f32)
    ang_r = row_idx[:, None] * inv_freq[None, :]
    ang_c = col_idx[:, None] * inv_freq[None, :]
    ang = np.concatenate([ang_r, ang_r, ang_c, ang_c], axis=-1).astype(f32)
    ropecos = np.ascontiguousarray(np.cos(ang).T.astype(f32))
    ropesin = np.ascontiguousarray(np.sin(ang).T.astype(f32))
    Rm = np.zeros((128, 128), f32)
    for half in range(2):
        for i in range(32):
            j = half * 64 + i
            Rm[j + 32, j] = -1.0
            Rm[j, j + 32] = 1.0
    m = np.arange(3968, dtype=f32)[None, :]
    p = np.arange(128, dtype=f32)[:, None]
    alibi = np.abs(m - 1920.0 - p).astype(f32)
    return {"ropecos": ropecos, "ropesin": ropesin, "rotmat": Rm, "alibi": alibi}

import math
import numpy as np

TWO_PI = 2.0 * math.pi
UOFF = 1536


def sin_rr(P, dst, src, tmp, tmpi, ksrc, kdst, ktmp):
    P.op("dve", lambda e: e.tensor_scalar(out=tmp, in0=src, scalar1=1.0 / TWO_PI, scalar2=16.0, op0=ALU.mult, op1=ALU.add), reads=[ksrc], writes=[ktmp])
    P.op("dve", lambda e: e.tensor_copy(out=tmpi, in_=tmp), reads=[ktmp], writes=[ktmp + "i"])
    P.op("dve", lambda e: e.tensor_copy(out=tmp, in_=tmpi), reads=[ktmp + "i"], writes=[ktmp])
    P.op("dve", lambda e: e.tensor_scalar(out=tmp, in0=tmp, scalar1=-16.0, scalar2=None, op0=ALU.add), reads=[ktmp], writes=[ktmp])
    P.op("dve", lambda e: e.scalar_tensor_tensor(out=src, in0=tmp, scalar=-TWO_PI, in1=src, op0=ALU.mult, op1=ALU.add), reads=[ktmp, ksrc], writes=[ksrc])
    P.op("dve", lambda e: e.tensor_scalar(out=src, in0=src, scalar1=3.1415925, scalar2=-3.1415925, op0=ALU.min, op1=ALU.max), reads=[ksrc], writes=[ksrc])
    P.op("act", lambda e: e.activation(out=dst, in_=src, func=AF.Sin), reads=[ksrc], writes=[kdst])


def phase_hyena(P, projT_d, prm, zx_d, obrT_d):
    c = P.consts
    col = lambda a: a.rearrange("(p o) -> p o", o=1)
    with ExitStack() as st:
      Yre = P.sbuf([128, 16, 512], BF16, stack=st)
      Yim = P.sbuf([128, 16, 512], BF16, stack=st)
      skc = P.sbuf([128, 4], F32, stack=st)
      P.dma("sp", skc[:], prm["b_skip"].rearrange("(cc p) -> p cc", p=128), writes=["skc"], allow_slow_non_contiguous=True)
      with ExitStack() as sK:
        st_outer = st
        st = sK
        Kre = P.sbuf([128, 16, 512], F32, stack=st)
        Kim = P.sbuf([128, 16, 512], F32, stack=st)
        sc = P.sbuf([128, 16], F32, stack=st)
        P.dma("sp", sc[:], c["dft_sc"], writes=["sc"])
        alt = P.sbuf([128, 1], BF16, stack=st)
        P.dma("pool", alt[:], c["altsign"], writes=["alt"])
        with ExitStack() as s2:
          Hs = P.sbuf([128, 16, 512], BF16, stack=s2)
          Hd = P.sbuf([128, 16, 512], BF16, stack=s2)
          rnorm = P.sbuf([128, 512], F32, stack=s2)
          with ExitStack() as s2a:
            s2_outer = s2
            s2 = s2a
            zT = P.sbuf([33, T], F32, stack=s2)
            P.dma("sp", zT[:], c["hy_z"], writes=["zT"])
            w1 = P.sbuf([33, 64], F32, stack=s2)
            w2 = P.sbuf([64, 64], F32, stack=s2)
            w3 = P.sbuf([64, 64], F32, stack=s2)
            w4 = P.sbuf([64, 1024], F32, stack=s2)
            for t_, a, k in ((w1, prm["b_filt_w1"], "w1"), (w2, prm["b_filt_w2"], "w2"), (w3, prm["b_filt_w3"], "w3"), (w4, prm["b_filt_w4"], "w4")):
                P.dma("sp", t_[:], a, writes=[k])
            bcol = P.sbuf([64, 4], F32, stack=s2)
            for i, nm in enumerate(("b_filt_b1", "b_filt_b2", "b_filt_b3", "b_filt_freq")):
                P.dma("sp", bcol[:, i:i + 1], col(prm[nm]), writes=["bcol"])
            hid = [P.sbuf([64, T], F32, stack=s2) for _ in range(2)]
            arg = [P.sbuf([64, 512], F32, stack=s2) for _ in range(2)]
            tmp = [P.sbuf([64, 512], F32, stack=s2) for _ in range(2)]
            tmpi = [P.sbuf([64, 512], I32, stack=s2) for _ in range(2)]
            n = 0
            for layer in range(3):
                w = (w1, w2, w3)[layer]
                wk = ("w1", "w2", "w3")[layer]
                kin = 33 if layer == 0 else 64
                src = zT if layer == 0 else hid[(layer - 1) % 2]
                skey = "zT" if layer == 0 else f"hid{(layer - 1) % 2}"
                dst = hid[layer % 2]
                for tt in range(4):
                    u = n % 2
                    n += 1
                    b = next_bank(P)
                    sl = slice(tt * 512, (tt + 1) * 512)
                    mm(P, P.banks[b][0:64, :], w[0:kin, :], src[0:kin, sl], True, True, [wk, skey], [f"pb{b}"])
                    P.op("dve", lambda e, b=b, u=u, layer=layer: e.tensor_scalar(out=arg[u][:], in0=P.banks[b][0:64, :], scalar1=bcol[:, layer:layer + 1], scalar2=bcol[:, 3:4], op0=ALU.add, op1=ALU.mult),
                         reads=[f"pb{b}", "bcol"], writes=[f"arg{u}"])
                    sin_rr(P, dst[:, sl], arg[u][:], tmp[u][:], tmpi[u][:], f"arg{u}", f"hid{layer % 2}", f"tmp{u}")
            hid3 = hid[0]
            win = [P.sbuf([128, 512], F32, stack=s2) for _ in range(2)]
            hf = [P.sbuf([128, 512], F32, stack=s2) for _ in range(2)]
            hb = [P.sbuf([128, 512], F32, stack=s2) for _ in range(2)]
            ab = [P.sbuf([128, 1024], F32, stack=s2) for _ in range(2)]
            for tt in range(16):
                u = tt % 2
                P.dma("sp", win[u][:], c["hy_win"][tt * 128:(tt + 1) * 128, :], writes=[f"win{u}"])
                bf_ = next_bank(P, 0, 6)
                bb_ = next_bank(P, 0, 6)
                mm(P, P.banks[bf_][:], hid3[0:64, tt * 128:(tt + 1) * 128], w4[0:64, 0:512], True, True, ["hid0", "w4"], [f"pb{bf_}"])
                mm(P, P.banks[bb_][:], hid3[0:64, tt * 128:(tt + 1) * 128], w4[0:64, 512:1024], True, True, ["hid0", "w4"], [f"pb{bb_}"])
                P.op("dve", lambda e, u=u, bf_=bf_: e.tensor_tensor(out=hf[u][:], in0=P.banks[bf_][:], in1=win[u][:], op=ALU.mult), reads=[f"pb{bf_}", f"win{u}"], writes=[f"hf{u}"])
                P.op("dve", lambda e, u=u, bb_=bb_: e.tensor_tensor(out=hb[u][:], in0=P.banks[bb_][:], in1=win[u][:], op=ALU.mult), reads=[f"pb{bb_}", f"win{u}"], writes=[f"hb{u}"])
                if tt == 0:
                    P.op("dve", lambda e, u=u: e.memset(hb[u][0:1, :], 0.0), reads=[f"hb{u}"], writes=[f"hb{u}"])
                P.op("act", lambda e, u=u: e.activation(out=ab[u][:, 0:512], in_=hf[u][:], func=AF.Abs), reads=[f"hf{u}"], writes=[f"ab{u}"])
                P.op("act", lambda e, u=u: e.activation(out=ab[u][:, 512:1024], in_=hb[u][:], func=AF.Abs), reads=[f"hb{u}"], writes=[f"ab{u}"])
                mm(P, P.banks[7][:], P.ones_f[:], ab[u][:, 0:512], tt == 0, False, ["ones_f", f"ab{u}"], ["pb7"])
                mm(P, P.banks[7][:], P.ones_f[:], ab[u][:, 512:1024], False, tt == 15, ["ones_f", f"ab{u}"], ["pb7"])
                P.op("pool", lambda e, u=u, tt=tt: e.tensor_tensor(out=Hs[:, tt, :], in0=hf[u][:], in1=hb[u][:], op=ALU.add), reads=[f"hf{u}", f"hb{u}"], writes=["Hs"])
                P.op("pool", lambda e, u=u, tt=tt: e.tensor_tensor(out=Hd[:, tt, :], in0=hf[u][:], in1=hb[u][:], op=ALU.subtract), reads=[f"hf{u}", f"hb{u}"], writes=["Hd"])
            P.op("dve", lambda e: e.reciprocal(out=rnorm[:], in_=P.banks[7][:]), reads=["pb7"], writes=["rnorm"])
          P.barrier()
          if True:
            s2 = s2_outer
            Ct = [P.sbuf([128, 16, 256], BF16, stack=s2) for _ in range(2)]
            St = [P.sbuf([128, 16, 256], BF16, stack=s2) for _ in range(2)]
            for f4 in range(8):
                u = f4 % 2
                P.dma("sp", Ct[u][:], c["dft_cos"][:, f4 * 256:(f4 + 1) * 256].rearrange("(kc p) f -> p kc f", p=128), writes=[f"Ct{u}"])
                P.dma("sp", St[u][:], c["dft_sn"][:, f4 * 256:(f4 + 1) * 256].rearrange("(kc p) f -> p kc f", p=128), writes=[f"St{u}"])
                for fi in range(2):
                    fc = f4 * 2 + fi
                    br = next_bank(P, 0, 6)
                    bi = next_bank(P, 0, 6)
                    for tc in range(16):
                        mm(P, P.banks[br][:], Ct[u][:, tc, fi * 128:(fi + 1) * 128], Hs[:, tc, :], tc == 0, tc == 15, [f"Ct{u}", "Hs"], [f"pb{br}"])
                    for tc in range(16):
                        mm(P, P.banks[bi][:], St[u][:, tc, fi * 128:(fi + 1) * 128], Hd[:, tc, :], tc == 0, tc == 15, [f"St{u}", "Hd"], [f"pb{bi}"])
                    P.op("dve", lambda e, br=br, fc=fc: e.scalar_tensor_tensor(out=Kre[:, fc, :], in0=P.banks[br][:], scalar=sc[:, fc:fc + 1], in1=rnorm[:], op0=ALU.mult, op1=ALU.mult),
                         reads=[f"pb{br}", "sc", "rnorm"], writes=["Kre"])
                    P.op("dve", lambda e, bi=bi, fc=fc: e.scalar_tensor_tensor(out=Kim[:, fc, :], in0=P.banks[bi][:], scalar=sc[:, fc:fc + 1], in1=rnorm[:], op0=ALU.mult, op1=ALU.mult),
                         reads=[f"pb{bi}", "sc", "rnorm"], writes=["Kim"])
            bn = next_bank(P, 0, 6)
            for tc in range(16):
                mm(P, P.banks[bn][0:1, :], alt[:, 0:1], Hs[:, tc, :], tc == 0, tc == 15, ["alt", "Hs"], [f"pb{bn}"])
            P.op("dve", lambda e, bn=bn: e.scalar_tensor_tensor(out=Kim[0:1, 0, :], in0=P.banks[bn][0:1, :], scalar=sc[0:1, 0:1], in1=rnorm[0:1, :], op0=ALU.mult, op1=ALU.mult),
                 reads=[f"pb{bn}", "sc", "rnorm", "Kim"], writes=["Kim"])
        P.barrier()
        z_tm = P.sbuf([128, 16, 512], BF16, stack=st)
        with ExitStack() as s3:
            cw = P.sbuf([128, 12, 3], F32, stack=s3)
            cb = P.sbuf([128, 12], F32, stack=s3)
            for k_ in range(3):
                P.dma("sp", cw[:, :, k_], prm["b_conv_w"][k_].rearrange("(cc p) -> p cc", p=128), writes=["cw"], allow_slow_non_contiguous=True)
            P.dma("sp", cb[:], prm["b_conv_b"].rearrange("(cc p) -> p cc", p=128), writes=["cb"], allow_slow_non_contiguous=True)
            uu = [P.sbuf([128, T], F32, stack=s3) for _ in range(3)]
            uc = [P.sbuf([128, T], F32, stack=s3) for _ in range(3)]
            for j in range(4):
                for part in range(3):
                    cc = part * 4 + j
                    r0 = UOFF + cc * 128
                    P.dma("sp", uu[part][:], projT_d[r0:r0 + 128, :], writes=[f"uu{part}"])
                    P.op("dve", lambda e, part=part, cc=cc: e.tensor_scalar(out=uc[part][:], in0=uu[part][:], scalar1=cw[:, cc, 1:2], scalar2=cb[:, cc:cc + 1], op0=ALU.mult, op1=ALU.add),
                         reads=[f"uu{part}", "cw", "cb"], writes=[f"uc{part}"])
                    P.op("dve", lambda e, part=part, cc=cc: e.scalar_tensor_tensor(out=uc[part][:, 1:T], in0=uu[part][:, 0:T - 1], scalar=cw[:, cc, 0:1], in1=uc[part][:, 1:T], op0=ALU.mult, op1=ALU.add),
                         reads=[f"uu{part}", "cw", f"uc{part}"], writes=[f"uc{part}"])
                    P.op("dve", lambda e, part=part, cc=cc: e.scalar_tensor_tensor(out=uc[part][:, 0:T - 1], in0=uu[part][:, 1:T], scalar=cw[:, cc, 2:3], in1=uc[part][:, 0:T - 1], op0=ALU.mult, op1=ALU.add),
                         reads=[f"uu{part}", "cw", f"uc{part}"], writes=[f"uc{part}"])
                P.op("pool", lambda e: e.tensor_tensor(out=uc[0][:], in0=uc[0][:], in1=uc[1][:], op=ALU.mult), reads=["uc0", "uc1"], writes=["uc0"])
                P.dma("pool", zx_d[0, j * 128:(j + 1) * 128, :], uc[0][:], reads=["uc0"], writes=["zx_d"])
                P.dma("pool", zx_d[1, j * 128:(j + 1) * 128, :], uc[2][:], reads=["uc2"], writes=["zx_d"])
                for t4 in range(4):
                    b = next_bank(P)
                    for i in range(4):
                        tc = t4 * 4 + i
                        P.op("pe", lambda e, b=b, i=i, tc=tc: e.transpose(P.banks[b][:, i * 128:(i + 1) * 128], uc[0][:, tc * 128:(tc + 1) * 128], P.ident_f[:]),
                             reads=["uc0", "ident_f"], writes=[f"pb{b}"], skip_self=True)
                    P.op("act", lambda e, b=b, t4=t4, j=j: e.copy(out=z_tm[:, t4 * 4:(t4 + 1) * 4, j * 128:(j + 1) * 128], in_=P.banks[b][:].rearrange("p (i t) -> p i t", i=4)),
                         reads=[f"pb{b}"], writes=["z_tm"])
        P.barrier()
        with ExitStack() as s4:
            Ct = [P.sbuf([128, 16, 256], BF16, stack=s4) for _ in range(2)]
            St = [P.sbuf([128, 16, 256], BF16, stack=s4) for _ in range(2)]
            zr = [P.sbuf([128, 512], F32, stack=s4) for _ in range(2)]
            zi = [P.sbuf([128, 512], F32, stack=s4) for _ in range(2)]
            t1 = [P.sbuf([128, 512], F32, stack=s4) for _ in range(2)]
            t2 = [P.sbuf([128, 512], F32, stack=s4) for _ in range(2)]
            t3 = [P.sbuf([128, 512], F32, stack=s4) for _ in range(2)]
            t4_ = [P.sbuf([128, 512], F32, stack=s4) for _ in range(2)]
            for f4 in range(8):
                u = f4 % 2
                P.dma("sp", Ct[u][:], c["dft_cos"][:, f4 * 256:(f4 + 1) * 256].rearrange("(kc p) f -> p kc f", p=128), writes=[f"Ct{u}"])
                P.dma("sp", St[u][:], c["dft_sn"][:, f4 * 256:(f4 + 1) * 256].rearrange("(kc p) f -> p kc f", p=128), writes=[f"St{u}"])
                for fi in range(2):
                    fc = f4 * 2 + fi
                    v = fc % 2
                    br = next_bank(P)
                    bi = next_bank(P)
                    for tc in range(16):
                        mm(P, P.banks[br][:], Ct[u][:, tc, fi * 128:(fi + 1) * 128], z_tm[:, tc, :], tc == 0, tc == 15, [f"Ct{u}", "z_tm"], [f"pb{br}"])
                    for tc in range(16):
                        mm(P, P.banks[bi][:], St[u][:, tc, fi * 128:(fi + 1) * 128], z_tm[:, tc, :], tc == 0, tc == 15, [f"St{u}", "z_tm"], [f"pb{bi}"])
                    P.op("act", lambda e, br=br, v=v: e.copy(out=zr[v][:], in_=P.banks[br][:]), reads=[f"pb{br}"], writes=[f"zr{v}"])
                    P.op("act", lambda e, bi=bi, v=v: e.copy(out=zi[v][:], in_=P.banks[bi][:]), reads=[f"pb{bi}"], writes=[f"zi{v}"])
                    P.op("dve", lambda e, v=v, fc=fc: e.tensor_tensor(out=t1[v][:], in0=zr[v][:], in1=Kre[:, fc, :], op=ALU.mult), reads=[f"zr{v}", "Kre"], writes=[f"t1{v}"])
                    P.op("pool", lambda e, v=v, fc=fc: e.tensor_tensor(out=t2[v][:], in0=zi[v][:], in1=Kim[:, fc, :], op=ALU.mult), reads=[f"zi{v}", "Kim"], writes=[f"t2{v}"])
                    P.op("pool", lambda e, v=v, fc=fc: e.tensor_tensor(out=t3[v][:], in0=zr[v][:], in1=Kim[:, fc, :], op=ALU.mult), reads=[f"zr{v}", "Kim"], writes=[f"t3{v}"])
                    P.op("dve", lambda e, v=v, fc=fc: e.tensor_tensor(out=t4_[v][:], in0=zi[v][:], in1=Kre[:, fc, :], op=ALU.mult), reads=[f"zi{v}", "Kre"], writes=[f"t4{v}"])
                    if fc == 0:
                        P.op("dve", lambda e, v=v: e.memset(t2[v][0:1, :], 0.0), reads=[f"t2{v}"], writes=[f"t2{v}"])
                        P.op("dve", lambda e, v=v: e.memset(t3[v][0:1, :], 0.0), reads=[f"t3{v}"], writes=[f"t3{v}"])
                        P.op("dve", lambda e, v=v: e.tensor_tensor(out=t4_[v][0:1, :], in0=zi[v][0:1, :], in1=Kim[0:1, 0, :], op=ALU.mult), reads=[f"zi{v}", "Kim", f"t4{v}"], writes=[f"t4{v}"])
                    P.op("dve", lambda e, v=v, fc=fc: e.tensor_tensor(out=Yre[:, fc, :], in0=t1[v][:], in1=t2[v][:], op=ALU.subtract), reads=[f"t1{v}", f"t2{v}"], writes=["Yre"])
                    P.op("pool", lambda e, v=v, fc=fc: e.tensor_tensor(out=Yim[:, fc, :], in0=t3[v][:], in1=t4_[v][:], op=ALU.add), reads=[f"t3{v}", f"t4{v}"], writes=["Yim"])
        P.barrier()
      if True:
        with ExitStack() as s5:
            Cr = [P.sbuf([128, 16, 512], BF16, stack=s5) for _ in range(2)]
            Sr = [P.sbuf([128, 16, 512], BF16, stack=s5) for _ in range(2)]
            zt = [P.sbuf([128, 512], F32, stack=s5) for _ in range(2)]
            xt = [P.sbuf([128, 512], F32, stack=s5) for _ in range(2)]
            tm = [P.sbuf([128, 512], F32, stack=s5) for _ in range(2)]
            ost = [P.sbuf([128, 512], BF16, stack=s5) for _ in range(2)]
            n = 0
            for tt in range(4):
                u = tt % 2
                ts_ = slice(tt * 512, (tt + 1) * 512)
                P.dma("sp", Cr[u][:], c["dft_cos"][:, ts_].rearrange("(kc p) t -> p kc t", p=128), writes=[f"Cr{u}"])
                P.dma("sp", Sr[u][:], c["dft_snT"][:, ts_].rearrange("(kc p) t -> p kc t", p=128), writes=[f"Sr{u}"])
                for cj in range(4):
                    v = n % 2
                    n += 1
                    cs = slice(cj * 128, (cj + 1) * 128)
                    P.dma("sp", zt[v][:], zx_d[0, cs, ts_], writes=[f"zt{v}"])
                    P.dma("sp", xt[v][:], zx_d[1, cs, ts_], writes=[f"xt{v}"])
                    b = next_bank(P)
                    for fc in range(16):
                        mm(P, P.banks[b][:], Yre[:, fc, cs], Cr[u][:, fc, :], fc == 0, False, ["Yre", f"Cr{u}"], [f"pb{b}"])
                    for fc in range(16):
                        mm(P, P.banks[b][:], Yim[:, fc, cs], Sr[u][:, fc, :], False, fc == 15, ["Yim", f"Sr{u}"], [f"pb{b}"])
                    P.op("dve", lambda e, v=v, b=b, cj=cj: e.scalar_tensor_tensor(out=tm[v][:], in0=zt[v][:], scalar=skc[:, cj:cj + 1], in1=P.banks[b][:], op0=ALU.mult, op1=ALU.add),
                         reads=[f"zt{v}", "skc", f"pb{b}"], writes=[f"tm{v}"])
                    P.op("pool", lambda e, v=v: e.tensor_tensor(out=ost[v][:], in0=tm[v][:], in1=xt[v][:], op=ALU.mult), reads=[f"tm{v}", f"xt{v}"], writes=[f"ost{v}"])
                    P.dma("act", obrT_d[1024 + cj * 128:1024 + (cj + 1) * 128, ts_], ost[v][:], reads=[f"ost{v}"], writes=["obrT_d"])
    P.barrier()


def hyena_consts():
    import ml_dtypes
    f32 = np.float32
    L = T
    t = np.linspace(0.0, 1.0, L, dtype=f32)[:, None]
    n_bands = 16
    bands = np.linspace(1e-4, n_bands - 1, n_bands, dtype=f32)[None, :]
    ang = (f32(2.0 * math.pi / L) * np.arange(L, dtype=f32)[:, None] * bands).astype(f32)
    z = np.concatenate([t, np.cos(ang), -np.sin(ang)], axis=-1).astype(f32)
    max_decay = math.log(1e-2) / 0.3
    min_decay = math.log(1e-2) / 1.5
    deltas = np.abs(np.linspace(min_decay, max_decay, 512, dtype=f32))
    window = np.exp(-t * deltas[None, :]).astype(f32)
    idx = np.arange(L, dtype=np.int64)
    prod = (idx[:, None] * idx[None, :]) % 4096
    th = prod.astype(np.float64) * (2.0 * math.pi / 4096.0)
    cosm = np.cos(th)
    snm = -np.sin(th)
    snm[:, 0] = np.where(idx % 2 == 0, 1.0, -1.0)
    bf = ml_dtypes.bfloat16
    sc = np.full((128, 16), 2.0 / 4096.0, f32)
    sc[0, 0] = 1.0 / 4096.0
    alt = np.where(np.arange(128) % 2 == 0, 1.0, -1.0).astype(f32)[:, None]
    return {"hy_z": np.ascontiguousarray(z.T), "hy_win": window, "dft_cos": cosm.astype(bf), "dft_sn": snm.astype(bf),
            "dft_snT": np.ascontiguousarray(snm.T).astype(bf), "dft_sc": sc, "altsign": alt}

import math
import numpy as np

ROFF, KOFF, VOFF, WOFF, AOFF, GOFF = 3072, 3584, 4096, 4608, 4704, 4800
NCH = 32
CH = 64
GN_EPS = 64e-5


def rwkv_consts():
    f32 = np.float32
    s = np.arange(64)[:, None]
    t = np.arange(64)[None, :]
    out = {}
    for d in range(2):
        strict = (s < t) if d == 0 else (s > t)
        incl = (s <= t) if d == 0 else (s >= t)
        m = np.zeros((128, 256), f32)
        m[0:64, 0:64] = strict
        m[0:64, 64:128] = incl
        m[64:128, 64:128] = incl
        m[0:64, 128:192] = strict.T
        m[0:64, 192:256] = strict.T
        out[f"rw_mask{d}"] = m
    bo = np.zeros((128, 128), f32)
    bo[0:64, 0:64] = 1.0
    bo[64:128, 64:128] = 1.0
    out["rw_bones"] = bo
    return out


def phase_rwkv(P, projT_d, prm, obrT_d):
    STOP = 9
    c = P.consts
    colv = lambda a, o, n: a[o:o + n].rearrange("(p o) -> p o", o=1)
    with ExitStack() as st:
        BO = P.sbuf([128, 128], F32, stack=st)
        P.dma("sp", BO[:], c["rw_bones"], writes=["BO"])
        masks = [P.sbuf([128, 256], F32, stack=st) for _ in range(2)]
        for d in range(2):
            P.dma("sp", masks[d][:], c[f"rw_mask{d}"], writes=[f"mask{d}"])
        sg = P.sbuf([128, 2, T], F32, stack=st)
        P.dma("sp", sg[:], projT_d[GOFF:GOFF + 256, :].rearrange("(kc p) t -> p kc t", p=128), writes=["sg"])
        P.op("act", lambda e: e.activation(out=sg[:], in_=sg[:], func=AF.Sigmoid), reads=["sg"], writes=["sg"])
        gup = P.sbuf([128, 2, 512], F32, stack=st)
        P.dma("sp", gup[:], prm["c_g_up"].rearrange("(kc p) c -> p kc c", p=128), writes=["gup"])
        YC = P.sbuf([128, T], F32, stack=st)
        for hp in range(4):
            ch0 = hp * 128
            for d in range(2):
                mk = masks[d]
                mkey = f"mask{d}"
                with ExitStack() as sj:
                    BK = P.sbuf([128, NCH, 5, CH], F32, stack=sj)
                    CORR = P.sbuf([128, T], F32, stack=sj)
                    gC = P.sbuf([128, NCH], F32, stack=sj)
                    cols = P.sbuf([128, 16], F32, stack=sj)
                    if STOP == 73 and (hp, d) == (0, 1):
                        P.barrier(); return
                    cdefs = [(prm["c_mu"][d], ch0), (prm["c_mu"][d], 512 + ch0), (prm["c_mu"][d], 1024 + ch0),
                             (prm["c_w0"][d], ch0), (prm["c_a0"][d], ch0), (prm["c_k_k"], ch0), (prm["c_k_a"], ch0),
                             (prm["c_r_k"].rearrange("h n -> (h n)"), ch0), (prm["c_ln_w"], ch0), (prm["c_ln_b"], ch0)]
                    for i, (a, o) in enumerate(cdefs):
                        P.dma("sp", cols[:, i:i + 1], colv(a, o, 128), writes=["cols"])
                    cw = P.sbuf([96, 4], F32, stack=sj)
                    P.dma("sp", cw[:, 0:1], colv(prm["c_mu"][d], 1536, 96), writes=["cw"])
                    P.dma("sp", cw[:, 1:2], colv(prm["c_mu"][d], 1632, 96), writes=["cw"])
                    P.op("dve", lambda e: e.tensor_scalar(out=cols[:, 10:13], in0=cols[:, 0:3], scalar1=-1.0, scalar2=1.0, op0=ALU.mult, op1=ALU.add), reads=["cols"], writes=["cols"])
                    P.op("dve", lambda e: e.tensor_scalar(out=cols[:, 13:14], in0=cols[:, 3:4], scalar1=-1.0, scalar2=None, op0=ALU.mult), reads=["cols"], writes=["cols"])
                    P.op("dve", lambda e: e.tensor_scalar(out=cols[:, 14:15], in0=cols[:, 6:7], scalar1=-1.0, scalar2=1.0, op0=ALU.mult, op1=ALU.add), reads=["cols"], writes=["cols"])
                    P.op("dve", lambda e: e.tensor_scalar(out=cw[:, 2:4], in0=cw[:, 0:2], scalar1=-1.0, scalar2=1.0, op0=ALU.mult, op1=ALU.add), reads=["cw"], writes=["cw"])
                    if STOP == 71 and (hp, d) == (0, 1):
                        P.barrier(); return
                    wup = P.sbuf([96, 128], F32, stack=sj)
                    aup = P.sbuf([96, 128], F32, stack=sj)
                    P.dma("sp", wup[:], prm["c_w_up"][d][:, ch0:ch0 + 128], writes=["wup"])
                    P.dma("sp", aup[:], prm["c_a_up"][d][:, ch0:ch0 + 128], writes=["aup"])
                    with ExitStack() as s1:
                        xt = P.sbuf([128, T], F32, stack=s1)
                        rf = P.sbuf([128, T], F32, stack=s1)
                        kf = P.sbuf([128, T], F32, stack=s1)
                        wl = P.sbuf([96, T], F32, stack=s1)
                        al = P.sbuf([96, T], F32, stack=s1)
                        EW = P.sbuf([128, T], F32, stack=s1)
                        AG = P.sbuf([128, T], F32, stack=s1)
                        KK = P.sbuf([128, T], F32, stack=s1)
                        KP = P.sbuf([128, T], F32, stack=s1)
                        LA = P.sbuf([128, T], F32, stack=s1)
                        LB = P.sbuf([128, T], F32, stack=s1)
                        tmp = P.sbuf([128, T], F32, stack=s1)
                        v3 = lambda a: a[:].rearrange("p (c j) -> p c j", j=CH)

                        def shift_mix(dst, dkey, row0, npart, mu, omu, mkeys):
                            P.dma("sp", xt[0:npart, :], projT_d[row0:row0 + npart, :], writes=["xt"])
                            P.op("dve", lambda e: e.tensor_scalar(out=dst[0:npart, :], in0=xt[0:npart, :], scalar1=omu, scalar2=None, op0=ALU.mult), reads=["xt"] + mkeys, writes=[dkey])
                            if d == 0:
                                P.op("dve", lambda e: e.scalar_tensor_tensor(out=dst[0:npart, 1:T], in0=xt[0:npart, 0:T - 1], scalar=mu, in1=dst[0:npart, 1:T], op0=ALU.mult, op1=ALU.add), reads=["xt", dkey] + mkeys, writes=[dkey])
                            else:
                                P.op("dve", lambda e: e.scalar_tensor_tensor(out=dst[0:npart, 0:T - 1], in0=xt[0:npart, 1:T], scalar=mu, in1=dst[0:npart, 0:T - 1], op0=ALU.mult, op1=ALU.add), reads=["xt", dkey] + mkeys, writes=[dkey])

                        shift_mix(rf, "rf", ROFF + ch0, 128, cols[:, 0:1], cols[:, 10:11], ["cols"])
                        shift_mix(kf, "kf", KOFF + ch0, 128, cols[:, 1:2], cols[:, 11:12], ["cols"])
                        shift_mix(tmp, "tmp", VOFF + ch0, 128, cols[:, 2:3], cols[:, 12:13], ["cols"])
                        P.op("pool", lambda e: e.tensor_copy(out=BK[:, :, 4, :], in_=v3(tmp)), reads=["tmp"], writes=["BK4"])
                        shift_mix(wl, "wl", WOFF, 96, cw[:, 0:1], cw[:, 2:3], ["cw"])
                        shift_mix(al, "al", AOFF, 96, cw[:, 1:2], cw[:, 3:4], ["cw"])
                        P.op("act", lambda e: e.activation(out=wl[:], in_=wl[:], func=AF.Tanh), reads=["wl"], writes=["wl"])
                        for tt in range(4):
                            sl = slice(tt * 512, (tt + 1) * 512)
                            b = next_bank(P)
                            mm(P, P.banks[b][:], wup[:], wl[0:96, sl], True, True, ["wup", "wl"], [f"pb{b}"])
                            P.op("act", lambda e, b=b, sl=sl: e.activation(out=EW[:, sl], in_=P.banks[b][:], func=AF.Exp, bias=cols[:, 13:14], scale=-1.0), reads=[f"pb{b}", "cols"], writes=["EW"])
                            b2 = next_bank(P)
                            mm(P, P.banks[b2][:], aup[:], al[0:96, sl], True, True, ["aup", "al"], [f"pb{b2}"])
                            P.op("act", lambda e, b2=b2, sl=sl: e.activation(out=AG[:, sl], in_=P.banks[b2][:], func=AF.Sigmoid, bias=cols[:, 4:5], scale=1.0), reads=[f"pb{b2}", "cols"], writes=["AG"])
                        P.op("act", lambda e: e.activation(out=EW[:], in_=EW[:], func=AF.Ln, bias=1.0, scale=1.0), reads=["EW"], writes=["EW"])
                        P.op("act", lambda e: e.activation(out=EW[:], in_=EW[:], func=AF.Exp, bias=-0.5, scale=-1.0), reads=["EW"], writes=["EW"])
                        P.op("dve", lambda e: e.tensor_scalar(out=KK[:], in0=kf[:], scalar1=cols[:, 5:6], scalar2=None, op0=ALU.mult), reads=["kf", "cols"], writes=["KK"])
                        P.op("act", lambda e: e.activation(out=tmp[:], in_=KK[:], func=AF.Square), reads=["KK", "BK4"], writes=["tmp"])
                        for tt in range(4):
                            sl = slice(tt * 512, (tt + 1) * 512)
                            b = next_bank(P)
                            mm(P, P.banks[b][:], BO[:], tmp[:, sl], True, True, ["BO", "tmp"], [f"pb{b}"])
                            P.op("act", lambda e, b=b, sl=sl: e.activation(out=LA[:, sl], in_=P.banks[b][:], func=AF.Sqrt, bias=1e-24, scale=1.0), reads=[f"pb{b}"], writes=["LA"])
                        P.op("dve", lambda e: e.reciprocal(out=LA[:], in_=LA[:]), reads=["LA"], writes=["LA"])
                        P.op("dve", lambda e: e.tensor_tensor(out=KK[:], in0=KK[:], in1=LA[:], op=ALU.mult), reads=["KK", "LA"], writes=["KK"])
                        P.op("dve", lambda e: e.tensor_scalar(out=KP[:], in0=AG[:], scalar1=cols[:, 6:7], scalar2=cols[:, 14:15], op0=ALU.mult, op1=ALU.add), reads=["AG", "cols"], writes=["KP"])
                        P.op("dve", lambda e: e.tensor_tensor(out=KP[:], in0=KP[:], in1=kf[:], op=ALU.mult), reads=["KP", "kf"], writes=["KP"])
                        P.op("dve", lambda e: e.scalar_tensor_tensor(out=tmp[:], in0=rf[:], scalar=cols[:, 7:8], in1=KP[:], op0=ALU.mult, op1=ALU.mult), reads=["rf", "KP", "cols", "tmp"], writes=["tmp"])
                        for tt in range(4):
                            sl = slice(tt * 512, (tt + 1) * 512)
                            b = next_bank(P)
                            mm(P, P.banks[b][:], BO[:], tmp[:, sl], True, True, ["BO", "tmp"], [f"pb{b}"])
                            P.op("dve", lambda e, b=b, tt=tt: e.tensor_tensor(out=CORR[:, tt * 512:(tt + 1) * 512].rearrange("p (c j) -> p c j", j=CH), in0=P.banks[b][:].rearrange("p (c j) -> p c j", j=CH), in1=BK[:, tt * 8:(tt + 1) * 8, 4, :], op=ALU.mult),
                                 reads=[f"pb{b}", "BK4"], writes=["CORR"])
                        src, skey = EW, "EW"
                        pp = [(LA, "LA"), (LB, "LB")]
                        for i, s_ in enumerate((1, 2, 4, 8, 16, 32)):
                            dst, dkey = pp[i % 2]
                            sv, dv = v3(src), v3(dst)
                            if d == 0:
                                P.op("dve", lambda e, sv=sv, dv=dv, s_=s_: e.tensor_tensor(out=dv[:, :, s_:CH], in0=sv[:, :, s_:CH], in1=sv[:, :, 0:CH - s_], op=ALU.add), reads=[skey], writes=[dkey])
                                P.op("pool", lambda e, sv=sv, dv=dv, s_=s_: e.tensor_copy(out=dv[:, :, 0:s_], in_=sv[:, :, 0:s_]), reads=[skey], writes=[dkey])
                            else:
                                P.op("dve", lambda e, sv=sv, dv=dv, s_=s_: e.tensor_tensor(out=dv[:, :, 0:CH - s_], in0=sv[:, :, 0:CH - s_], in1=sv[:, :, s_:CH], op=ALU.add), reads=[skey], writes=[dkey])
                                P.op("pool", lambda e, sv=sv, dv=dv, s_=s_: e.tensor_copy(out=dv[:, :, CH - s_:CH], in_=sv[:, :, CH - s_:CH]), reads=[skey], writes=[dkey])
                            src, skey = dst, dkey
                        LP = src
                        lend = CH - 1 if d == 0 else 0
                        P.op("act", lambda e: e.activation(out=gC[:], in_=v3(LP)[:, :, lend], func=AF.Exp, scale=-1.0), reads=[skey], writes=["gC"])
                        P.op("dve", lambda e: e.tensor_tensor(out=EW[:], in0=EW[:], in1=LP[:], op=ALU.subtract), reads=["EW", skey], writes=["EW"])
                        P.op("act", lambda e: e.activation(out=EW[:], in_=EW[:], func=AF.Exp), reads=["EW"], writes=["EW"])
                        P.op("act", lambda e: e.activation(out=LA[:], in_=LP[:], func=AF.Exp), reads=[skey, "LA"], writes=["LA"])
                        P.op("act", lambda e: e.activation(out=tmp[:], in_=LP[:], func=AF.Exp, scale=-1.0), reads=[skey, "tmp"], writes=["tmp"])
                        P.op("dve", lambda e: e.tensor_tensor(out=AG[:], in0=AG[:], in1=KK[:], op=ALU.mult), reads=["AG", "KK"], writes=["AG"])
                        P.op("dve", lambda e: e.tensor_tensor(out=BK[:, :, 0, :], in0=v3(AG), in1=v3(LA), op=ALU.mult), reads=["AG", "LA"], writes=["BK0"])
                        P.op("pool", lambda e: e.tensor_tensor(out=BK[:, :, 1, :], in0=v3(KP), in1=v3(LA), op=ALU.mult), reads=["KP", "LA"], writes=["BK1"])
                        P.op("pool", lambda e: e.tensor_tensor(out=BK[:, :, 2, :], in0=v3(rf), in1=v3(tmp), op=ALU.mult), reads=["rf", "tmp"], writes=["BK2"])
                        P.op("dve", lambda e: e.scalar_tensor_tensor(out=BK[:, :, 3, :], in0=v3(KK), scalar=-1.0, in1=v3(EW), op0=ALU.mult, op1=ALU.mult), reads=["KK", "EW"], writes=["BK3"])
                    P.barrier()
                    if getattr(P, "dbg", None) is not None and hp == P.dbg["hp"] and d == P.dbg["d"]:
                        P.dma("sp", P.dbg["BK"], BK[:].rearrange("p c s j -> p (c s j)"), reads=["BK0", "BK1", "BK2", "BK3", "BK4"], writes=["dbgBK"])
                        P.dma("sp", P.dbg["gC"], gC[:], reads=["gC"], writes=["dbggC"])
                        P.dma("sp", P.dbg["CORR"], CORR[:], reads=["CORR"], writes=["dbgCORR"])
                    if STOP <= 1 or (STOP == 72 and (hp, d) == (0, 1)):
                        return
                    BKK = ["BK0", "BK1", "BK2", "BK3", "BK4"]
                    RT1 = P.sbuf([64, NCH, CH], F32, stack=sj)
                    GC = P.sbuf([64, 2, NCH], F32, stack=sj)
                    P.dma("sp", RT1[:], BK[64:128, :, 2, :], reads=["BK2"], writes=["RT1"])
                    P.dma("sp", GC[:, 0, :], gC[0:64, :], reads=["gC"], writes=["GC"])
                    P.dma("sp", GC[:, 1, :], gC[64:128, :], reads=["gC"], writes=["GC"])
                    OFM = P.sbuf([128, T], F32, stack=sj)
                    ST = [P.sbuf([64, 2, CH], F32, stack=sj) for _ in range(2)]
                    P.op("pool", lambda e: e.memset(ST[0][:], 0.0), writes=["ST0"])
                    stn = 0
                    G = 8
                    BKb = P.sbuf([128, NCH, 4, CH], BF16, stack=sj)
                    P.op("pool", lambda e: e.tensor_copy(out=BKb[:], in_=BK[:, :, 0:4, :]), reads=BKK, writes=["BKb"])
                    M1t = P.sbuf([64, 2 * G, CH], BF16, stack=sj)
                    N1b = P.sbuf([64, 2 * G, CH], BF16, stack=sj)
                    Yf = P.sbuf([64, 2 * G, CH], F32, stack=sj)
                    Nt = P.sbuf([64, 2 * G, 128], F32, stack=sj)
                    Mbr = P.sbuf([64, 2 * G, CH], F32, stack=sj)
                    Mkr = P.sbuf([64, 2 * G, CH], F32, stack=sj)
                    Mp = [P.sbuf([64, 2 * G, CH], BF16, stack=sj) for _ in range(2)]
                    Np = [P.sbuf([64, 2 * G, CH], BF16, stack=sj) for _ in range(2)]
                    Yt = [P.sbuf([64, 2 * G, CH], BF16, stack=sj) for _ in range(2)]
                    TR = P.sbuf([64, G, 4, 128], F32, stack=sj)
                    UT = P.sbuf([64, G, 128], F32, stack=sj)
                    W1T = P.sbuf([64, 2 * G, CH], F32, stack=sj)
                    W2T = P.sbuf([64, 2 * G, CH], F32, stack=sj)
                    OTs = [P.sbuf([64, 128], F32, stack=sj) for _ in range(2)]
                    chunk_order = list(range(NCH)) if d == 0 else list(range(NCH - 1, -1, -1))
                    fl = lambda ap: ap.rearrange("p a b -> p (a b)")
                    for gi in range(NCH // G):
                        chunks = chunk_order[gi * G:(gi + 1) * G]
                        for h2 in range(2):
                            hb = slice(h2 * 64, (h2 + 1) * 64)
                            for q in range(2):
                                b1 = next_bank(P)
                                b2 = next_bank(P)
                                b3 = next_bank(P)
                                for i in range(4):
                                    cc = chunks[q * 4 + i]
                                    mm(P, P.banks[b1][0:64, i * 128:(i + 1) * 128], BKb[hb, cc, 0, :], fl(BKb[hb, cc, 2:4, :]), True, True, ["BKb"], [f"pb{b1}"])
                                    mm(P, P.banks[b2][0:64, i * 128:(i + 1) * 128], BKb[hb, cc, 3, :], fl(BKb[hb, cc, 0:2, :]), True, True, ["BKb"], [f"pb{b2}"])
                                    mm(P, P.banks[b3][0:64, i * 64:(i + 1) * 64], BKb[hb, cc, 1, :], BKb[hb, cc, 2, :], True, True, ["BKb"], [f"pb{b3}"])
                                u0 = h2 * G + q * 4
                                pv1 = P.banks[b1][0:64, :].rearrange("p (i x) -> p i x", i=4)
                                pv2 = P.banks[b2][0:64, :].rearrange("p (i x) -> p i x", i=4)
                                pv3 = P.banks[b3][0:64, 0:256].rearrange("p (i x) -> p i x", i=4)
                                P.op("dve", lambda e, pv1=pv1, u0=u0: e.tensor_tensor(out=M1t[:, u0:u0 + 4, :], in0=pv1[:, :, 64:128], in1=mk[0:64, 0:64].unsqueeze(1).to_broadcast([64, 4, 64]), op=ALU.mult),
                                     reads=[f"pb{b1}", mkey], writes=["M1t"])
                                P.op("dve", lambda e, pv1=pv1, u0=u0: e.tensor_tensor(out=Mbr[:, u0:u0 + 4, :], in0=pv1[:, :, 0:64], in1=mk[0:64, 64:128].unsqueeze(1).to_broadcast([64, 4, 64]), op=ALU.mult),
                                     reads=[f"pb{b1}", mkey], writes=["Mbr"])
                                P.op("dve", lambda e, pv2=pv2, u0=u0: e.tensor_tensor(out=Nt[:, u0:u0 + 4, :], in0=pv2, in1=mk[0:64, 128:256].unsqueeze(1).to_broadcast([64, 4, 128]), op=ALU.mult),
                                     reads=[f"pb{b2}", mkey], writes=["Nt"])
                                P.op("dve", lambda e, pv2=pv2, u0=u0: e.tensor_tensor(out=N1b[:, u0:u0 + 4, :], in0=pv2[:, :, 0:64], in1=mk[0:64, 128:192].unsqueeze(1).to_broadcast([64, 4, 64]), op=ALU.mult),
                                     reads=[f"pb{b2}", mkey], writes=["N1b"])
                                P.op("dve", lambda e, pv3=pv3, u0=u0: e.tensor_tensor(out=Mkr[:, u0:u0 + 4, :], in0=pv3, in1=mk[0:64, 64:128].unsqueeze(1).to_broadcast([64, 4, 64]), op=ALU.mult),
                                     reads=[f"pb{b3}", mkey], writes=["Mkr"])
                        if STOP <= 2:
                            P.barrier(); return
                        for ci, cc in enumerate(chunks):
                            b = next_bank(P)
                            for j, slot in enumerate((0, 1, 3, 4)):
                                P.op("pe", lambda e, b=b, j=j, cc=cc, slot=slot: e.transpose(P.banks[b][0:64, j * 128:(j + 1) * 128], BK[:, cc, slot, :], P.ident_f[:]), reads=BKK + ["ident_f"], writes=[f"pb{b}"], skip_self=True)
                            if ci % 2 == 0:
                                P.op("act", lambda e, b=b, ci=ci: e.copy(out=TR[:, ci, :, :], in_=P.banks[b][0:64, :].rearrange("p (j x) -> p j x", j=4)), reads=[f"pb{b}"], writes=["TR"])
                            else:
                                P.op("dve", lambda e, b=b, ci=ci: e.tensor_copy(out=TR[:, ci, :, :], in_=P.banks[b][0:64, :].rearrange("p (j x) -> p j x", j=4)), reads=[f"pb{b}"], writes=["TR"])
                        P.op("pool", lambda e: e.tensor_tensor(out=Yt[0][:], in0=M1t[:], in1=P.ident_b[0:64, 0:64].unsqueeze(1).to_broadcast([64, 2 * G, 64]), op=ALU.add), reads=["M1t", "ident_b"], writes=["Yt0"])
                        Mc, Mk_, Nc, Nk_ = M1t, "M1t", None, "N1b"
                        yi = 0
                        for lev in range(5):
                            mo, no = Mp[lev % 2], Np[lev % 2]
                            mok, nok = f"Mp{lev % 2}", f"Np{lev % 2}"
                            for q in range(2):
                                bm = next_bank(P)
                                bn = next_bank(P)
                                for i in range(8):
                                    u = q * 8 + i
                                    nsrc = N1b[:, u, :] if lev == 0 else Nc[:, u, :]
                                    msrc = Mc[:, u, :]
                                    if lev < 4:
                                        mm(P, P.banks[bm][0:64, i * 64:(i + 1) * 64], nsrc, msrc, True, True, [Mk_, Nk_], [f"pb{bm}"])
                                    mm(P, P.banks[bn][0:64, i * 64:(i + 1) * 64], msrc, nsrc, True, True, [Mk_, Nk_], [f"pb{bn}"])
                                if lev < 4:
                                    P.op("act", lambda e, bm=bm, q=q, mo=mo: e.copy(out=mo[:, q * 8:(q + 1) * 8, :], in_=P.banks[bm][0:64, :].rearrange("p (i x) -> p i x", i=8)), reads=[f"pb{bm}"], writes=[mok])
                                P.op("dve", lambda e, bn=bn, q=q, no=no: e.tensor_copy(out=no[:, q * 8:(q + 1) * 8, :], in_=P.banks[bn][0:64, :].rearrange("p (i x) -> p i x", i=8)), reads=[f"pb{bn}"], writes=[nok])
                            ys, yd = Yt[yi % 2], Yt[(yi + 1) % 2]
                            ysk, ydk = f"Yt{yi % 2}", f"Yt{(yi + 1) % 2}"
                            for q in range(2):
                                by = next_bank(P)
                                for i in range(8):
                                    u = q * 8 + i
                                    mm(P, P.banks[by][0:64, i * 64:(i + 1) * 64], no[:, u, :], ys[:, u, :], True, True, [nok, ysk], [f"pb{by}"])
                                P.op("dve", lambda e, by=by, q=q, ys=ys, yd=yd: e.tensor_tensor(out=yd[:, q * 8:(q + 1) * 8, :], in0=ys[:, q * 8:(q + 1) * 8, :], in1=P.banks[by][0:64, :].rearrange("p (i x) -> p i x", i=8), op=ALU.add),
                                     reads=[f"pb{by}", ysk], writes=[ydk])
                            yi += 1
                            Mc, Mk_, Nc, Nk_ = mo, mok, no, nok
                        Yb_, Ybk = Yt[yi % 2], f"Yt{yi % 2}"
                        P.op("act", lambda e, Yb_=Yb_: e.copy(out=Yf[:], in_=Yb_[:]), reads=[Ybk], writes=["Yf"])
                        Y, Yk = Yf, "Yf"
                        if STOP <= 3:
                            P.barrier(); return
                        for q in range(2):
                            bw1 = next_bank(P)
                            bw2 = next_bank(P)
                            for i in range(8):
                                u = q * 8 + i
                                h2, ci = divmod(u, G)
                                mm(P, P.banks[bw1][0:64, i * 64:(i + 1) * 64], TR[:, ci, 2, h2 * 64:(h2 + 1) * 64], Y[:, u, :], True, True, ["TR", Yk], [f"pb{bw1}"])
                                mm(P, P.banks[bw2][0:64, i * 64:(i + 1) * 64], Nt[:, u, 64:128], Y[:, u, :], True, True, ["Nt", Yk], [f"pb{bw2}"])
                            P.op("act", lambda e, bw1=bw1, q=q: e.copy(out=W1T[:, q * 8:(q + 1) * 8, :], in_=P.banks[bw1][0:64, :].rearrange("p (i x) -> p i x", i=8)), reads=[f"pb{bw1}"], writes=["W1T"])
                            P.op("dve", lambda e, bw2=bw2, q=q: e.tensor_copy(out=W2T[:, q * 8:(q + 1) * 8, :], in_=P.banks[bw2][0:64, :].rearrange("p (i x) -> p i x", i=8)), reads=[f"pb{bw2}"], writes=["W2T"])
                        if STOP <= 4:
                            P.barrier(); return
                        for ci, cc in enumerate(chunks):
                            Sc, Sk = ST[stn % 2], f"ST{stn % 2}"
                            Sn, Snk = ST[(stn + 1) % 2], f"ST{(stn + 1) % 2}"
                            stn += 1
                            bu = next_bank(P)
                            for h2 in range(2):
                                u = h2 * G + ci
                                hs = slice(h2 * 64, (h2 + 1) * 64)
                                mm(P, P.banks[bu][0:64, hs], W1T[:, u, :], Sc[:, h2, :], True, False, ["W1T", Sk], [f"pb{bu}"])
                                mm(P, P.banks[bu][0:64, hs], W2T[:, u, :], TR[:, ci, 3, hs], False, True, ["W2T", "TR"], [f"pb{bu}"])
                            P.op("dve", lambda e, bu=bu, ci=ci: e.tensor_copy(out=UT[:, ci, :], in_=P.banks[bu][0:64, 0:128]), reads=[f"pb{bu}"], writes=["UT"])
                            bo_ = next_bank(P)
                            bs_ = next_bank(P)
                            for h2 in range(2):
                                u = h2 * G + ci
                                hs = slice(h2 * 64, (h2 + 1) * 64)
                                rt = BK[0:64, cc, 2, :] if h2 == 0 else RT1[:, cc, :]
                                mm(P, P.banks[bo_][0:64, hs], rt, Sc[:, h2, :], True, False, ["BK2", "RT1", Sk], [f"pb{bo_}"])
                                mm(P, P.banks[bo_][0:64, hs], Mbr[:, u, :], UT[:, ci, hs], False, False, ["Mbr", "UT"], [f"pb{bo_}"])
                                mm(P, P.banks[bo_][0:64, hs], Mkr[:, u, :], TR[:, ci, 3, hs], False, True, ["Mkr", "TR"], [f"pb{bo_}"])
                                mm(P, P.banks[bs_][0:64, hs], P.ident_f[0:64, 0:64], Sc[:, h2, :], True, False, ["ident_f", Sk], [f"pb{bs_}"])
                                mm(P, P.banks[bs_][0:64, hs], TR[:, ci, 0, hs], UT[:, ci, hs], False, False, ["TR", "UT"], [f"pb{bs_}"])
                                mm(P, P.banks[bs_][0:64, hs], TR[:, ci, 1, hs], TR[:, ci, 3, hs], False, True, ["TR"], [f"pb{bs_}"])
                            P.op("dve", lambda e, bs_=bs_, cc=cc, Sn=Sn: e.tensor_tensor(out=Sn[:], in0=P.banks[bs_][0:64, 0:128].rearrange("p (h v) -> p h v", h=2), in1=GC[:, :, cc].unsqueeze(2).to_broadcast([64, 2, 64]), op=ALU.mult),
                                 reads=[f"pb{bs_}", "GC"], writes=[Snk])
                            ot = OTs[ci % 2]
                            otk = f"OTs{ci % 2}"
                            P.op("act", lambda e, bo_=bo_, ot=ot: e.copy(out=ot[:], in_=P.banks[bo_][0:64, 0:128]), reads=[f"pb{bo_}"], writes=[otk])
                            bt = next_bank(P)
                            mm(P, P.banks[bt][:, 0:64], ot[:], P.ident_f[0:64, 0:64], True, True, [otk, "ident_f"], [f"pb{bt}"])
                            P.op("act", lambda e, bt=bt, cc=cc: e.copy(out=OFM[:, cc * 64:(cc + 1) * 64], in_=P.banks[bt][:, 0:64]), reads=[f"pb{bt}"], writes=["OFM"])
                            if STOP == 45:
                                P.barrier(); return
                    if getattr(P, "dbg", None) is not None and hp == P.dbg["hp"] and d == P.dbg["d"]:
                        P.dma("sp", P.dbg["OFM"], OFM[:], reads=["OFM"], writes=["dbgOFM"])
                    if STOP <= 5:
                        P.barrier(); return
                    cen = P.sbuf([128, 512], F32, stack=sj)
                    sq = P.sbuf([128, 512], F32, stack=sj)
                    rs = P.sbuf([128, 512], F32, stack=sj)
                    for tt in range(4):
                        sl = slice(tt * 512, (tt + 1) * 512)
                        b = next_bank(P)
                        mm(P, P.banks[b][:], BO[:], OFM[:, sl], True, True, ["BO", "OFM"], [f"pb{b}"])
                        P.op("dve", lambda e, b=b, sl=sl: e.scalar_tensor_tensor(out=cen[:], in0=P.banks[b][:], scalar=-1.0 / 64, in1=OFM[:, sl], op0=ALU.mult, op1=ALU.add), reads=[f"pb{b}", "OFM"], writes=["cen"])
                        if getattr(P, "dbg", None) is not None and hp == P.dbg["hp"] and d == P.dbg["d"] and tt == 0:
                            P.dma("sp", P.dbg["C0"], cen[:], reads=["cen"], writes=["dbgC0"])
                            P.op("dve", lambda e, b=b: e.tensor_copy(out=rs[:], in_=P.banks[b][:]), reads=[f"pb{b}"], writes=["rs"])
                            P.dma("sp", P.dbg["PS"], rs[:], reads=["rs"], writes=["dbgPS"])
                        P.op("act", lambda e: e.activation(out=sq[:], in_=cen[:], func=AF.Square), reads=["cen"], writes=["sq"])
                        b2 = next_bank(P)
                        mm(P, P.banks[b2][:], BO[:], sq[:], True, True, ["BO", "sq"], [f"pb{b2}"])
                        P.op("act", lambda e, b2=b2: e.activation(out=rs[:], in_=P.banks[b2][:], func=AF.Sqrt, bias=GN_EPS, scale=1.0 / 64), reads=[f"pb{b2}"], writes=["rs"])
                        P.op("dve", lambda e: e.reciprocal(out=rs[:], in_=rs[:]), reads=["rs"], writes=["rs"])
                        P.op("dve", lambda e: e.tensor_tensor(out=cen[:], in0=cen[:], in1=rs[:], op=ALU.mult), reads=["cen", "rs"], writes=["cen"])
                        P.op("dve", lambda e: e.tensor_scalar(out=cen[:], in0=cen[:], scalar1=cols[:, 8:9], scalar2=cols[:, 9:10], op0=ALU.mult, op1=ALU.add), reads=["cen", "cols"], writes=["cen"])
                        if getattr(P, "dbg", None) is not None and hp == P.dbg["hp"] and d == P.dbg["d"] and tt == 0:
                            P.dma("sp", P.dbg["CEN"], cen[:], reads=["cen"], writes=["dbgCEN"])
                            P.dma("sp", P.dbg["RS"], rs[:], reads=["rs"], writes=["dbgRS"])
                        if d == 0:
                            P.op("pool", lambda e, sl=sl: e.tensor_tensor(out=YC[:, sl], in0=cen[:], in1=CORR[:, sl], op=ALU.add), reads=["cen", "CORR"], writes=["YC"])
                        else:
                            P.op("pool", lambda e, sl=sl: e.tensor_tensor(out=cen[:], in0=cen[:], in1=CORR[:, sl], op=ALU.add), reads=["cen", "CORR"], writes=["cen"])
                            P.op("pool", lambda e, sl=sl: e.tensor_tensor(out=YC[:, sl], in0=YC[:, sl], in1=cen[:], op=ALU.add), reads=["cen", "YC"], writes=["YC"])
                if STOP == 6:
                    P.barrier(); return
                P.barrier()
            if getattr(P, "dbg", None) is not None and hp == P.dbg["hp"]:
                P.dma("sp", P.dbg["YC"], YC[:], reads=["YC"], writes=["dbgYC"])
                P.dma("sp", P.dbg["SG"], sg[:].rearrange("p a t -> p (a t)"), reads=["sg"], writes=["dbgSG"])
            with ExitStack() as sg_:
                ost = [P.sbuf([128, 512], BF16, stack=sg_) for _ in range(2)]
                for tt in range(4):
                    sl = slice(tt * 512, (tt + 1) * 512)
                    b = next_bank(P)
                    for kc in range(2):
                        mm(P, P.banks[b][:], gup[:, kc, ch0:ch0 + 128], sg[:, kc, sl], kc == 0, kc == 1, ["gup", "sg"], [f"pb{b}"])
                    u = tt % 2
                    P.op("dve", lambda e, b=b, u=u, sl=sl: e.tensor_tensor(out=ost[u][:], in0=YC[:, sl], in1=P.banks[b][:], op=ALU.mult), reads=[f"pb{b}", "YC"], writes=[f"ost{u}"])
                    P.dma("sp", obrT_d[1536 + ch0:1536 + ch0 + 128, sl], ost[u][:], reads=[f"ost{u}"], writes=["obrT_d"])
            P.barrier()
    P.barrier()


import ml_dtypes
from concourse.bass_utils import run_bass_kernel_spmd

PARAM_NAMES = ["norm_mix", "w_in", "a_q_norm", "a_k_norm", "b_conv_w", "b_conv_b", "b_filt_w1", "b_filt_b1", "b_filt_w2", "b_filt_b2",
               "b_filt_w3", "b_filt_b3", "b_filt_w4", "b_filt_freq", "b_skip", "c_mu", "c_w0", "c_w_up", "c_a0", "c_a_up", "c_g_up", "c_k_k",
               "c_k_a", "c_r_k", "c_ln_w", "c_ln_b", "d_lq1", "d_lk1", "d_lq2", "d_lk2", "d_subln", "w_gate", "w_branch", "w_out", "norm_ffn",
               "w_ff_gate", "w_ff_up", "w_ff_down", "norm_final"]
DEPTH = 2


def host_consts():
    cs = {"ident": np.eye(128, dtype=np.float32)}
    cs.update(attn_consts())
    cs.update(hyena_consts())
    cs.update(rwkv_consts())
    return cs


def build_program(shapes, consts):
    P = Prog()
    x = P.dram("x", [T, D], F32, kind="ExternalInput").ap()
    prm = {n: P.dram(n, list(shapes[n]), F32, kind="ExternalInput").ap() for n in PARAM_NAMES}
    P.consts = {}
    for k, v in consts.items():
        P.consts[k] = P.dram("c_" + k, list(v.shape), BF16 if v.dtype == ml_dtypes.bfloat16 else F32, kind="ExternalInput").ap()
    out = P.dram("out", [T, D], F32, kind="ExternalOutput").ap()
    xs = P.dram("xs", [T, D], F32).ap()
    hT_d = P.dram("hT_d", [D, T], BF16).ap()
    projT_d = P.dram("projT_d", [D_IN, T], F32).ap()
    avt_d = P.dram("avt_d", [T, 256], BF16).ap()
    dvt_d = P.dram("dvt_d", [T, 512], BF16).ap()
    obrT_d = P.dram("obrT_d", [2560, T], BF16).ap()
    mergedT_d = P.dram("mergedT_d", [D, T], BF16).ap()
    zx_d = P.dram("zx_d", [2, 512, T], F32).ap()
    act_d = P.dram("act_d", [16, 128, FFN // 128, 128], BF16).ap()
    setup_common(P)
    P.barrier()
    for l in range(DEPTH):
        xsrc = x if l == 0 else xs
        pl = {n: prm[n][l] for n in PARAM_NAMES if n != "norm_final"}
        phase_norm_hT(P, xsrc, pl["norm_mix"], hT_d)
        phase_inproj(P, hT_d, pl["w_in"], projT_d, avt_d, dvt_d)
        phase_attn_a(P, projT_d, avt_d, pl["a_q_norm"], pl["a_k_norm"], obrT_d)
        lam_init = 0.8 - 0.6 * math.exp(-0.3 * l)
        phase_attn_d(P, projT_d, dvt_d, pl["d_lq1"], pl["d_lk1"], pl["d_lq2"], pl["d_lk2"], pl["d_subln"], lam_init, obrT_d)
        phase_hyena(P, projT_d, pl, zx_d, obrT_d)
        phase_rwkv(P, projT_d, pl, obrT_d)
        phase_merge(P, hT_d, obrT_d, pl["w_gate"], pl["w_branch"], mergedT_d)
        phase_outproj(P, mergedT_d, pl["w_out"], xsrc, xs)
        phase_norm_hT(P, xs, pl["norm_ffn"], hT_d)
        phase_ffn_up(P, hT_d, pl["w_ff_gate"], pl["w_ff_up"], act_d)
        phase_ffn_down(P, act_d, pl["w_ff_down"], xs, xs)
    phase_final_norm(P, xs, prm["norm_final"], out)
    P.barrier()
    P.emit()
    P.close()
    return P


def kernel(**inputs):
    n = 8
    consts = host_consts()
    shapes = {k: np.asarray(inputs[k]).shape for k in PARAM_NAMES}
    P = build_program(shapes, consts)
    x = np.ascontiguousarray(np.asarray(inputs["x"], dtype=np.float32))
    shared = {k: np.ascontiguousarray(np.asarray(inputs[k], dtype=np.float32)) for k in PARAM_NAMES}
    for k, v in consts.items():
        shared["c_" + k] = v
    in_maps = []
    for b in range(n):
        m = dict(shared)
        m["x"] = x[b]
        in_maps.append(m)
    res = run_bass_kernel_spmd(P.nc, in_maps, core_ids=list(range(n)))
    return np.stack([np.asarray(r["out"], dtype=np.float32) for r in res.results], axis=0)
```

```python
import numpy as np
from contextlib import ExitStack
import concourse.bass as bass
import concourse.mybir as mybir

F32 = mybir.dt.float32
BF16 = mybir.dt.bfloat16
AF = mybir.ActivationFunctionType
ALU = mybir.AluOpType
AX = mybir.AxisListType

EPOCH = 30000
NDMASEM = 24


class _Rec:
    def __init__(self):
        self.call = None

    def __getattr__(self, name):
        def f(*a, **k):
            self.call = (name, a, k)
            return self
        return f


class Prog:
    ENGS = ("pe", "act", "dve", "pool", "sp")

    def __init__(self):
        self.nc = bass.Bass("TRN2", target_bir_lowering=False)
        self.es = ExitStack()
        self.items = {e: [] for e in self.ENGS}
        self.seq = {e: 0 for e in self.ENGS}
        self.esems = {e: [] for e in self.ENGS}
        self.known = {e: {} for e in self.ENGS}
        self.lastw = {}
        self.readers = {}
        self.dma_cnt = {e: 0 for e in self.ENGS}
        self.dsems = {e: [] for e in self.ENGS}
        self.nsem = 0
        self._uid = 0

    def uid(self, p="t"):
        self._uid += 1
        return f"{p}{self._uid}"

    def sem(self, name):
        self.nsem += 1
        return self.es.enter_context(self.nc.semaphore(name))

    def sbuf(self, shape, dtype, name=None, stack=None):
        st = stack if stack is not None else self.es
        return st.enter_context(self.nc.sbuf_tensor(name or self.uid("sb"), list(shape), dtype))

    def psum(self, shape, dtype=F32, name=None, stack=None):
        st = stack if stack is not None else self.es
        return st.enter_context(self.nc.psum_tensor(name or self.uid("ps"), list(shape), dtype))

    def dram(self, name, shape, dtype, kind="Internal"):
        return self.nc.dram_tensor(name, list(shape), dtype, kind=kind)

    def _esem(self, e, idx):
        while len(self.esems[e]) <= idx:
            self.esems[e].append(self.sem(f"s_{e}_{len(self.esems[e])}"))
        return self.esems[e][idx]

    def _tok_sem(self, tok):
        kind = tok[0]
        if kind == "e":
            _, e, s = tok
            idx = (s - 1) // EPOCH
            return ("e", e, idx), self._esem(e, idx), (s - 1) % EPOCH + 1
        _, q, i = tok
        return ("d", q, i % NDMASEM), self.dsems[q][i % NDMASEM], 16 * (i // NDMASEM + 1)

    def _need(self, F, tok):
        key, sem, val = self._tok_sem(tok)
        if tok[0] == "e":
            _, e, idx = key
            for k2, v2 in self.known[F].items():
                if k2[0] == "e" and k2[1] == e and k2[2] > idx:
                    return
        if self.known[F].get(key, 0) >= val:
            return
        self.known[F][key] = val
        self.items[F].append(("wait", sem, val))

    def _deps(self, F, reads, writes, skip_self=False):
        toks = []
        for k in reads:
            t = self.lastw.get(k)
            if t is not None:
                toks.append(t)
        for k in writes:
            t = self.lastw.get(k)
            if t is not None:
                toks.append(t)
            toks.extend(self.readers.get(k, ()))
        for t in toks:
            if skip_self and t[0] == "e" and t[1] == F:
                continue
            self._need(F, t)

    def _commit(self, tok, reads, writes):
        for k in reads:
            lst = self.readers.setdefault(k, [])
            if tok[0] == "e":
                lst[:] = [t for t in lst if not (t[0] == "e" and t[1] == tok[1])]
            lst.append(tok)
        for k in writes:
            self.lastw[k] = tok
            self.readers[k] = []

    def op(self, eng, fn, reads=(), writes=(), skip_self=False):
        self._deps(eng, reads, writes, skip_self)
        self.seq[eng] += 1
        s = self.seq[eng]
        idx = (s - 1) // EPOCH
        sem = self._esem(eng, idx)
        r = _Rec()
        fn(r)
        assert r.call is not None
        self.items[eng].append(("op", r.call, sem, 1))
        tok = ("e", eng, s)
        self._commit(tok, reads, writes)
        return tok

    def dma(self, q, out, in_, reads=(), writes=(), **kw):
        if not self.dsems[q]:
            self.dsems[q] = [self.sem(f"d_{q}_{i}") for i in range(NDMASEM)]
        i = self.dma_cnt[q]
        self.dma_cnt[q] += 1
        if i >= NDMASEM:
            self._need(q, ("d", q, i - NDMASEM))
        self._deps(q, reads, writes)
        sem = self.dsems[q][i % NDMASEM]
        self.items[q].append(("op", ("dma_start", (), dict(out=out, in_=in_, **kw)), sem, 16))
        tok = ("d", q, i)
        self._commit(tok, reads, writes)
        return tok

    def barrier(self):
        toks = []
        for e in self.ENGS:
            if self.seq[e] > 0:
                toks.append(("e", e, self.seq[e]))
            n = self.dma_cnt[e]
            for i in range(max(0, n - NDMASEM), n):
                toks.append(("d", e, i))
        for F in self.ENGS:
            for t in toks:
                self._need(F, t)
        self.lastw.clear()
        self.readers.clear()

    def wait_all_on(self, F, keys):
        for k in keys:
            t = self.lastw.get(k)
            if t is not None:
                self._need(F, t)

    def emit(self):
        nc = self.nc
        block = self.es.enter_context(nc.Block())
        items = self.items

        def replay(eng, lst):
            for it in lst:
                if it[0] == "wait":
                    eng.wait_ge(it[1], it[2])
                else:
                    name, a, k = it[1]
                    ins = getattr(eng, name)(*a, **k)
                    ins.then_inc(it[2], it[3])

        @block.tensor
        def _(e):
            replay(e, items["pe"])

        @block.scalar
        def _(e):
            replay(e, items["act"])

        @block.vector
        def _(e):
            replay(e, items["dve"])

        @block.gpsimd
        def _(e):
            replay(e, items["pool"])

        @block.sync
        def _(e):
            replay(e, items["sp"])

    def close(self):
        self.es.close()

import math
import numpy as np

T = 2048
D = 2048
KC = D // 128
D_IN = 6592
FFN = 5632
EPS = 1e-6
I32 = mybir.dt.int32


def setup_common(P):
    P.banks = [P.psum([128, 512], F32, name=f"bank{i}") for i in range(8)]
    P.bank_rr = 0
    P.ident_f = P.sbuf([128, 128], F32, name="ident_f")
    P.ident_b = P.sbuf([128, 128], BF16, name="ident_b")
    P.ones_f = P.sbuf([128, 128], F32, name="ones_f")
    P.ones_b = P.sbuf([128, 128], BF16, name="ones_b")
    c = P.consts
    P.dma("sp", P.ident_f[:], c["ident"], writes=["ident_f"])
    P.dma("pool", P.ident_b[:], c["ident"], writes=["ident_b"])
    P.op("pool", lambda e: e.memset(P.ones_f[:], 1.0), writes=["ones_f"])
    P.op("pool", lambda e: e.memset(P.ones_b[:], 1.0), writes=["ones_b"])


def next_bank(P, lo=0, hi=8):
    n = hi - lo
    i = lo + (P.bank_rr % n)
    P.bank_rr += 1
    return i


def mm(P, out, lhsT, rhs, start, stop, reads, writes):
    return P.op("pe", lambda e: e.matmul(out, lhsT, rhs, start=start, stop=stop), reads=reads, writes=writes, skip_self=True)


def rstd_from_ss(P, ss, rstd, scale, eps, kss, krstd):
    P.op("act", lambda e: e.activation(out=rstd, in_=ss, func=AF.Sqrt, bias=eps, scale=scale), reads=[kss], writes=[krstd])
    P.op("dve", lambda e: e.reciprocal(out=rstd, in_=rstd), reads=[krstd], writes=[krstd])


def phase_norm_hT(P, xsrc, gain, hT_d):
    with ExitStack() as st:
        g_t = P.sbuf([128, KC], F32, stack=st)
        P.dma("sp", g_t[:], gain.rearrange("(kc p) -> p kc", p=128), writes=["g_t"], allow_slow_non_contiguous=True)
        xt = [P.sbuf([128, D], F32, stack=st) for _ in range(2)]
        junk = P.sbuf([128, D], BF16, stack=st)
        xn = [P.sbuf([128, D], BF16, stack=st) for _ in range(2)]
        ss = [P.sbuf([128, 1], F32, stack=st) for _ in range(2)]
        rs = [P.sbuf([128, 1], F32, stack=st) for _ in range(2)]
        hst = [P.sbuf([128, KC, 512], BF16, stack=st) for _ in range(2)]
        for tt in range(T // 128):
            s = tt % 2
            grp, gi = divmod(tt, 4)
            hs = grp % 2
            P.dma("sp", xt[s][:], xsrc[tt * 128:(tt + 1) * 128, :], writes=[f"xt{s}"])
            P.op("pool", lambda e, s=s: e.memset(ss[s][:], 0.0), writes=[f"ss{s}"])
            P.op("act", lambda e, s=s: e.activation(out=junk[:], in_=xt[s][:], func=AF.Square, accum_out=ss[s][:]),
                 reads=[f"xt{s}"], writes=["junk", f"ss{s}"])
            rstd_from_ss(P, ss[s][:], rs[s][:], 1.0 / D, EPS, f"ss{s}", f"rs{s}")
            P.op("dve", lambda e, s=s: e.tensor_scalar(out=xn[s][:], in0=xt[s][:], scalar1=rs[s][:, 0:1], scalar2=None, op0=ALU.mult),
                 reads=[f"xt{s}", f"rs{s}"], writes=[f"xn{s}"])
            for half in range(2):
                b = next_bank(P)
                pb = P.banks[b][:].bitcast(BF16)
                for j in range(8):
                    kc = half * 8 + j
                    P.op("pe", lambda e, pb=pb, j=j, kc=kc, s=s: e.transpose(pb[:, j * 128:(j + 1) * 128], xn[s][:, kc * 128:(kc + 1) * 128], P.ident_b[:]),
                         reads=[f"xn{s}", "ident_b"], writes=[f"pb{b}"], skip_self=True)
                P.op("dve", lambda e, pb=pb, half=half, hs=hs, gi=gi: e.tensor_tensor(
                    out=hst[hs][:, half * 8:(half + 1) * 8, gi * 128:(gi + 1) * 128],
                    in0=pb.rearrange("p (j t) -> p j t", j=8),
                    in1=g_t[:, half * 8:(half + 1) * 8].unsqueeze(2).to_broadcast([128, 8, 128]), op=ALU.mult),
                    reads=[f"pb{b}", "g_t"], writes=[f"hst{hs}"])
            if gi == 3:
                P.dma("pool", hT_d[:, grp * 512:(grp + 1) * 512].rearrange("(kc p) t -> p kc t", p=128), hst[hs][:],
                      reads=[f"hst{hs}"], writes=["hT_d"])
    P.barrier()


def load_fm(P, dst, src_d, key, q="sp"):
    P.dma(q, dst, src_d.rearrange("(kc p) t -> p kc t", p=128), reads=[], writes=[key])


def phase_inproj(P, hT_d, w_in, projT_d, avt_d, dvt_d):
    with ExitStack() as st:
        hT = P.sbuf([128, KC, T], BF16, stack=st)
        for q4 in range(4):
            P.dma("sp", hT[:, q4 * 4:(q4 + 1) * 4, :], hT_d[q4 * 512:(q4 + 1) * 512, :].rearrange("(kc p) t -> p kc t", p=128), writes=["hT"])
        wt = [P.sbuf([128, KC, 512], BF16, stack=st) for _ in range(2)]
        stg = [P.sbuf([128, T], F32, stack=st) for _ in range(2)]
        ncb = (D_IN + 511) // 512
        nfb = 0
        for cb in range(ncb):
            c0 = cb * 512
            w = min(512, D_IN - c0)
            s = cb % 2
            P.dma("pool", wt[s][:, :, 0:w], w_in[:, c0:c0 + w].rearrange("(kc p) c -> p kc c", p=128), writes=[f"wt{s}"])
            for fbi in range(w // 128 if w % 128 == 0 else (w + 127) // 128):
                f0 = fbi * 128
                fw_ = min(128, w - f0)
                ss_ = nfb % 2
                nfb += 1
                for tt in range(4):
                    b = next_bank(P)
                    for kc in range(KC):
                        mm(P, P.banks[b][0:fw_, :], wt[s][:, kc, f0:f0 + fw_], hT[:, kc, tt * 512:(tt + 1) * 512], kc == 0, kc == KC - 1,
                           [f"wt{s}", "hT"], [f"pb{b}"])
                    eng = "act" if tt % 2 == 0 else "dve"
                    if eng == "act":
                        P.op("act", lambda e, b=b, ss_=ss_, tt=tt, fw_=fw_: e.copy(out=stg[ss_][0:fw_, tt * 512:(tt + 1) * 512], in_=P.banks[b][0:fw_, :]),
                             reads=[f"pb{b}"], writes=[f"stg{ss_}"])
                    else:
                        P.op("dve", lambda e, b=b, ss_=ss_, tt=tt, fw_=fw_: e.tensor_copy(out=stg[ss_][0:fw_, tt * 512:(tt + 1) * 512], in_=P.banks[b][0:fw_, :]),
                             reads=[f"pb{b}"], writes=[f"stg{ss_}"])
                P.dma("sp", projT_d[c0 + f0:c0 + f0 + fw_, :], stg[ss_][0:fw_, :], reads=[f"stg{ss_}"], writes=["projT_d"])
        for (c0, w, dst, nm) in ((1280, 256, avt_d, "av"), (6080, 512, dvt_d, "dv")):
            wv = P.sbuf([128, KC, w], BF16, stack=st)
            P.dma("pool", wv[:], w_in[:, c0:c0 + w].rearrange("(kc p) c -> p kc c", p=128), writes=[nm + "w"])
            vst = [P.sbuf([128, w], BF16, stack=st) for _ in range(2)]
            for tt in range(T // 128):
                s = tt % 2
                b = next_bank(P)
                for kc in range(KC):
                    mm(P, P.banks[b][:, 0:w], hT[:, kc, tt * 128:(tt + 1) * 128], wv[:, kc, :], kc == 0, kc == KC - 1, [nm + "w", "hT"], [f"pb{b}"])
                P.op("act", lambda e, b=b, s=s, w=w, vst=vst: e.copy(out=vst[s][:], in_=P.banks[b][:, 0:w]), reads=[f"pb{b}"], writes=[f"{nm}st{s}"])
                P.dma("sp", dst[tt * 128:(tt + 1) * 128, :], vst[s][:], reads=[f"{nm}st{s}"], writes=[nm + "_d"])
    P.barrier()


BR_OFF = (0, 1024, 1536, 2048)
BR_W = (1024, 512, 512, 512)


def phase_merge(P, hT_d, obrT_d, w_gate, w_branch, mergedT_d):
    with ExitStack() as st:
        hT = P.sbuf([128, KC, 1024], BF16, stack=st)
        ob = P.sbuf([128, 20, 1024], BF16, stack=st)
        wg = [P.sbuf([128, KC, 512], BF16, stack=st) for _ in range(2)]
        wb = [P.sbuf([128, 8, 512], BF16, stack=st) for _ in range(2)]
        acc = P.sbuf([128, 4, 2, 512], F32, stack=st)
        sg = [P.sbuf([128, 512], F32, stack=st) for _ in range(2)]
        tmp = [P.sbuf([128, 512], F32, stack=st) for _ in range(2)]
        mst = [P.sbuf([128, 1024], BF16, stack=st) for _ in range(2)]
        it = 0
        nw = 0
        for th in range(2):
            t0 = th * 1024
            for q4 in range(4):
                P.dma("sp", hT[:, q4 * 4:(q4 + 1) * 4, :], hT_d[q4 * 512:(q4 + 1) * 512, t0:t0 + 1024].rearrange("(kc p) t -> p kc t", p=128), writes=["hT"])
            for q4 in range(5):
                P.dma("sp", ob[:, q4 * 4:(q4 + 1) * 4, :], obrT_d[q4 * 512:(q4 + 1) * 512, t0:t0 + 1024].rearrange("(kc p) t -> p kc t", p=128), writes=["ob"])
            for cb in range(4):
                for i in range(4):
                    s = nw % 2
                    nw += 1
                    kci = BR_W[i] // 128
                    P.dma("pool", wg[s][:], w_gate[i][:, cb * 512:(cb + 1) * 512].rearrange("(kc p) c -> p kc c", p=128), writes=[f"wg{s}"])
                    P.dma("pool", wb[s][:, 0:kci, :], w_branch[BR_OFF[i]:BR_OFF[i] + BR_W[i], cb * 512:(cb + 1) * 512].rearrange("(kc p) c -> p kc c", p=128), writes=[f"wb{s}"])
                    for fbi in range(4):
                        for tt in range(2):
                            bg = next_bank(P)
                            by = next_bank(P)
                            for kc in range(KC):
                                mm(P, P.banks[bg][:], wg[s][:, kc, fbi * 128:(fbi + 1) * 128], hT[:, kc, tt * 512:(tt + 1) * 512], kc == 0, kc == KC - 1, [f"wg{s}", "hT"], [f"pb{bg}"])
                            for kc in range(kci):
                                mm(P, P.banks[by][:], wb[s][:, kc, fbi * 128:(fbi + 1) * 128], ob[:, BR_OFF[i] // 128 + kc, tt * 512:(tt + 1) * 512], kc == 0, kc == kci - 1, [f"wb{s}", "ob"], [f"pb{by}"])
                            u = it % 2
                            it += 1
                            P.op("act", lambda e, bg=bg, u=u: e.activation(out=sg[u][:], in_=P.banks[bg][:], func=AF.Sigmoid), reads=[f"pb{bg}"], writes=[f"sg{u}"])
                            ak = f"acc{fbi}_{tt}"
                            if i == 0:
                                P.op("dve", lambda e, by=by, u=u, fbi=fbi, tt=tt: e.tensor_tensor(out=acc[:, fbi, tt, :], in0=sg[u][:], in1=P.banks[by][:], op=ALU.mult),
                                     reads=[f"sg{u}", f"pb{by}"], writes=[ak])
                            else:
                                P.op("dve", lambda e, by=by, u=u: e.tensor_tensor(out=tmp[u][:], in0=sg[u][:], in1=P.banks[by][:], op=ALU.mult),
                                     reads=[f"sg{u}", f"pb{by}"], writes=[f"tmp{u}"])
                                if i < 3:
                                    P.op("dve", lambda e, u=u, fbi=fbi, tt=tt: e.tensor_tensor(out=acc[:, fbi, tt, :], in0=acc[:, fbi, tt, :], in1=tmp[u][:], op=ALU.add),
                                         reads=[f"tmp{u}", ak], writes=[ak])
                                else:
                                    ms = fbi % 2
                                    P.op("dve", lambda e, u=u, fbi=fbi, tt=tt, ms=ms: e.tensor_tensor(out=mst[ms][:, tt * 512:(tt + 1) * 512], in0=acc[:, fbi, tt, :], in1=tmp[u][:], op=ALU.add),
                                         reads=[f"tmp{u}", ak], writes=[f"mst{ms}"])
                                    if tt == 1:
                                        r0 = cb * 512 + fbi * 128
                                        P.dma("sp", mergedT_d[r0:r0 + 128, t0:t0 + 1024], mst[ms][:], reads=[f"mst{ms}"], writes=["mergedT_d"])
    P.barrier()


def phase_outproj(P, mergedT_d, w_out, xsrc, xdst):
    with ExitStack() as st:
        mT = P.sbuf([128, KC, T], BF16, stack=st)
        for q4 in range(4):
            P.dma("sp", mT[:, q4 * 4:(q4 + 1) * 4, :], mergedT_d[q4 * 512:(q4 + 1) * 512, :].rearrange("(kc p) t -> p kc t", p=128), writes=["mT"])
        wt = [P.sbuf([128, KC, 512], BF16, stack=st) for _ in range(2)]
        xt = [P.sbuf([128, 512], F32, stack=st) for _ in range(3)]
        n = 0
        for cb in range(4):
            s = cb % 2
            P.dma("pool", wt[s][:], w_out[:, cb * 512:(cb + 1) * 512].rearrange("(kc p) c -> p kc c", p=128), writes=[f"wt{s}"])
            for tt in range(T // 128):
                u = n % 3
                n += 1
                P.dma("sp", xt[u][:], xsrc[tt * 128:(tt + 1) * 128, cb * 512:(cb + 1) * 512], writes=[f"xt{u}"])
                b = next_bank(P)
                for kc in range(KC):
                    mm(P, P.banks[b][:], mT[:, kc, tt * 128:(tt + 1) * 128], wt[s][:, kc, :], kc == 0, kc == KC - 1, [f"wt{s}", "mT"], [f"pb{b}"])
                P.op("dve", lambda e, b=b, u=u: e.tensor_tensor(out=xt[u][:], in0=xt[u][:], in1=P.banks[b][:], op=ALU.add), reads=[f"pb{b}", f"xt{u}"], writes=[f"xt{u}"])
                P.dma("act", xdst[tt * 128:(tt + 1) * 128, cb * 512:(cb + 1) * 512], xt[u][:], reads=[f"xt{u}"], writes=[f"xd{tt}_{cb}"])
    P.barrier()


def phase_ffn_up(P, hT_d, w_g, w_u, act_d):
    with ExitStack() as st:
        hT = P.sbuf([128, KC, T], BF16, stack=st)
        for q4 in range(4):
            P.dma("sp", hT[:, q4 * 4:(q4 + 1) * 4, :], hT_d[q4 * 512:(q4 + 1) * 512, :].rearrange("(kc p) t -> p kc t", p=128), writes=["hT"])
        wg = [P.sbuf([128, KC, 512], BF16, stack=st) for _ in range(2)]
        wu = [P.sbuf([128, KC, 512], BF16, stack=st) for _ in range(2)]
        sg = [P.sbuf([128, 512], F32, stack=st) for _ in range(2)]
        ast = [P.sbuf([128, 512], BF16, stack=st) for _ in range(3)]
        it = 0
        for hb in range(FFN // 512):
            s = hb % 2
            P.dma("pool", wg[s][:], w_g[:, hb * 512:(hb + 1) * 512].rearrange("(kc p) c -> p kc c", p=128), writes=[f"wg{s}"])
            P.dma("pool", wu[s][:], w_u[:, hb * 512:(hb + 1) * 512].rearrange("(kc p) c -> p kc c", p=128), writes=[f"wu{s}"])
            for fbi in range(4):
                kcf = hb * 4 + fbi
                for tt in range(4):
                    bg = next_bank(P)
                    bu = next_bank(P)
                    for kc in range(KC):
                        mm(P, P.banks[bg][:], wg[s][:, kc, fbi * 128:(fbi + 1) * 128], hT[:, kc, tt * 512:(tt + 1) * 512], kc == 0, kc == KC - 1, [f"wg{s}", "hT"], [f"pb{bg}"])
                    for kc in range(KC):
                        mm(P, P.banks[bu][:], wu[s][:, kc, fbi * 128:(fbi + 1) * 128], hT[:, kc, tt * 512:(tt + 1) * 512], kc == 0, kc == KC - 1, [f"wu{s}", "hT"], [f"pb{bu}"])
                    u = it % 2
                    a3 = it % 3
                    it += 1
                    P.op("act", lambda e, bg=bg, u=u: e.activation(out=sg[u][:], in_=P.banks[bg][:], func=AF.Silu), reads=[f"pb{bg}"], writes=[f"sg{u}"])
                    P.op("dve", lambda e, bu=bu, u=u, a3=a3: e.tensor_tensor(out=ast[a3][:], in0=sg[u][:], in1=P.banks[bu][:], op=ALU.mult),
                         reads=[f"sg{u}", f"pb{bu}"], writes=[f"ast{a3}"])
                    P.dma("sp", act_d[tt * 4:(tt + 1) * 4, :, kcf, :].rearrange("a p t -> p a t"), ast[a3][:].rearrange("p (a t) -> p a t", a=4),
                          reads=[f"ast{a3}"], writes=["act_d"])
    P.barrier()


def phase_ffn_down(P, act_d, w_d, xsrc, xdst):
    NK = FFN // 128
    with ExitStack() as st:
        wd = [P.sbuf([128, NK, 512], BF16, stack=st) for _ in range(2)]
        at = [P.sbuf([128, NK, 128], BF16, stack=st) for _ in range(2)]
        xt = [P.sbuf([128, 512], F32, stack=st) for _ in range(3)]
        n = 0
        for cb in range(4):
            s = cb % 2
            for h2 in range(2):
                P.dma("pool", wd[s][:, h2 * 22:(h2 + 1) * 22, :], w_d[h2 * 2816:(h2 + 1) * 2816, cb * 512:(cb + 1) * 512].rearrange("(kc p) c -> p kc c", p=128), writes=[f"wd{s}"])
            for tt in range(T // 128):
                u = n % 3
                a = n % 2
                n += 1
                P.dma("sp", at[a][:], act_d[tt], writes=[f"at{a}"])
                P.dma("sp", xt[u][:], xsrc[tt * 128:(tt + 1) * 128, cb * 512:(cb + 1) * 512], writes=[f"xt{u}"])
                b = next_bank(P)
                for kc in range(NK):
                    mm(P, P.banks[b][:], at[a][:, kc, :], wd[s][:, kc, :], kc == 0, kc == NK - 1, [f"wd{s}", f"at{a}"], [f"pb{b}"])
                P.op("dve", lambda e, b=b, u=u: e.tensor_tensor(out=xt[u][:], in0=xt[u][:], in1=P.banks[b][:], op=ALU.add), reads=[f"pb{b}", f"xt{u}"], writes=[f"xt{u}"])
                P.dma("act", xdst[tt * 128:(tt + 1) * 128, cb * 512:(cb + 1) * 512], xt[u][:], reads=[f"xt{u}"], writes=[f"xd{tt}_{cb}"])
    P.barrier()


def phase_final_norm(P, xsrc, gain, out_d):
    with ExitStack() as st:
        gb = P.sbuf([128, D], F32, stack=st)
        P.dma("sp", gb[:], gain.partition_broadcast(128), writes=["gb"])
        xt = [P.sbuf([128, D], F32, stack=st) for _ in range(2)]
        junk = P.sbuf([128, D], BF16, stack=st)
        ss = [P.sbuf([128, 1], F32, stack=st) for _ in range(2)]
        rs = [P.sbuf([128, 1], F32, stack=st) for _ in range(2)]
        for tt in range(T // 128):
            s = tt % 2
            P.dma("sp", xt[s][:], xsrc[tt * 128:(tt + 1) * 128, :], writes=[f"xt{s}"])
            P.op("pool", lambda e, s=s: e.memset(ss[s][:], 0.0), writes=[f"ss{s}"])
            P.op("act", lambda e, s=s: e.activation(out=junk[:], in_=xt[s][:], func=AF.Square, accum_out=ss[s][:]), reads=[f"xt{s}"], writes=["junk", f"ss{s}"])
            rstd_from_ss(P, ss[s][:], rs[s][:], 1.0 / D, EPS, f"ss{s}", f"rs{s}")
            P.op("dve", lambda e, s=s: e.scalar_tensor_tensor(out=xt[s][:], in0=xt[s][:], scalar=rs[s][:, 0:1], in1=gb[:], op0=ALU.mult, op1=ALU.mult),
                 reads=[f"xt{s}", f"rs{s}", "gb"], writes=[f"xt{s}"])
            P.dma("sp", out_d[tt * 128:(tt + 1) * 128, :], xt[s][:], reads=[f"xt{s}"], writes=[f"out{tt}"])
    P.barrier()

import math
import numpy as np


def phase_attn_a(P, projT_d, avt_d, qnorm, knorm, obrT_d):
    c = P.consts
    with ExitStack() as st:
        cosT = P.sbuf([128, T], F32, stack=st)
        sinT = P.sbuf([128, T], F32, stack=st)
        Rm = P.sbuf([128, 128], BF16, stack=st)
        P.dma("sp", cosT[:], c["ropecos"], writes=["cosT"])
        P.dma("sp", sinT[:], c["ropesin"], writes=["sinT"])
        P.dma("pool", Rm[:], c["rotmat"], writes=["Rm"])
        gq = P.sbuf([128, 1], F32, stack=st)
        gk = P.sbuf([128, 1], F32, stack=st)
        P.dma("sp", gq[:], qnorm.rearrange("(p o) -> p o", o=1), writes=["gq"])
        P.dma("sp", gk[:], knorm.rearrange("(p o) -> p o", o=1), writes=["gk"])
        P.op("dve", lambda e: e.tensor_scalar(out=gq[:], in0=gq[:], scalar1=128.0 ** -0.5, scalar2=None, op0=ALU.mult), reads=["gq"], writes=["gq"])
        qr = P.sbuf([128, 10, T], BF16, stack=st)
        with ExitStack() as st2:
            qf = [P.sbuf([128, T], F32, stack=st2) for _ in range(2)]
            sq = [P.sbuf([128, T], BF16, stack=st2) for _ in range(2)]
            rstd = [P.sbuf([128, 512], F32, stack=st2) for _ in range(2)]
            qn = [P.sbuf([128, 512], BF16, stack=st2) for _ in range(2)]
            t1 = [P.sbuf([128, 512], F32, stack=st2) for _ in range(2)]
            t2 = [P.sbuf([128, 512], F32, stack=st2) for _ in range(2)]
            n = 0
            for hh in range(10):
                s = hh % 2
                r0 = hh * 128 if hh < 8 else 1024 + (hh - 8) * 128
                g = gq if hh < 8 else gk
                gkey = "gq" if hh < 8 else "gk"
                P.dma("sp", qf[s][:], projT_d[r0:r0 + 128, :], writes=[f"qf{s}"])
                P.op("act", lambda e, s=s: e.activation(out=sq[s][:], in_=qf[s][:], func=AF.Square), reads=[f"qf{s}"], writes=[f"sq{s}"])
                for tt in range(4):
                    u = n % 2
                    n += 1
                    sl = slice(tt * 512, (tt + 1) * 512)
                    b = next_bank(P)
                    mm(P, P.banks[b][:], P.ones_b[:], sq[s][:, sl], True, True, ["ones_b", f"sq{s}"], [f"pb{b}"])
                    P.op("act", lambda e, b=b, u=u: e.activation(out=rstd[u][:], in_=P.banks[b][:], func=AF.Sqrt, bias=EPS, scale=1.0 / 128), reads=[f"pb{b}"], writes=[f"rstd{u}"])
                    P.op("dve", lambda e, u=u: e.reciprocal(out=rstd[u][:], in_=rstd[u][:]), reads=[f"rstd{u}"], writes=[f"rstd{u}"])
                    P.op("dve", lambda e, u=u, s=s, sl=sl, g=g: e.scalar_tensor_tensor(out=qn[u][:], in0=qf[s][:, sl], scalar=g[:, 0:1], in1=rstd[u][:], op0=ALU.mult, op1=ALU.mult),
                         reads=[f"qf{s}", gkey, f"rstd{u}"], writes=[f"qn{u}"])
                    b2 = next_bank(P)
                    mm(P, P.banks[b2][:], Rm[:], qn[u][:], True, True, ["Rm", f"qn{u}"], [f"pb{b2}"])
                    P.op("pool", lambda e, u=u, sl=sl: e.tensor_tensor(out=t1[u][:], in0=qn[u][:], in1=cosT[:, sl], op=ALU.mult), reads=[f"qn{u}", "cosT"], writes=[f"t1{u}"])
                    P.op("dve", lambda e, u=u, sl=sl, b2=b2: e.tensor_tensor(out=t2[u][:], in0=P.banks[b2][:], in1=sinT[:, sl], op=ALU.mult), reads=[f"pb{b2}", "sinT"], writes=[f"t2{u}"])
                    P.op("pool", lambda e, u=u, sl=sl, hh=hh: e.tensor_tensor(out=qr[:, hh, sl], in0=t1[u][:], in1=t2[u][:], op=ALU.add), reads=[f"t1{u}", f"t2{u}"], writes=["qr"])
        P.barrier()
        V = [P.sbuf([128, 16, 128], BF16, stack=st) for _ in range(2)]
        pT = [P.sbuf([128, 512], BF16, stack=st) for _ in range(4)]
        rden = [P.sbuf([128, 512], F32, stack=st) for _ in range(2)]
        ost = [P.sbuf([128, 512], BF16, stack=st) for _ in range(2)]
        iters = [(g, hq, qt, kt) for g in range(2) for hq in range(4) for qt in range(4) for kt in range(16)]
        sbank = {}

        def issue_S(it):
            g, hq, qt, kt = it
            h = g * 4 + hq
            if hq == 0 and qt == 0 and kt == 0:
                P.dma("sp", V[g][:], avt_d[:, g * 128:(g + 1) * 128].rearrange("(kt p) d -> p kt d", p=128), writes=[f"V{g}"])
            bs = next_bank(P, 0, 4)
            sbank[it] = bs
            mm(P, P.banks[bs][:], qr[:, 8 + g, kt * 128:(kt + 1) * 128], qr[:, h, qt * 512:(qt + 1) * 512], True, True, ["qr"], [f"pb{bs}"])

        DEPTH = 2
        for n in range(DEPTH):
            issue_S(iters[n])
        for n, it in enumerate(iters):
            if n + DEPTH < len(iters):
                issue_S(iters[n + DEPTH])
            g, hq, qt, kt = it
            h = g * 4 + hq
            u = (n // 16) % 2
            bo, bd = 4 + u, 6 + u
            qs = slice(qt * 512, (qt + 1) * 512)
            bs = sbank.pop(it)
            v = n % 4
            P.op("act", lambda e, bs=bs, v=v: e.activation(out=pT[v][:], in_=P.banks[bs][:], func=AF.Exp), reads=[f"pb{bs}"], writes=[f"pT{v}"])
            mm(P, P.banks[bo][:], V[g][:, kt, :], pT[v][:], kt == 0, kt == 15, [f"V{g}", f"pT{v}"], [f"pb{bo}"])
            mm(P, P.banks[bd][:], P.ones_b[:], pT[v][:], kt == 0, kt == 15, ["ones_b", f"pT{v}"], [f"pb{bd}"])
            if kt == 15:
                P.op("dve", lambda e, u=u, bd=bd: e.reciprocal(out=rden[u][:], in_=P.banks[bd][:]), reads=[f"pb{bd}"], writes=[f"rden{u}"])
                P.op("dve", lambda e, u=u, bo=bo: e.tensor_tensor(out=ost[u][:], in0=P.banks[bo][:], in1=rden[u][:], op=ALU.mult), reads=[f"pb{bo}", f"rden{u}"], writes=[f"ost{u}"])
                P.dma("sp", obrT_d[h * 128:(h + 1) * 128, qs], ost[u][:], reads=[f"ost{u}"], writes=["obrT_d"])
    P.barrier()


def phase_attn_d(P, projT_d, dvt_d, lq1, lk1, lq2, lk2, subln, lam_init, obrT_d):
    c = P.consts
    QOFF, KOFF = 5056, 5568
    with ExitStack() as st:
        dist = P.sbuf([128, 3968], F32, stack=st)
        P.dma("sp", dist[:], c["alibi"], writes=["dist"])
        lv = [P.sbuf([128, 64], F32, stack=st) for _ in range(4)]
        for i, a in enumerate((lq1, lk1, lq2, lk2)):
            P.dma("sp", lv[i][:], a.partition_broadcast(128), writes=[f"lv{i}"])
        pr = P.sbuf([128, 64], F32, stack=st)
        e12 = P.sbuf([128, 2], F32, stack=st)
        nlam = P.sbuf([128, 1], F32, stack=st)
        for j in range(2):
            P.op("dve", lambda e, j=j: e.tensor_tensor(out=pr[:], in0=lv[2 * j][:], in1=lv[2 * j + 1][:], op=ALU.mult), reads=[f"lv{2*j}", f"lv{2*j+1}"], writes=["pr"])
            P.op("dve", lambda e, j=j: e.reduce_sum(out=e12[:, j:j + 1], in_=pr[:], axis=AX.X), reads=["pr"], writes=["e12"])
        P.op("act", lambda e: e.activation(out=e12[:], in_=e12[:], func=AF.Exp), reads=["e12"], writes=["e12"])
        P.op("dve", lambda e: e.tensor_tensor(out=nlam[:], in0=e12[:, 1:2], in1=e12[:, 0:1], op=ALU.subtract), reads=["e12"], writes=["nlam"])
        P.op("dve", lambda e: e.tensor_scalar(out=nlam[:], in0=nlam[:], scalar1=-float(lam_init), scalar2=None, op0=ALU.add), reads=["nlam"], writes=["nlam"])
        gs = P.sbuf([128, 1], F32, stack=st)
        P.dma("sp", gs[:], subln.rearrange("(p o) -> p o", o=1), writes=["gs"])
        P.op("dve", lambda e: e.tensor_scalar(out=gs[:], in0=gs[:], scalar1=float(1.0 - lam_init), scalar2=None, op0=ALU.mult), reads=["gs"], writes=["gs"])
        qd = P.sbuf([128, 4, T], BF16, stack=st)
        kd = P.sbuf([128, 4, T], BF16, stack=st)
        with ExitStack() as st2:
            tmpf = [P.sbuf([128, T], F32, stack=st2) for _ in range(2)]
            for i in range(8):
                s = i % 2
                h = i % 4
                isq = i < 4
                r0 = (QOFF if isq else KOFF) + h * 128
                P.dma("sp", tmpf[s][:], projT_d[r0:r0 + 128, :], writes=[f"tmpf{s}"])
                dst = qd if isq else kd
                P.op("act", lambda e, s=s, h=h, dst=dst, isq=isq: e.mul(out=dst[:, h, :], in_=tmpf[s][:], mul=(0.125 if isq else 1.0)),
                     reads=[f"tmpf{s}"], writes=["qd" if isq else "kd"])
        P.barrier()
        V = [P.sbuf([128, 16, 128], BF16, stack=st) for _ in range(2)]
        sb = [P.sbuf([128, 512], F32, stack=st) for _ in range(4)]
        pT = [P.sbuf([128, 512], BF16, stack=st) for _ in range(4)]
        rd = [P.sbuf([128, 512], F32, stack=st) for _ in range(2)]
        o0 = P.sbuf([128, 512], F32, stack=st)
        o1 = P.sbuf([128, 512], F32, stack=st)
        osq = P.sbuf([128, 512], F32, stack=st)
        rstd = P.sbuf([128, 512], F32, stack=st)
        ost = [P.sbuf([128, 512], BF16, stack=st) for _ in range(2)]
        iters = [(h, qt, kt, cc) for h in range(4) for qt in range(4) for kt in range(16) for cc in range(2)]
        sbank = {}

        def issue_S(it):
            h, qt, kt, cc = it
            vs = h % 2
            if qt == 0 and kt == 0 and cc == 0:
                P.dma("sp", V[vs][:], dvt_d[:, h * 128:(h + 1) * 128].rearrange("(kt p) d -> p kt d", p=128), writes=[f"V{vs}"])
            bs = next_bank(P, 0, 4)
            sbank[it] = bs
            pr_ = slice(cc * 64, (cc + 1) * 64)
            mm(P, P.banks[bs][:], kd[pr_, h, kt * 128:(kt + 1) * 128], qd[pr_, h, qt * 512:(qt + 1) * 512], True, True, ["qd", "kd"], [f"pb{bs}"])

        DEPTH = 2
        for n in range(DEPTH):
            issue_S(iters[n])
        nq = 0
        for n, it in enumerate(iters):
            if n + DEPTH < len(iters):
                issue_S(iters[n + DEPTH])
            h, qt, kt, cc = it
            slope = 2.0 ** (-2.0 * (h + 1))
            vs = h % 2
            qs = slice(qt * 512, (qt + 1) * 512)
            off = qt * 512 - kt * 128 + 1920
            bs = sbank.pop(it)
            v = n % 4
            P.op("dve", lambda e, v=v, bs=bs, off=off, slope=slope: e.scalar_tensor_tensor(out=sb[v][:], in0=dist[:, off:off + 512], scalar=-slope, in1=P.banks[bs][:], op0=ALU.mult, op1=ALU.add),
                 reads=["dist", f"pb{bs}"], writes=[f"sb{v}"])
            P.op("act", lambda e, v=v: e.activation(out=pT[v][:], in_=sb[v][:], func=AF.Exp), reads=[f"sb{v}"], writes=[f"pT{v}"])
            mm(P, P.banks[4 + cc][:], V[vs][:, kt, :], pT[v][:], kt == 0, kt == 15, [f"V{vs}", f"pT{v}"], [f"pb{4+cc}"])
            mm(P, P.banks[6 + cc][:], P.ones_b[:], pT[v][:], kt == 0, kt == 15, ["ones_b", f"pT{v}"], [f"pb{6+cc}"])
            if kt == 15 and cc == 1:
                for c2 in range(2):
                    P.op("dve", lambda e, c2=c2: e.reciprocal(out=rd[c2][:], in_=P.banks[6 + c2][:]), reads=[f"pb{6+c2}"], writes=[f"rd{c2}"])
                P.op("dve", lambda e: e.tensor_tensor(out=o0[:], in0=P.banks[4][:], in1=rd[0][:], op=ALU.mult), reads=["pb4", "rd0"], writes=["o0"])
                P.op("dve", lambda e: e.tensor_tensor(out=o1[:], in0=P.banks[5][:], in1=rd[1][:], op=ALU.mult), reads=["pb5", "rd1"], writes=["o1"])
                P.op("dve", lambda e: e.scalar_tensor_tensor(out=o0[:], in0=o1[:], scalar=nlam[:, 0:1], in1=o0[:], op0=ALU.mult, op1=ALU.add), reads=["o0", "o1", "nlam"], writes=["o0"])
                P.op("act", lambda e: e.activation(out=osq[:], in_=o0[:], func=AF.Square), reads=["o0"], writes=["osq"])
                bn_ = next_bank(P, 0, 4)
                mm(P, P.banks[bn_][:], P.ones_f[:], osq[:], True, True, ["ones_f", "osq"], [f"pb{bn_}"])
                P.op("act", lambda e, bn_=bn_: e.activation(out=rstd[:], in_=P.banks[bn_][:], func=AF.Sqrt, bias=EPS, scale=1.0 / 128), reads=[f"pb{bn_}"], writes=["rstd"])
                P.op("dve", lambda e: e.reciprocal(out=rstd[:], in_=rstd[:]), reads=["rstd"], writes=["rstd"])
                u = nq % 2
                nq += 1
                P.op("dve", lambda e, u=u: e.scalar_tensor_tensor(out=ost[u][:], in0=o0[:], scalar=gs[:, 0:1], in1=rstd[:], op0=ALU.mult, op1=ALU.mult), reads=["o0", "gs", "rstd"], writes=[f"ost{u}"])
                P.dma("sp", obrT_d[2048 + h * 128:2048 + (h + 1) * 128, qs], ost[u][:], reads=[f"ost{u}"], writes=["obrT_d"])
    P.barrier()


def attn_consts():
    f32 = np.float32
    GRID_W, HD = 64, 128
    rows = T // GRID_W
    row_idx = np.repeat(np.arange(rows, dtype=f32), GRID_W)
    col_idx = np.tile(np.arange(GRID_W, dtype=f32), rows)
    axis_dim = HD // 2
    inv_freq = (f32(10000.0) ** (-np.arange(0, axis_dim, 2, dtype=f32) / f32(axis_dim))).astype(f32)
    ang_r = row_idx[:, None] * inv_freq[None, :]
    ang_c = col_idx[:, None] * inv_freq[None, :]
    ang = np.concatenate([ang_r, ang_r, ang_c, ang_c], axis=-1).astype(f32)
    ropecos = np.ascontiguousarray(np.cos(ang).T.astype(f32))
    ropesin = np.ascontiguousarray(np.sin(ang).T.astype(f32))
    Rm = np.zeros((128, 128), f32)
    for half in range(2):
        for i in range(32):
            j = half * 64 + i
            Rm[j + 32, j] = -1.0
            Rm[j, j + 32] = 1.0
    m = np.arange(3968, dtype=f32)[None, :]
    p = np.arange(128, dtype=f32)[:, None]
    alibi = np.abs(m - 1920.0 - p).astype(f32)
    return {"ropecos": ropecos, "ropesin": ropesin, "rotmat": Rm, "alibi": alibi}

import math
import numpy as np

TWO_PI = 2.0 * math.pi
UOFF = 1536


def sin_rr(P, dst, src, tmp, tmpi, ksrc, kdst, ktmp):
    P.op("dve", lambda e: e.tensor_scalar(out=tmp, in0=src, scalar1=1.0 / TWO_PI, scalar2=16.0, op0=ALU.mult, op1=ALU.add), reads=[ksrc], writes=[ktmp])
    P.op("dve", lambda e: e.tensor_copy(out=tmpi, in_=tmp), reads=[ktmp], writes=[ktmp + "i"])
    P.op("dve", lambda e: e.tensor_copy(out=tmp, in_=tmpi), reads=[ktmp + "i"], writes=[ktmp])
    P.op("dve", lambda e: e.tensor_scalar(out=tmp, in0=tmp, scalar1=-16.0, scalar2=None, op0=ALU.add), reads=[ktmp], writes=[ktmp])
    P.op("dve", lambda e: e.scalar_tensor_tensor(out=src, in0=tmp, scalar=-TWO_PI, in1=src, op0=ALU.mult, op1=ALU.add), reads=[ktmp, ksrc], writes=[ksrc])
    P.op("dve", lambda e: e.tensor_scalar(out=src, in0=src, scalar1=3.1415925, scalar2=-3.1415925, op0=ALU.min, op1=ALU.max), reads=[ksrc], writes=[ksrc])
    P.op("act", lambda e: e.activation(out=dst, in_=src, func=AF.Sin), reads=[ksrc], writes=[kdst])


def phase_hyena(P, projT_d, prm, zx_d, obrT_d):
    c = P.consts
    col = lambda a: a.rearrange("(p o) -> p o", o=1)
    with ExitStack() as st:
      Yre = P.sbuf([128, 16, 512], BF16, stack=st)
      Yim = P.sbuf([128, 16, 512], BF16, stack=st)
      skc = P.sbuf([128, 4], F32, stack=st)
      P.dma("sp", skc[:], prm["b_skip"].rearrange("(cc p) -> p cc", p=128), writes=["skc"], allow_slow_non_contiguous=True)
      with ExitStack() as sK:
        st_outer = st
        st = sK
        Kre = P.sbuf([128, 16, 512], F32, stack=st)
        Kim = P.sbuf([128, 16, 512], F32, stack=st)
        sc = P.sbuf([128, 16], F32, stack=st)
        P.dma("sp", sc[:], c["dft_sc"], writes=["sc"])
        alt = P.sbuf([128, 1], BF16, stack=st)
        P.dma("pool", alt[:], c["altsign"], writes=["alt"])
        with ExitStack() as s2:
          Hs = P.sbuf([128, 16, 512], BF16, stack=s2)
          Hd = P.sbuf([128, 16, 512], BF16, stack=s2)
          rnorm = P.sbuf([128, 512], F32, stack=s2)
          with ExitStack() as s2a:
            s2_outer = s2
            s2 = s2a
            zT = P.sbuf([33, T], F32, stack=s2)
            P.dma("sp", zT[:], c["hy_z"], writes=["zT"])
            w1 = P.sbuf([33, 64], F32, stack=s2)
            w2 = P.sbuf([64, 64], F32, stack=s2)
            w3 = P.sbuf([64, 64], F32, stack=s2)
            w4 = P.sbuf([64, 1024], F32, stack=s2)
            for t_, a, k in ((w1, prm["b_filt_w1"], "w1"), (w2, prm["b_filt_w2"], "w2"), (w3, prm["b_filt_w3"], "w3"), (w4, prm["b_filt_w4"], "w4")):
                P.dma("sp", t_[:], a, writes=[k])
            bcol = P.sbuf([64, 4], F32, stack=s2)
            for i, nm in enumerate(("b_filt_b1", "b_filt_b2", "b_filt_b3", "b_filt_freq")):
                P.dma("sp", bcol[:, i:i + 1], col(prm[nm]), writes=["bcol"])
            hid = [P.sbuf([64, T], F32, stack=s2) for _ in range(2)]
            arg = [P.sbuf([64, 512], F32, stack=s2) for _ in range(2)]
            tmp = [P.sbuf([64, 512], F32, stack=s2) for _ in range(2)]
            tmpi = [P.sbuf([64, 512], I32, stack=s2) for _ in range(2)]
            n = 0
            for layer in range(3):
                w = (w1, w2, w3)[layer]
                wk = ("w1", "w2", "w3")[layer]
                kin = 33 if layer == 0 else 64
                src = zT if layer == 0 else hid[(layer - 1) % 2]
                skey = "zT" if layer == 0 else f"hid{(layer - 1) % 2}"
                dst = hid[layer % 2]
                for tt in range(4):
                    u = n % 2
                    n += 1
                    b = next_bank(P)
                    sl = slice(tt * 512, (tt + 1) * 512)
                    mm(P, P.banks[b][0:64, :], w[0:kin, :], src[0:kin, sl], True, True, [wk, skey], [f"pb{b}"])
                    P.op("dve", lambda e, b=b, u=u, layer=layer: e.tensor_scalar(out=arg[u][:], in0=P.banks[b][0:64, :], scalar1=bcol[:, layer:layer + 1], scalar2=bcol[:, 3:4], op0=ALU.add, op1=ALU.mult),
                         reads=[f"pb{b}", "bcol"], writes=[f"arg{u}"])
                    sin_rr(P, dst[:, sl], arg[u][:], tmp[u][:], tmpi[u][:], f"arg{u}", f"hid{layer % 2}", f"tmp{u}")
            hid3 = hid[0]
            win = [P.sbuf([128, 512], F32, stack=s2) for _ in range(2)]
            hf = [P.sbuf([128, 512], F32, stack=s2) for _ in range(2)]
            hb = [P.sbuf([128, 512], F32, stack=s2) for _ in range(2)]
            ab = [P.sbuf([128, 1024], F32, stack=s2) for _ in range(2)]
            for tt in range(16):
                u = tt % 2
                P.dma("sp", win[u][:], c["hy_win"][tt * 128:(tt + 1) * 128, :], writes=[f"win{u}"])
                bf_ = next_bank(P, 0, 6)
                bb_ = next_bank(P, 0, 6)
                mm(P, P.banks[bf_][:], hid3[0:64, tt * 128:(tt + 1) * 128], w4[0:64, 0:512], True, True, ["hid0", "w4"], [f"pb{bf_}"])
                mm(P, P.banks[bb_][:], hid3[0:64, tt * 128:(tt + 1) * 128], w4[0:64, 512:1024], True, True, ["hid0", "w4"], [f"pb{bb_}"])
                P.op("dve", lambda e, u=u, bf_=bf_: e.tensor_tensor(out=hf[u][:], in0=P.banks[bf_][:], in1=win[u][:], op=ALU.mult), reads=[f"pb{bf_}", f"win{u}"], writes=[f"hf{u}"])
                P.op("dve", lambda e, u=u, bb_=bb_: e.tensor_tensor(out=hb[u][:], in0=P.banks[bb_][:], in1=win[u][:], op=ALU.mult), reads=[f"pb{bb_}", f"win{u}"], writes=[f"hb{u}"])
                if tt == 0:
                    P.op("dve", lambda e, u=u: e.memset(hb[u][0:1, :], 0.0), reads=[f"hb{u}"], writes=[f"hb{u}"])
                P.op("act", lambda e, u=u: e.activation(out=ab[u][:, 0:512], in_=hf[u][:], func=AF.Abs), reads=[f"hf{u}"], writes=[f"ab{u}"])
                P.op("act", lambda e, u=u: e.activation(out=ab[u][:, 512:1024], in_=hb[u][:], func=AF.Abs), reads=[f"hb{u}"], writes=[f"ab{u}"])
                mm(P, P.banks[7][:], P.ones_f[:], ab[u][:, 0:512], tt == 0, False, ["ones_f", f"ab{u}"], ["pb7"])
                mm(P, P.banks[7][:], P.ones_f[:], ab[u][:, 512:1024], False, tt == 15, ["ones_f", f"ab{u}"], ["pb7"])
                P.op("pool", lambda e, u=u, tt=tt: e.tensor_tensor(out=Hs[:, tt, :], in0=hf[u][:], in1=hb[u][:], op=ALU.add), reads=[f"hf{u}", f"hb{u}"], writes=["Hs"])
                P.op("pool", lambda e, u=u, tt=tt: e.tensor_tensor(out=Hd[:, tt, :], in0=hf[u][:], in1=hb[u][:], op=ALU.subtract), reads=[f"hf{u}", f"hb{u}"], writes=["Hd"])
            P.op("dve", lambda e: e.reciprocal(out=rnorm[:], in_=P.banks[7][:]), reads=["pb7"], writes=["rnorm"])
          P.barrier()
          if True:
            s2 = s2_outer
            Ct = [P.sbuf([128, 16, 256], BF16, stack=s2) for _ in range(2)]
            St = [P.sbuf([128, 16, 256], BF16, stack=s2) for _ in range(2)]
            for f4 in range(8):
                u = f4 % 2
                P.dma("sp", Ct[u][:], c["dft_cos"][:, f4 * 256:(f4 + 1) * 256].rearrange("(kc p) f -> p kc f", p=128), writes=[f"Ct{u}"])
                P.dma("sp", St[u][:], c["dft_sn"][:, f4 * 256:(f4 + 1) * 256].rearrange("(kc p) f -> p kc f", p=128), writes=[f"St{u}"])
                for fi in range(2):
                    fc = f4 * 2 + fi
                    br = next_bank(P, 0, 6)
                    bi = next_bank(P, 0, 6)
                    for tc in range(16):
                        mm(P, P.banks[br][:], Ct[u][:, tc, fi * 128:(fi + 1) * 128], Hs[:, tc, :], tc == 0, tc == 15, [f"Ct{u}", "Hs"], [f"pb{br}"])
                    for tc in range(16):
                        mm(P, P.banks[bi][:], St[u][:, tc, fi * 128:(fi + 1) * 128], Hd[:, tc, :], tc == 0, tc == 15, [f"St{u}", "Hd"], [f"pb{bi}"])
                    P.op("dve", lambda e, br=br, fc=fc: e.scalar_tensor_tensor(out=Kre[:, fc, :], in0=P.banks[br][:], scalar=sc[:, fc:fc + 1], in1=rnorm[:], op0=ALU.mult, op1=ALU.mult),
                         reads=[f"pb{br}", "sc", "rnorm"], writes=["Kre"])
                    P.op("dve", lambda e, bi=bi, fc=fc: e.scalar_tensor_tensor(out=Kim[:, fc, :], in0=P.banks[bi][:], scalar=sc[:, fc:fc + 1], in1=rnorm[:], op0=ALU.mult, op1=ALU.mult),
                         reads=[f"pb{bi}", "sc", "rnorm"], writes=["Kim"])
            bn = next_bank(P, 0, 6)
            for tc in range(16):
                mm(P, P.banks[bn][0:1, :], alt[:, 0:1], Hs[:, tc, :], tc == 0, tc == 15, ["alt", "Hs"], [f"pb{bn}"])
            P.op("dve", lambda e, bn=bn: e.scalar_tensor_tensor(out=Kim[0:1, 0, :], in0=P.banks[bn][0:1, :], scalar=sc[0:1, 0:1], in1=rnorm[0:1, :], op0=ALU.mult, op1=ALU.mult),
                 reads=[f"pb{bn}", "sc", "rnorm", "Kim"], writes=["Kim"])
        P.barrier()
        z_tm = P.sbuf([128, 16, 512], BF16, stack=st)
        with ExitStack() as s3:
            cw = P.sbuf([128, 12, 3], F32, stack=s3)
            cb = P.sbuf([128, 12], F32, stack=s3)
            for k_ in range(3):
                P.dma("sp", cw[:, :, k_], prm["b_conv_w"][k_].rearrange("(cc p) -> p cc", p=128), writes=["cw"], allow_slow_non_contiguous=True)
            P.dma("sp", cb[:], prm["b_conv_b"].rearrange("(cc p) -> p cc", p=128), writes=["cb"], allow_slow_non_contiguous=True)
            uu = [P.sbuf([128, T], F32, stack=s3) for _ in range(3)]
            uc = [P.sbuf([128, T], F32, stack=s3) for _ in range(3)]
            for j in range(4):
                for part in range(3):
                    cc = part * 4 + j
                    r0 = UOFF + cc * 128
                    P.dma("sp", uu[part][:], projT_d[r0:r0 + 128, :], writes=[f"uu{part}"])
                    P.op("dve", lambda e, part=part, cc=cc: e.tensor_scalar(out=uc[part][:], in0=uu[part][:], scalar1=cw[:, cc, 1:2], scalar2=cb[:, cc:cc + 1], op0=ALU.mult, op1=ALU.add),
                         reads=[f"uu{part}", "cw", "cb"], writes=[f"uc{part}"])
                    P.op("dve", lambda e, part=part, cc=cc: e.scalar_tensor_tensor(out=uc[part][:, 1:T], in0=uu[part][:, 0:T - 1], scalar=cw[:, cc, 0:1], in1=uc[part][:, 1:T], op0=ALU.mult, op1=ALU.add),
                         reads=[f"uu{part}", "cw", f"uc{part}"], writes=[f"uc{part}"])
                    P.op("dve", lambda e, part=part, cc=cc: e.scalar_tensor_tensor(out=uc[part][:, 0:T - 1], in0=uu[part][:, 1:T], scalar=cw[:, cc, 2:3], in1=uc[part][:, 0:T - 1], op0=ALU.mult, op1=ALU.add),
                         reads=[f"uu{part}", "cw", f"uc{part}"], writes=[f"uc{part}"])
                P.op("pool", lambda e: e.tensor_tensor(out=uc[0][:], in0=uc[0][:], in1=uc[1][:], op=ALU.mult), reads=["uc0", "uc1"], writes=["uc0"])
                P.dma("pool", zx_d[0, j * 128:(j + 1) * 128, :], uc[0][:], reads=["uc0"], writes=["zx_d"])
                P.dma("pool", zx_d[1, j * 128:(j + 1) * 128, :], uc[2][:], reads=["uc2"], writes=["zx_d"])
                for t4 in range(4):
                    b = next_bank(P)
                    for i in range(4):
                        tc = t4 * 4 + i
                        P.op("pe", lambda e, b=b, i=i, tc=tc: e.transpose(P.banks[b][:, i * 128:(i + 1) * 128], uc[0][:, tc * 128:(tc + 1) * 128], P.ident_f[:]),
                             reads=["uc0", "ident_f"], writes=[f"pb{b}"], skip_self=True)
                    P.op("act", lambda e, b=b, t4=t4, j=j: e.copy(out=z_tm[:, t4 * 4:(t4 + 1) * 4, j * 128:(j + 1) * 128], in_=P.banks[b][:].rearrange("p (i t) -> p i t", i=4)),
                         reads=[f"pb{b}"], writes=["z_tm"])
        P.barrier()
        with ExitStack() as s4:
            Ct = [P.sbuf([128, 16, 256], BF16, stack=s4) for _ in range(2)]
            St = [P.sbuf([128, 16, 256], BF16, stack=s4) for _ in range(2)]
            zr = [P.sbuf([128, 512], F32, stack=s4) for _ in range(2)]
            zi = [P.sbuf([128, 512], F32, stack=s4) for _ in range(2)]
            t1 = [P.sbuf([128, 512], F32, stack=s4) for _ in range(2)]
            t2 = [P.sbuf([128, 512], F32, stack=s4) for _ in range(2)]
            t3 = [P.sbuf([128, 512], F32, stack=s4) for _ in range(2)]
            t4_ = [P.sbuf([128, 512], F32, stack=s4) for _ in range(2)]
            for f4 in range(8):
                u = f4 % 2
                P.dma("sp", Ct[u][:], c["dft_cos"][:, f4 * 256:(f4 + 1) * 256].rearrange("(kc p) f -> p kc f", p=128), writes=[f"Ct{u}"])
                P.dma("sp", St[u][:], c["dft_sn"][:, f4 * 256:(f4 + 1) * 256].rearrange("(kc p) f -> p kc f", p=128), writes=[f"St{u}"])
                for fi in range(2):
                    fc = f4 * 2 + fi
                    v = fc % 2
                    br = next_bank(P)
                    bi = next_bank(P)
                    for tc in range(16):
                        mm(P, P.banks[br][:], Ct[u][:, tc, fi * 128:(fi + 1) * 128], z_tm[:, tc, :], tc == 0, tc == 15, [f"Ct{u}", "z_tm"], [f"pb{br}"])
                    for tc in range(16):
                        mm(P, P.banks[bi][:], St[u][:, tc, fi * 128:(fi + 1) * 128], z_tm[:, tc, :], tc == 0, tc == 15, [f"St{u}", "z_tm"], [f"pb{bi}"])
                    P.op("act", lambda e, br=br, v=v: e.copy(out=zr[v][:], in_=P.banks[br][:]), reads=[f"pb{br}"], writes=[f"zr{v}"])
                    P.op("act", lambda e, bi=bi, v=v: e.copy(out=zi[v][:], in_=P.banks[bi][:]), reads=[f"pb{bi}"], writes=[f"zi{v}"])
                    P.op("dve", lambda e, v=v, fc=fc: e.tensor_tensor(out=t1[v][:], in0=zr[v][:], in1=Kre[:, fc, :], op=ALU.mult), reads=[f"zr{v}", "Kre"], writes=[f"t1{v}"])
                    P.op("pool", lambda e, v=v, fc=fc: e.tensor_tensor(out=t2[v][:], in0=zi[v][:], in1=Kim[:, fc, :], op=ALU.mult), reads=[f"zi{v}", "Kim"], writes=[f"t2{v}"])
                    P.op("pool", lambda e, v=v, fc=fc: e.tensor_tensor(out=t3[v][:], in0=zr[v][:], in1=Kim[:, fc, :], op=ALU.mult), reads=[f"zr{v}", "Kim"], writes=[f"t3{v}"])
                    P.op("dve", lambda e, v=v, fc=fc: e.tensor_tensor(out=t4_[v][:], in0=zi[v][:], in1=Kre[:, fc, :], op=ALU.mult), reads=[f"zi{v}", "Kre"], writes=[f"t4{v}"])
                    if fc == 0:
                        P.op("dve", lambda e, v=v: e.memset(t2[v][0:1, :], 0.0), reads=[f"t2{v}"], writes=[f"t2{v}"])
                        P.op("dve", lambda e, v=v: e.memset(t3[v][0:1, :], 0.0), reads=[f"t3{v}"], writes=[f"t3{v}"])
                        P.op("dve", lambda e, v=v: e.tensor_tensor(out=t4_[v][0:1, :], in0=zi[v][0:1, :], in1=Kim[0:1, 0, :], op=ALU.mult), reads=[f"zi{v}", "Kim", f"t4{v}"], writes=[f"t4{v}"])
                    P.op("dve", lambda e, v=v, fc=fc: e.tensor_tensor(out=Yre[:, fc, :], in0=t1[v][:], in1=t2[v][:], op=ALU.subtract), reads=[f"t1{v}", f"t2{v}"], writes=["Yre"])
                    P.op("pool", lambda e, v=v, fc=fc: e.tensor_tensor(out=Yim[:, fc, :], in0=t3[v][:], in1=t4_[v][:], op=ALU.add), reads=[f"t3{v}", f"t4{v}"], writes=["Yim"])
        P.barrier()
      if True:
        with ExitStack() as s5:
            Cr = [P.sbuf([128, 16, 512], BF16, stack=s5) for _ in range(2)]
            Sr = [P.sbuf([128, 16, 512], BF16, stack=s5) for _ in range(2)]
            zt = [P.sbuf([128, 512], F32, stack=s5) for _ in range(2)]
            xt = [P.sbuf([128, 512], F32, stack=s5) for _ in range(2)]
            tm = [P.sbuf([128, 512], F32, stack=s5) for _ in range(2)]
            ost = [P.sbuf([128, 512], BF16, stack=s5) for _ in range(2)]
            n = 0
            for tt in range(4):
                u = tt % 2
                ts_ = slice(tt * 512, (tt + 1) * 512)
                P.dma("sp", Cr[u][:], c["dft_cos"][:, ts_].rearrange("(kc p) t -> p kc t", p=128), writes=[f"Cr{u}"])
                P.dma("sp", Sr[u][:], c["dft_snT"][:, ts_].rearrange("(kc p) t -> p kc t", p=128), writes=[f"Sr{u}"])
                for cj in range(4):
                    v = n % 2
                    n += 1
                    cs = slice(cj * 128, (cj + 1) * 128)
                    P.dma("sp", zt[v][:], zx_d[0, cs, ts_], writes=[f"zt{v}"])
                    P.dma("sp", xt[v][:], zx_d[1, cs, ts_], writes=[f"xt{v}"])
                    b = next_bank(P)
                    for fc in range(16):
                        mm(P, P.banks[b][:], Yre[:, fc, cs], Cr[u][:, fc, :], fc == 0, False, ["Yre", f"Cr{u}"], [f"pb{b}"])
                    for fc in range(16):
                        mm(P, P.banks[b][:], Yim[:, fc, cs], Sr[u][:, fc, :], False, fc == 15, ["Yim", f"Sr{u}"], [f"pb{b}"])
                    P.op("dve", lambda e, v=v, b=b, cj=cj: e.scalar_tensor_tensor(out=tm[v][:], in0=zt[v][:], scalar=skc[:, cj:cj + 1], in1=P.banks[b][:], op0=ALU.mult, op1=ALU.add),
                         reads=[f"zt{v}", "skc", f"pb{b}"], writes=[f"tm{v}"])
                    P.op("pool", lambda e, v=v: e.tensor_tensor(out=ost[v][:], in0=tm[v][:], in1=xt[v][:], op=ALU.mult), reads=[f"tm{v}", f"xt{v}"], writes=[f"ost{v}"])
                    P.dma("act", obrT_d[1024 + cj * 128:1024 + (cj + 1) * 128, ts_], ost[v][:], reads=[f"ost{v}"], writes=["obrT_d"])
    P.barrier()


def hyena_consts():
    import ml_dtypes
    f32 = np.float32
    L = T
    t = np.linspace(0.0, 1.0, L, dtype=f32)[:, None]
    n_bands = 16
    bands = np.linspace(1e-4, n_bands - 1, n_bands, dtype=f32)[None, :]
    ang = (f32(2.0 * math.pi / L) * np.arange(L, dtype=f32)[:, None] * bands).astype(f32)
    z = np.concatenate([t, np.cos(ang), -np.sin(ang)], axis=-1).astype(f32)
    max_decay = math.log(1e-2) / 0.3
    min_decay = math.log(1e-2) / 1.5
    deltas = np.abs(np.linspace(min_decay, max_decay, 512, dtype=f32))
    window = np.exp(-t * deltas[None, :]).astype(f32)
    idx = np.arange(L, dtype=np.int64)
    prod = (idx[:, None] * idx[None, :]) % 4096
    th = prod.astype(np.float64) * (2.0 * math.pi / 4096.0)
    cosm = np.cos(th)
    snm = -np.sin(th)
    snm[:, 0] = np.where(idx % 2 == 0, 1.0, -1.0)
    bf = ml_dtypes.bfloat16
    sc = np.full((128, 16), 2.0 / 4096.0, f32)
    sc[0, 0] = 1.0 / 4096.0
    alt = np.where(np.arange(128) % 2 == 0, 1.0, -1.0).astype(f32)[:, None]
    return {"hy_z": np.ascontiguousarray(z.T), "hy_win": window, "dft_cos": cosm.astype(bf), "dft_sn": snm.astype(bf),
            "dft_snT": np.ascontiguousarray(snm.T).astype(bf), "dft_sc": sc, "altsign": alt}

import math
import numpy as np

ROFF, KOFF, VOFF, WOFF, AOFF, GOFF = 3072, 3584, 4096, 4608, 4704, 4800
NCH = 32
CH = 64
GN_EPS = 64e-5


def rwkv_consts():
    f32 = np.float32
    s = np.arange(64)[:, None]
    t = np.arange(64)[None, :]
    out = {}
    for d in range(2):
        strict = (s < t) if d == 0 else (s > t)
        incl = (s <= t) if d == 0 else (s >= t)
        m = np.zeros((128, 256), f32)
        m[0:64, 0:64] = strict
        m[0:64, 64:128] = incl
        m[64:128, 64:128] = incl
        m[0:64, 128:192] = strict.T
        m[0:64, 192:256] = strict.T
        out[f"rw_mask{d}"] = m
    bo = np.zeros((128, 128), f32)
    bo[0:64, 0:64] = 1.0
    bo[64:128, 64:128] = 1.0
    out["rw_bones"] = bo
    return out


def phase_rwkv(P, projT_d, prm, obrT_d):
    STOP = 9
    c = P.consts
    colv = lambda a, o, n: a[o:o + n].rearrange("(p o) -> p o", o=1)
    with ExitStack() as st:
        BO = P.sbuf([128, 128], F32, stack=st)
        P.dma("sp", BO[:], c["rw_bones"], writes=["BO"])
        masks = [P.sbuf([128, 256], F32, stack=st) for _ in range(2)]
        for d in range(2):
            P.dma("sp", masks[d][:], c[f"rw_mask{d}"], writes=[f"mask{d}"])
        sg = P.sbuf([128, 2, T], F32, stack=st)
        P.dma("sp", sg[:], projT_d[GOFF:GOFF + 256, :].rearrange("(kc p) t -> p kc t", p=128), writes=["sg"])
        P.op("act", lambda e: e.activation(out=sg[:], in_=sg[:], func=AF.Sigmoid), reads=["sg"], writes=["sg"])
        gup = P.sbuf([128, 2, 512], F32, stack=st)
        P.dma("sp", gup[:], prm["c_g_up"].rearrange("(kc p) c -> p kc c", p=128), writes=["gup"])
        YC = P.sbuf([128, T], F32, stack=st)
        for hp in range(4):
            ch0 = hp * 128
            for d in range(2):
                mk = masks[d]
                mkey = f"mask{d}"
                with ExitStack() as sj:
                    BK = P.sbuf([128, NCH, 5, CH], F32, stack=sj)
                    CORR = P.sbuf([128, T], F32, stack=sj)
                    gC = P.sbuf([128, NCH], F32, stack=sj)
                    cols = P.sbuf([128, 16], F32, stack=sj)
                    if STOP == 73 and (hp, d) == (0, 1):
                        P.barrier(); return
                    cdefs = [(prm["c_mu"][d], ch0), (prm["c_mu"][d], 512 + ch0), (prm["c_mu"][d], 1024 + ch0),
                             (prm["c_w0"][d], ch0), (prm["c_a0"][d], ch0), (prm["c_k_k"], ch0), (prm["c_k_a"], ch0),
                             (prm["c_r_k"].rearrange("h n -> (h n)"), ch0), (prm["c_ln_w"], ch0), (prm["c_ln_b"], ch0)]
                    for i, (a, o) in enumerate(cdefs):
                        P.dma("sp", cols[:, i:i + 1], colv(a, o, 128), writes=["cols"])
                    cw = P.sbuf([96, 4], F32, stack=sj)
                    P.dma("sp", cw[:, 0:1], colv(prm["c_mu"][d], 1536, 96), writes=["cw"])
                    P.dma("sp", cw[:, 1:2], colv(prm["c_mu"][d], 1632, 96), writes=["cw"])
                    P.op("dve", lambda e: e.tensor_scalar(out=cols[:, 10:13], in0=cols[:, 0:3], scalar1=-1.0, scalar2=1.0, op0=ALU.mult, op1=ALU.add), reads=["cols"], writes=["cols"])
                    P.op("dve", lambda e: e.tensor_scalar(out=cols[:, 13:14], in0=cols[:, 3:4], scalar1=-1.0, scalar2=None, op0=ALU.mult), reads=["cols"], writes=["cols"])
                    P.op("dve", lambda e: e.tensor_scalar(out=cols[:, 14:15], in0=cols[:, 6:7], scalar1=-1.0, scalar2=1.0, op0=ALU.mult, op1=ALU.add), reads=["cols"], writes=["cols"])
                    P.op("dve", lambda e: e.tensor_scalar(out=cw[:, 2:4], in0=cw[:, 0:2], scalar1=-1.0, scalar2=1.0, op0=ALU.mult, op1=ALU.add), reads=["cw"], writes=["cw"])
                    if STOP == 71 and (hp, d) == (0, 1):
                        P.barrier(); return
                    wup = P.sbuf([96, 128], F32, stack=sj)
                    aup = P.sbuf([96, 128], F32, stack=sj)
                    P.dma("sp", wup[:], prm["c_w_up"][d][:, ch0:ch0 + 128], writes=["wup"])
                    P.dma("sp", aup[:], prm["c_a_up"][d][:, ch0:ch0 + 128], writes=["aup"])
                    with ExitStack() as s1:
                        xts = [P.sbuf([128, T], F32, stack=s1) for _ in range(3)]
                        xcnt = [0]
                        rf = P.sbuf([128, T], F32, stack=s1)
                        kf = P.sbuf([128, T], F32, stack=s1)
                        wl = P.sbuf([96, T], F32, stack=s1)
                        al = P.sbuf([96, T], F32, stack=s1)
                        EW = P.sbuf([128, T], F32, stack=s1)
                        AG = P.sbuf([128, T], F32, stack=s1)
                        KK = P.sbuf([128, T], F32, stack=s1)
                        KP = P.sbuf([128, T], F32, stack=s1)
                        LA = P.sbuf([128, T], F32, stack=s1)
                        LB = P.sbuf([128, T], F32, stack=s1)
                        tmp = P.sbuf([128, T], F32, stack=s1)
                        v3 = lambda a: a[:].rearrange("p (c j) -> p c j", j=CH)

                        def shift_mix(dst, dkey, row0, npart, mu, omu, mkeys):
                            i_ = xcnt[0] % 3
                            xcnt[0] += 1
                            xt, xk = xts[i_], f"xt{i_}"
                            eng2 = "dve"
                            P.dma("sp", xt[0:npart, :], projT_d[row0:row0 + npart, :], writes=[xk])
                            P.op("act", lambda e: e.mul(out=dst[0:npart, :], in_=xt[0:npart, :], mul=omu), reads=[xk] + mkeys, writes=[dkey])
                            if d == 0:
                                P.op(eng2, lambda e: e.scalar_tensor_tensor(out=dst[0:npart, 1:T], in0=xt[0:npart, 0:T - 1], scalar=mu, in1=dst[0:npart, 1:T], op0=ALU.mult, op1=ALU.add), reads=[xk, dkey] + mkeys, writes=[dkey])
                            else:
                                P.op(eng2, lambda e: e.scalar_tensor_tensor(out=dst[0:npart, 0:T - 1], in0=xt[0:npart, 1:T], scalar=mu, in1=dst[0:npart, 0:T - 1], op0=ALU.mult, op1=ALU.add), reads=[xk, dkey] + mkeys, writes=[dkey])

                        shift_mix(rf, "rf", ROFF + ch0, 128, cols[:, 0:1], cols[:, 10:11], ["cols"])
                        shift_mix(kf, "kf", KOFF + ch0, 128, cols[:, 1:2], cols[:, 11:12], ["cols"])
                        shift_mix(tmp, "tmp", VOFF + ch0, 128, cols[:, 2:3], cols[:, 12:13], ["cols"])
                        P.op("pool", lambda e: e.tensor_copy(out=BK[:, :, 4, :], in_=v3(tmp)), reads=["tmp"], writes=["BK4"])
                        shift_mix(wl, "wl", WOFF, 96, cw[:, 0:1], cw[:, 2:3], ["cw"])
                        shift_mix(al, "al", AOFF, 96, cw[:, 1:2], cw[:, 3:4], ["cw"])
                        P.op("act", lambda e: e.activation(out=wl[:], in_=wl[:], func=AF.Tanh), reads=["wl"], writes=["wl"])
                        for tt in range(4):
                            sl = slice(tt * 512, (tt + 1) * 512)
                            b = next_bank(P)
                            mm(P, P.banks[b][:], wup[:], wl[0:96, sl], True, True, ["wup", "wl"], [f"pb{b}"])
                            P.op("act", lambda e, b=b, sl=sl: e.activation(out=EW[:, sl], in_=P.banks[b][:], func=AF.Exp, bias=cols[:, 13:14], scale=-1.0), reads=[f"pb{b}", "cols"], writes=["EW"])
                            b2 = next_bank(P)
                            mm(P, P.banks[b2][:], aup[:], al[0:96, sl], True, True, ["aup", "al"], [f"pb{b2}"])
                            P.op("act", lambda e, b2=b2, sl=sl: e.activation(out=AG[:, sl], in_=P.banks[b2][:], func=AF.Sigmoid, bias=cols[:, 4:5], scale=1.0), reads=[f"pb{b2}", "cols"], writes=["AG"])
                        P.op("act", lambda e: e.activation(out=EW[:], in_=EW[:], func=AF.Ln, bias=1.0, scale=1.0), reads=["EW"], writes=["EW"])
                        P.op("act", lambda e: e.activation(out=EW[:], in_=EW[:], func=AF.Exp, bias=-0.5, scale=-1.0), reads=["EW"], writes=["EW"])
                        P.op("act", lambda e: e.mul(out=KK[:], in_=kf[:], mul=cols[:, 5:6]), reads=["kf", "cols"], writes=["KK"])
                        P.op("act", lambda e: e.activation(out=tmp[:], in_=KK[:], func=AF.Square), reads=["KK", "BK4"], writes=["tmp"])
                        for tt in range(4):
                            sl = slice(tt * 512, (tt + 1) * 512)
                            b = next_bank(P)
                            mm(P, P.banks[b][:], BO[:], tmp[:, sl], True, True, ["BO", "tmp"], [f"pb{b}"])
                            P.op("act", lambda e, b=b, sl=sl: e.activation(out=LA[:, sl], in_=P.banks[b][:], func=AF.Sqrt, bias=1e-24, scale=1.0), reads=[f"pb{b}"], writes=["LA"])
                        P.op("dve", lambda e: e.reciprocal(out=LA[:], in_=LA[:]), reads=["LA"], writes=["LA"])
                        P.op("dve", lambda e: e.tensor_tensor(out=KK[:], in0=KK[:], in1=LA[:], op=ALU.mult), reads=["KK", "LA"], writes=["KK"])
                        P.op("act", lambda e: e.activation(out=KP[:], in_=AG[:], func=AF.Identity, bias=cols[:, 14:15], scale=cols[:, 6:7]), reads=["AG", "cols"], writes=["KP"])
                        P.op("pool", lambda e: e.tensor_tensor(out=KP[:], in0=KP[:], in1=kf[:], op=ALU.mult), reads=["KP", "kf"], writes=["KP"])
                        P.op("dve", lambda e: e.scalar_tensor_tensor(out=tmp[:], in0=rf[:], scalar=cols[:, 7:8], in1=KP[:], op0=ALU.mult, op1=ALU.mult), reads=["rf", "KP", "cols", "tmp"], writes=["tmp"])
                        for tt in range(4):
                            sl = slice(tt * 512, (tt + 1) * 512)
                            b = next_bank(P)
                            mm(P, P.banks[b][:], BO[:], tmp[:, sl], True, True, ["BO", "tmp"], [f"pb{b}"])
                            P.op("dve", lambda e, b=b, tt=tt: e.tensor_tensor(out=CORR[:, tt * 512:(tt + 1) * 512].rearrange("p (c j) -> p c j", j=CH), in0=P.banks[b][:].rearrange("p (c j) -> p c j", j=CH), in1=BK[:, tt * 8:(tt + 1) * 8, 4, :], op=ALU.mult),
                                 reads=[f"pb{b}", "BK4"], writes=["CORR"])
                        src, skey = EW, "EW"
                        pp = [(LA, "LA"), (LB, "LB")]
                        for i, s_ in enumerate((1, 2, 4, 8, 16, 32)):
                            dst, dkey = pp[i % 2]
                            sv, dv = v3(src), v3(dst)
                            if d == 0:
                                P.op("dve", lambda e, sv=sv, dv=dv, s_=s_: e.tensor_tensor(out=dv[:, :, s_:CH], in0=sv[:, :, s_:CH], in1=sv[:, :, 0:CH - s_], op=ALU.add), reads=[skey], writes=[dkey])
                                P.op("pool", lambda e, sv=sv, dv=dv, s_=s_: e.tensor_copy(out=dv[:, :, 0:s_], in_=sv[:, :, 0:s_]), reads=[skey], writes=[dkey])
                            else:
                                P.op("dve", lambda e, sv=sv, dv=dv, s_=s_: e.tensor_tensor(out=dv[:, :, 0:CH - s_], in0=sv[:, :, 0:CH - s_], in1=sv[:, :, s_:CH], op=ALU.add), reads=[skey], writes=[dkey])
                                P.op("pool", lambda e, sv=sv, dv=dv, s_=s_: e.tensor_copy(out=dv[:, :, CH - s_:CH], in_=sv[:, :, CH - s_:CH]), reads=[skey], writes=[dkey])
                            src, skey = dst, dkey
                        LP = src
                        lend = CH - 1 if d == 0 else 0
                        P.op("act", lambda e: e.activation(out=gC[:], in_=v3(LP)[:, :, lend], func=AF.Exp, scale=-1.0), reads=[skey], writes=["gC"])
                        P.op("pool", lambda e: e.tensor_tensor(out=EW[:], in0=EW[:], in1=LP[:], op=ALU.subtract), reads=["EW", skey], writes=["EW"])
                        P.op("act", lambda e: e.activation(out=EW[:], in_=EW[:], func=AF.Exp), reads=["EW"], writes=["EW"])
                        P.op("act", lambda e: e.activation(out=LA[:], in_=LP[:], func=AF.Exp), reads=[skey, "LA"], writes=["LA"])
                        P.op("act", lambda e: e.activation(out=tmp[:], in_=LP[:], func=AF.Exp, scale=-1.0), reads=[skey, "tmp"], writes=["tmp"])
                        P.op("pool", lambda e: e.tensor_tensor(out=AG[:], in0=AG[:], in1=KK[:], op=ALU.mult), reads=["AG", "KK"], writes=["AG"])
                        P.op("dve", lambda e: e.tensor_tensor(out=BK[:, :, 0, :], in0=v3(AG), in1=v3(LA), op=ALU.mult), reads=["AG", "LA"], writes=["BK0"])
                        P.op("pool", lambda e: e.tensor_tensor(out=BK[:, :, 1, :], in0=v3(KP), in1=v3(LA), op=ALU.mult), reads=["KP", "LA"], writes=["BK1"])
                        P.op("pool", lambda e: e.tensor_tensor(out=BK[:, :, 2, :], in0=v3(rf), in1=v3(tmp), op=ALU.mult), reads=["rf", "tmp"], writes=["BK2"])
                        P.op("dve", lambda e: e.scalar_tensor_tensor(out=BK[:, :, 3, :], in0=v3(KK), scalar=-1.0, in1=v3(EW), op0=ALU.mult, op1=ALU.mult), reads=["KK", "EW"], writes=["BK3"])
                    P.barrier()
                    if getattr(P, "dbg", None) is not None and hp == P.dbg["hp"] and d == P.dbg["d"]:
                        P.dma("sp", P.dbg["BK"], BK[:].rearrange("p c s j -> p (c s j)"), reads=["BK0", "BK1", "BK2", "BK3", "BK4"], writes=["dbgBK"])
                        P.dma("sp", P.dbg["gC"], gC[:], reads=["gC"], writes=["dbggC"])
                        P.dma("sp", P.dbg["CORR"], CORR[:], reads=["CORR"], writes=["dbgCORR"])
                    if STOP <= 1 or (STOP == 72 and (hp, d) == (0, 1)):
                        return
                    BKK = ["BK0", "BK1", "BK2", "BK3", "BK4"]
                    RT1 = P.sbuf([64, NCH, CH], F32, stack=sj)
                    GC = P.sbuf([64, 2, NCH], F32, stack=sj)
                    P.dma("sp", RT1[:], BK[64:128, :, 2, :], reads=["BK2"], writes=["RT1"])
                    P.dma("sp", GC[:, 0, :], gC[0:64, :], reads=["gC"], writes=["GC"])
                    P.dma("sp", GC[:, 1, :], gC[64:128, :], reads=["gC"], writes=["GC"])
                    OFM = P.sbuf([128, T], F32, stack=sj)
                    ST = [P.sbuf([64, 2, CH], F32, stack=sj) for _ in range(2)]
                    P.op("pool", lambda e: e.memset(ST[0][:], 0.0), writes=["ST0"])
                    stn = 0
                    G = 8
                    BKb = P.sbuf([128, NCH, 4, CH], BF16, stack=sj)
                    P.op("pool", lambda e: e.tensor_copy(out=BKb[:], in_=BK[:, :, 0:4, :]), reads=BKK, writes=["BKb"])
                    M1t = P.sbuf([64, 2 * G, CH], BF16, stack=sj)
                    N1b = P.sbuf([64, 2 * G, CH], BF16, stack=sj)
                    Yf = P.sbuf([64, 2 * G, CH], F32, stack=sj)
                    Nt = P.sbuf([64, 2 * G, 128], F32, stack=sj)
                    Mbr = P.sbuf([64, 2 * G, CH], F32, stack=sj)
                    Mkr = P.sbuf([64, 2 * G, CH], F32, stack=sj)
                    Mp = [P.sbuf([64, 2 * G, CH], BF16, stack=sj) for _ in range(2)]
                    Np = [P.sbuf([64, 2 * G, CH], BF16, stack=sj) for _ in range(2)]
                    Yt = [P.sbuf([64, 2 * G, CH], BF16, stack=sj) for _ in range(2)]
                    TR = P.sbuf([64, G, 4, 128], F32, stack=sj)
                    UT = P.sbuf([64, G, 128], F32, stack=sj)
                    W1T = P.sbuf([64, 2 * G, CH], F32, stack=sj)
                    W2T = P.sbuf([64, 2 * G, CH], F32, stack=sj)
                    OTs = [P.sbuf([64, 128], F32, stack=sj) for _ in range(2)]
                    chunk_order = list(range(NCH)) if d == 0 else list(range(NCH - 1, -1, -1))
                    fl = lambda ap: ap.rearrange("p a b -> p (a b)")
                    for gi in range(NCH // G):
                        chunks = chunk_order[gi * G:(gi + 1) * G]
                        for h2 in range(2):
                            hb = slice(h2 * 64, (h2 + 1) * 64)
                            for q in range(2):
                                b1 = next_bank(P)
                                b2 = next_bank(P)
                                b3 = next_bank(P)
                                for i in range(4):
                                    cc = chunks[q * 4 + i]
                                    mm(P, P.banks[b1][0:64, i * 128:(i + 1) * 128], BKb[hb, cc, 0, :], fl(BKb[hb, cc, 2:4, :]), True, True, ["BKb"], [f"pb{b1}"])
                                    mm(P, P.banks[b2][0:64, i * 128:(i + 1) * 128], BKb[hb, cc, 3, :], fl(BKb[hb, cc, 0:2, :]), True, True, ["BKb"], [f"pb{b2}"])
                                    mm(P, P.banks[b3][0:64, i * 64:(i + 1) * 64], BKb[hb, cc, 1, :], BKb[hb, cc, 2, :], True, True, ["BKb"], [f"pb{b3}"])
                                u0 = h2 * G + q * 4
                                pv1 = P.banks[b1][0:64, :].rearrange("p (i x) -> p i x", i=4)
                                pv2 = P.banks[b2][0:64, :].rearrange("p (i x) -> p i x", i=4)
                                pv3 = P.banks[b3][0:64, 0:256].rearrange("p (i x) -> p i x", i=4)
                                P.op("dve", lambda e, pv1=pv1, u0=u0: e.tensor_tensor(out=M1t[:, u0:u0 + 4, :], in0=pv1[:, :, 64:128], in1=mk[0:64, 0:64].unsqueeze(1).to_broadcast([64, 4, 64]), op=ALU.mult),
                                     reads=[f"pb{b1}", mkey], writes=["M1t"])
                                P.op("dve", lambda e, pv1=pv1, u0=u0: e.tensor_tensor(out=Mbr[:, u0:u0 + 4, :], in0=pv1[:, :, 0:64], in1=mk[0:64, 64:128].unsqueeze(1).to_broadcast([64, 4, 64]), op=ALU.mult),
                                     reads=[f"pb{b1}", mkey], writes=["Mbr"])
                                P.op("dve", lambda e, pv2=pv2, u0=u0: e.tensor_tensor(out=Nt[:, u0:u0 + 4, :], in0=pv2, in1=mk[0:64, 128:256].unsqueeze(1).to_broadcast([64, 4, 128]), op=ALU.mult),
                                     reads=[f"pb{b2}", mkey], writes=["Nt"])
                                P.op("dve", lambda e, pv2=pv2, u0=u0: e.tensor_tensor(out=N1b[:, u0:u0 + 4, :], in0=pv2[:, :, 0:64], in1=mk[0:64, 128:192].unsqueeze(1).to_broadcast([64, 4, 64]), op=ALU.mult),
                                     reads=[f"pb{b2}", mkey], writes=["N1b"])
                                P.op("dve", lambda e, pv3=pv3, u0=u0: e.tensor_tensor(out=Mkr[:, u0:u0 + 4, :], in0=pv3, in1=mk[0:64, 64:128].unsqueeze(1).to_broadcast([64, 4, 64]), op=ALU.mult),
                                     reads=[f"pb{b3}", mkey], writes=["Mkr"])
                        if STOP <= 2:
                            P.barrier(); return
                        for ci, cc in enumerate(chunks):
                            b = next_bank(P)
                            for j, slot in enumerate((0, 1, 3, 4)):
                                P.op("pe", lambda e, b=b, j=j, cc=cc, slot=slot: e.transpose(P.banks[b][0:64, j * 128:(j + 1) * 128], BK[:, cc, slot, :], P.ident_f[:]), reads=BKK + ["ident_f"], writes=[f"pb{b}"], skip_self=True)
                            if ci % 2 == 0:
                                P.op("act", lambda e, b=b, ci=ci: e.copy(out=TR[:, ci, :, :], in_=P.banks[b][0:64, :].rearrange("p (j x) -> p j x", j=4)), reads=[f"pb{b}"], writes=["TR"])
                            else:
                                P.op("dve", lambda e, b=b, ci=ci: e.tensor_copy(out=TR[:, ci, :, :], in_=P.banks[b][0:64, :].rearrange("p (j x) -> p j x", j=4)), reads=[f"pb{b}"], writes=["TR"])
                        P.op("pool", lambda e: e.tensor_tensor(out=Yt[0][:], in0=M1t[:], in1=P.ident_b[0:64, 0:64].unsqueeze(1).to_broadcast([64, 2 * G, 64]), op=ALU.add), reads=["M1t", "ident_b"], writes=["Yt0"])
                        Mc, Mk_, Nc, Nk_ = M1t, "M1t", None, "N1b"
                        yi = 0
                        for lev in range(5):
                            mo, no = Mp[lev % 2], Np[lev % 2]
                            mok, nok = f"Mp{lev % 2}", f"Np{lev % 2}"
                            for q in range(2):
                                bm = next_bank(P)
                                bn = next_bank(P)
                                for i in range(8):
                                    u = q * 8 + i
                                    nsrc = N1b[:, u, :] if lev == 0 else Nc[:, u, :]
                                    msrc = Mc[:, u, :]
                                    if lev < 4:
                                        mm(P, P.banks[bm][0:64, i * 64:(i + 1) * 64], nsrc, msrc, True, True, [Mk_, Nk_], [f"pb{bm}"])
                                    mm(P, P.banks[bn][0:64, i * 64:(i + 1) * 64], msrc, nsrc, True, True, [Mk_, Nk_], [f"pb{bn}"])
                                if lev < 4:
                                    P.op("act", lambda e, bm=bm, q=q, mo=mo: e.copy(out=mo[:, q * 8:(q + 1) * 8, :], in_=P.banks[bm][0:64, :].rearrange("p (i x) -> p i x", i=8)), reads=[f"pb{bm}"], writes=[mok])
                                P.op("dve", lambda e, bn=bn, q=q, no=no: e.tensor_copy(out=no[:, q * 8:(q + 1) * 8, :], in_=P.banks[bn][0:64, :].rearrange("p (i x) -> p i x", i=8)), reads=[f"pb{bn}"], writes=[nok])
                            ys, yd = Yt[yi % 2], Yt[(yi + 1) % 2]
                            ysk, ydk = f"Yt{yi % 2}", f"Yt{(yi + 1) % 2}"
                            for q in range(2):
                                by = next_bank(P)
                                for i in range(8):
                                    u = q * 8 + i
                                    mm(P, P.banks[by][0:64, i * 64:(i + 1) * 64], no[:, u, :], ys[:, u, :], True, True, [nok, ysk], [f"pb{by}"])
                                P.op("dve", lambda e, by=by, q=q, ys=ys, yd=yd: e.tensor_tensor(out=yd[:, q * 8:(q + 1) * 8, :], in0=ys[:, q * 8:(q + 1) * 8, :], in1=P.banks[by][0:64, :].rearrange("p (i x) -> p i x", i=8), op=ALU.add),
                                     reads=[f"pb{by}", ysk], writes=[ydk])
                            yi += 1
                            Mc, Mk_, Nc, Nk_ = mo, mok, no, nok
                        Yb_, Ybk = Yt[yi % 2], f"Yt{yi % 2}"
                        P.op("act", lambda e, Yb_=Yb_: e.copy(out=Yf[:], in_=Yb_[:]), reads=[Ybk], writes=["Yf"])
                        Y, Yk = Yf, "Yf"
                        if STOP <= 3:
                            P.barrier(); return
                        for q in range(2):
                            bw1 = next_bank(P)
                            bw2 = next_bank(P)
                            for i in range(8):
                                u = q * 8 + i
                                h2, ci = divmod(u, G)
                                mm(P, P.banks[bw1][0:64, i * 64:(i + 1) * 64], TR[:, ci, 2, h2 * 64:(h2 + 1) * 64], Y[:, u, :], True, True, ["TR", Yk], [f"pb{bw1}"])
                                mm(P, P.banks[bw2][0:64, i * 64:(i + 1) * 64], Nt[:, u, 64:128], Y[:, u, :], True, True, ["Nt", Yk], [f"pb{bw2}"])
                            P.op("act", lambda e, bw1=bw1, q=q: e.copy(out=W1T[:, q * 8:(q + 1) * 8, :], in_=P.banks[bw1][0:64, :].rearrange("p (i x) -> p i x", i=8)), reads=[f"pb{bw1}"], writes=["W1T"])
                            P.op("dve", lambda e, bw2=bw2, q=q: e.tensor_copy(out=W2T[:, q * 8:(q + 1) * 8, :], in_=P.banks[bw2][0:64, :].rearrange("p (i x) -> p i x", i=8)), reads=[f"pb{bw2}"], writes=["W2T"])
                        if STOP <= 4:
                            P.barrier(); return
                        for ci, cc in enumerate(chunks):
                            Sc, Sk = ST[stn % 2], f"ST{stn % 2}"
                            Sn, Snk = ST[(stn + 1) % 2], f"ST{(stn + 1) % 2}"
                            stn += 1
                            bu = next_bank(P)
                            for h2 in range(2):
                                u = h2 * G + ci
                                hs = slice(h2 * 64, (h2 + 1) * 64)
                                mm(P, P.banks[bu][0:64, hs], W1T[:, u, :], Sc[:, h2, :], True, False, ["W1T", Sk], [f"pb{bu}"])
                                mm(P, P.banks[bu][0:64, hs], W2T[:, u, :], TR[:, ci, 3, hs], False, True, ["W2T", "TR"], [f"pb{bu}"])
                            P.op("dve", lambda e, bu=bu, ci=ci: e.tensor_copy(out=UT[:, ci, :], in_=P.banks[bu][0:64, 0:128]), reads=[f"pb{bu}"], writes=["UT"])
                            bo_ = next_bank(P)
                            bs_ = next_bank(P)
                            for h2 in range(2):
                                u = h2 * G + ci
                                hs = slice(h2 * 64, (h2 + 1) * 64)
                                rt = BK[0:64, cc, 2, :] if h2 == 0 else RT1[:, cc, :]
                                mm(P, P.banks[bo_][0:64, hs], rt, Sc[:, h2, :], True, False, ["BK2", "RT1", Sk], [f"pb{bo_}"])
                                mm(P, P.banks[bo_][0:64, hs], Mbr[:, u, :], UT[:, ci, hs], False, False, ["Mbr", "UT"], [f"pb{bo_}"])
                                mm(P, P.banks[bo_][0:64, hs], Mkr[:, u, :], TR[:, ci, 3, hs], False, True, ["Mkr", "TR"], [f"pb{bo_}"])
                                mm(P, P.banks[bs_][0:64, hs], P.ident_f[0:64, 0:64], Sc[:, h2, :], True, False, ["ident_f", Sk], [f"pb{bs_}"])
                                mm(P, P.banks[bs_][0:64, hs], TR[:, ci, 0, hs], UT[:, ci, hs], False, False, ["TR", "UT"], [f"pb{bs_}"])
                                mm(P, P.banks[bs_][0:64, hs], TR[:, ci, 1, hs], TR[:, ci, 3, hs], False, True, ["TR"], [f"pb{bs_}"])
                            P.op("dve", lambda e, bs_=bs_, cc=cc, Sn=Sn: e.tensor_tensor(out=Sn[:], in0=P.banks[bs_][0:64, 0:128].rearrange("p (h v) -> p h v", h=2), in1=GC[:, :, cc].unsqueeze(2).to_broadcast([64, 2, 64]), op=ALU.mult),
                                 reads=[f"pb{bs_}", "GC"], writes=[Snk])
                            ot = OTs[ci % 2]
                            otk = f"OTs{ci % 2}"
                            P.op("act", lambda e, bo_=bo_, ot=ot: e.copy(out=ot[:], in_=P.banks[bo_][0:64, 0:128]), reads=[f"pb{bo_}"], writes=[otk])
                            bt = next_bank(P)
                            mm(P, P.banks[bt][:, 0:64], ot[:], P.ident_f[0:64, 0:64], True, True, [otk, "ident_f"], [f"pb{bt}"])
                            P.op("act", lambda e, bt=bt, cc=cc: e.copy(out=OFM[:, cc * 64:(cc + 1) * 64], in_=P.banks[bt][:, 0:64]), reads=[f"pb{bt}"], writes=["OFM"])
                            if STOP == 45:
                                P.barrier(); return
                    if getattr(P, "dbg", None) is not None and hp == P.dbg["hp"] and d == P.dbg["d"]:
                        P.dma("sp", P.dbg["OFM"], OFM[:], reads=["OFM"], writes=["dbgOFM"])
                    if STOP <= 5:
                        P.barrier(); return
                    cen = P.sbuf([128, 512], F32, stack=sj)
                    sq = P.sbuf([128, 512], F32, stack=sj)
                    rs = P.sbuf([128, 512], F32, stack=sj)
                    for tt in range(4):
                        sl = slice(tt * 512, (tt + 1) * 512)
                        b = next_bank(P)
                        mm(P, P.banks[b][:], BO[:], OFM[:, sl], True, True, ["BO", "OFM"], [f"pb{b}"])
                        P.op("dve", lambda e, b=b, sl=sl: e.scalar_tensor_tensor(out=cen[:], in0=P.banks[b][:], scalar=-1.0 / 64, in1=OFM[:, sl], op0=ALU.mult, op1=ALU.add), reads=[f"pb{b}", "OFM"], writes=["cen"])
                        if getattr(P, "dbg", None) is not None and hp == P.dbg["hp"] and d == P.dbg["d"] and tt == 0:
                            P.dma("sp", P.dbg["C0"], cen[:], reads=["cen"], writes=["dbgC0"])
                            P.op("dve", lambda e, b=b: e.tensor_copy(out=rs[:], in_=P.banks[b][:]), reads=[f"pb{b}"], writes=["rs"])
                            P.dma("sp", P.dbg["PS"], rs[:], reads=["rs"], writes=["dbgPS"])
                        P.op("act", lambda e: e.activation(out=sq[:], in_=cen[:], func=AF.Square), reads=["cen"], writes=["sq"])
                        b2 = next_bank(P)
                        mm(P, P.banks[b2][:], BO[:], sq[:], True, True, ["BO", "sq"], [f"pb{b2}"])
                        P.op("act", lambda e, b2=b2: e.activation(out=rs[:], in_=P.banks[b2][:], func=AF.Sqrt, bias=GN_EPS, scale=1.0 / 64), reads=[f"pb{b2}"], writes=["rs"])
                        P.op("dve", lambda e: e.reciprocal(out=rs[:], in_=rs[:]), reads=["rs"], writes=["rs"])
                        P.op("dve", lambda e: e.tensor_tensor(out=cen[:], in0=cen[:], in1=rs[:], op=ALU.mult), reads=["cen", "rs"], writes=["cen"])
                        P.op("dve", lambda e: e.tensor_scalar(out=cen[:], in0=cen[:], scalar1=cols[:, 8:9], scalar2=cols[:, 9:10], op0=ALU.mult, op1=ALU.add), reads=["cen", "cols"], writes=["cen"])
                        if getattr(P, "dbg", None) is not None and hp == P.dbg["hp"] and d == P.dbg["d"] and tt == 0:
                            P.dma("sp", P.dbg["CEN"], cen[:], reads=["cen"], writes=["dbgCEN"])
                            P.dma("sp", P.dbg["RS"], rs[:], reads=["rs"], writes=["dbgRS"])
                        if d == 0:
                            P.op("pool", lambda e, sl=sl: e.tensor_tensor(out=YC[:, sl], in0=cen[:], in1=CORR[:, sl], op=ALU.add), reads=["cen", "CORR"], writes=["YC"])
                        else:
                            P.op("pool", lambda e, sl=sl: e.tensor_tensor(out=cen[:], in0=cen[:], in1=CORR[:, sl], op=ALU.add), reads=["cen", "CORR"], writes=["cen"])
                            P.op("pool", lambda e, sl=sl: e.tensor_tensor(out=YC[:, sl], in0=YC[:, sl], in1=cen[:], op=ALU.add), reads=["cen", "YC"], writes=["YC"])
                if STOP == 6:
                    P.barrier(); return
                P.barrier()
            if getattr(P, "dbg", None) is not None and hp == P.dbg["hp"]:
                P.dma("sp", P.dbg["YC"], YC[:], reads=["YC"], writes=["dbgYC"])
                P.dma("sp", P.dbg["SG"], sg[:].rearrange("p a t -> p (a t)"), reads=["sg"], writes=["dbgSG"])
            with ExitStack() as sg_:
                ost = [P.sbuf([128, 512], BF16, stack=sg_) for _ in range(2)]
                for tt in range(4):
                    sl = slice(tt * 512, (tt + 1) * 512)
                    b = next_bank(P)
                    for kc in range(2):
                        mm(P, P.banks[b][:], gup[:, kc, ch0:ch0 + 128], sg[:, kc, sl], kc == 0, kc == 1, ["gup", "sg"], [f"pb{b}"])
                    u = tt % 2
                    P.op("dve", lambda e, b=b, u=u, sl=sl: e.tensor_tensor(out=ost[u][:], in0=YC[:, sl], in1=P.banks[b][:], op=ALU.mult), reads=[f"pb{b}", "YC"], writes=[f"ost{u}"])
                    P.dma("sp", obrT_d[1536 + ch0:1536 + ch0 + 128, sl], ost[u][:], reads=[f"ost{u}"], writes=["obrT_d"])
            P.barrier()
    P.barrier()


import ml_dtypes
from concourse.bass_utils import run_bass_kernel_spmd

PARAM_NAMES = ["norm_mix", "w_in", "a_q_norm", "a_k_norm", "b_conv_w", "b_conv_b", "b_filt_w1", "b_filt_b1", "b_filt_w2", "b_filt_b2",
               "b_filt_w3", "b_filt_b3", "b_filt_w4", "b_filt_freq", "b_skip", "c_mu", "c_w0", "c_w_up", "c_a0", "c_a_up", "c_g_up", "c_k_k",
               "c_k_a", "c_r_k", "c_ln_w", "c_ln_b", "d_lq1", "d_lk1", "d_lq2", "d_lk2", "d_subln", "w_gate", "w_branch", "w_out", "norm_ffn",
               "w_ff_gate", "w_ff_up", "w_ff_down", "norm_final"]
DEPTH = 2


def host_consts():
    cs = {"ident": np.eye(128, dtype=np.float32)}
    cs.update(attn_consts())
    cs.update(hyena_consts())
    cs.update(rwkv_consts())
    return cs


def build_program(shapes, consts):
    P = Prog()
    x = P.dram("x", [T, D], F32, kind="ExternalInput").ap()
    prm = {n: P.dram(n, list(shapes[n]), F32, kind="ExternalInput").ap() for n in PARAM_NAMES}
    P.consts = {}
    for k, v in consts.items():
        P.consts[k] = P.dram("c_" + k, list(v.shape), BF16 if v.dtype == ml_dtypes.bfloat16 else F32, kind="ExternalInput").ap()
    out = P.dram("out", [T, D], F32, kind="ExternalOutput").ap()
    xs = P.dram("xs", [T, D], F32).ap()
    hT_d = P.dram("hT_d", [D, T], BF16).ap()
    projT_d = P.dram("projT_d", [D_IN, T], F32).ap()
    avt_d = P.dram("avt_d", [T, 256], BF16).ap()
    dvt_d = P.dram("dvt_d", [T, 512], BF16).ap()
    obrT_d = P.dram("obrT_d", [2560, T], BF16).ap()
    mergedT_d = P.dram("mergedT_d", [D, T], BF16).ap()
    zx_d = P.dram("zx_d", [2, 512, T], F32).ap()
    act_d = P.dram("act_d", [16, 128, FFN // 128, 128], BF16).ap()
    setup_common(P)
    P.barrier()
    for l in range(DEPTH):
        xsrc = x if l == 0 else xs
        pl = {n: prm[n][l] for n in PARAM_NAMES if n != "norm_final"}
        phase_norm_hT(P, xsrc, pl["norm_mix"], hT_d)
        phase_inproj(P, hT_d, pl["w_in"], projT_d, avt_d, dvt_d)
        phase_attn_a(P, projT_d, avt_d, pl["a_q_norm"], pl["a_k_norm"], obrT_d)
        lam_init = 0.8 - 0.6 * math.exp(-0.3 * l)
        phase_attn_d(P, projT_d, dvt_d, pl["d_lq1"], pl["d_lk1"], pl["d_lq2"], pl["d_lk2"], pl["d_subln"], lam_init, obrT_d)
        phase_hyena(P, projT_d, pl, zx_d, obrT_d)
        phase_rwkv(P, projT_d, pl, obrT_d)
        phase_merge(P, hT_d, obrT_d, pl["w_gate"], pl["w_branch"], mergedT_d)
        phase_outproj(P, mergedT_d, pl["w_out"], xsrc, xs)
        phase_norm_hT(P, xs, pl["norm_ffn"], hT_d)
        phase_ffn_up(P, hT_d, pl["w_ff_gate"], pl["w_ff_up"], act_d)
        phase_ffn_down(P, act_d, pl["w_ff_down"], xs, xs)
    phase_final_norm(P, xs, prm["norm_final"], out)
    P.barrier()
    P.emit()
    P.close()
    return P


def kernel(**inputs):
    n = 8
    consts = host_consts()
    shapes = {k: np.asarray(inputs[k]).shape for k in PARAM_NAMES}
    P = build_program(shapes, consts)
    x = np.ascontiguousarray(np.asarray(inputs["x"], dtype=np.float32))
    shared = {k: np.ascontiguousarray(np.asarray(inputs[k], dtype=np.float32)) for k in PARAM_NAMES}
    for k, v in consts.items():
        shared["c_" + k] = v
    in_maps = []
    for b in range(n):
        m = dict(shared)
        m["x"] = x[b]
        in_maps.append(m)
    res = run_bass_kernel_spmd(P.nc, in_maps, core_ids=list(range(n)))
    return np.stack([np.asarray(r["out"], dtype=np.float32) for r in res.results], axis=0)
```
